# Optimizing a Trainium2 kernel written in Bass

```python
import math
import jax, jax.numpy as jnp
from jax import lax
import numpy as np

D_MODEL = 1024
BATCH = 8
SEQ = 4096
DEPTH = 4

CHUNK = 64
D_SSM = D_MODEL // 2
SSM_GROUP = 16
N_SSM_GROUPS = D_SSM // SSM_GROUP
SSM_STATE = 64
D_SGU = D_MODEL - D_SSM
SGU_HEADS = 8
SGU_HEAD_DIM = D_SGU // SGU_HEADS
SGU_BLOCK = 128
D_IN = D_SSM + 2 * D_SGU
D_FF = -(-8 * D_MODEL // (3 * 256)) * 256
PLE_DIM = 256
ALPHA = (2 * DEPTH) ** 0.25
BETA = (8 * DEPTH) ** -0.25
LN_EPS = 1e-5
DT_MIN, DT_MAX = 1e-3, 1e-1

kernel_name = "hybrid_s5_sgu_deepnorm_encoder"


def layer_norm(x, g, b):
    xf = x.astype(jnp.float32)
    mu = jnp.mean(xf, -1, keepdims=True)
    var = jnp.mean(jnp.square(xf - mu), -1, keepdims=True)
    return ((xf - mu) * lax.rsqrt(var + LN_EPS) * g.astype(jnp.float32)
            + b.astype(jnp.float32)).astype(x.dtype)


def rms_norm(x, g):
    xf = x.astype(jnp.float32)
    return (xf * lax.rsqrt(jnp.mean(xf * xf, -1, keepdims=True) + LN_EPS)
            * g.astype(jnp.float32)).astype(x.dtype)


def s5_mixer(u, a_re, a_im, log_dt, b_re, b_im, c_re, c_im, d_skip, w_glu):
    bsz, seq, _ = u.shape
    f32 = jnp.float32
    ug = u.astype(f32).reshape(bsz, seq, N_SSM_GROUPS, SSM_GROUP)
    lam = lax.complex(a_re.astype(f32), a_im.astype(f32))
    dt = jnp.exp(log_dt.astype(f32))[:, None]
    a_bar = jnp.exp(lam * dt)
    b_cplx = lax.complex(b_re.astype(f32), b_im.astype(f32))
    b_bar = ((a_bar - 1.0) / lam)[..., None] * b_cplx
    bu = jnp.einsum('gph,blgh->blgp', b_bar, ug.astype(jnp.complex64))
    a_seq = jnp.broadcast_to(a_bar[None, None], (1, seq) + a_bar.shape)

    def combine(left, right):
        a_l, b_l = left
        a_r, b_r = right
        return a_l * a_r, a_r * b_l + b_r

    _, states = lax.associative_scan(combine, (a_seq, bu), axis=1)
    c_cplx = lax.complex(c_re.astype(f32), c_im.astype(f32))
    y = jnp.einsum('ghp,blgp->blgh', c_cplx, states).real + d_skip.astype(f32) * ug
    y = jax.nn.gelu(y.reshape(bsz, seq, D_SSM))
    y = y * jax.nn.sigmoid(y @ w_glu.astype(f32))
    return y.astype(u.dtype)


def sgu_mixer(z, ln_g, ln_b, w_s, b_s):
    z = jax.nn.gelu(z)
    u, v = z[..., :D_SGU], z[..., D_SGU:]
    v = layer_norm(v, ln_g, ln_b)
    bsz, seq, _ = v.shape
    n_blk = seq // SGU_BLOCK
    vb = v.reshape(bsz, n_blk, SGU_BLOCK, SGU_HEADS, SGU_HEAD_DIM)
    chunk_id = jnp.arange(SGU_BLOCK) // CHUNK
    mask = chunk_id[:, None] >= chunk_id[None, :]
    w = jnp.where(mask[None], w_s, jnp.zeros_like(w_s))
    s = jnp.einsum('hij,bnjhc->bnihc', w, vb) + b_s.T[None, None, :, :, None]
    return u * s.reshape(bsz, seq, D_SGU)


def setup_inputs(seed: int = 0) -> dict:
    key = jax.random.key(seed)
    ks = iter(jax.random.split(key, 40))

    def nrm(shape, scale):
        return scale * jax.random.normal(next(ks), shape, jnp.float32)

    n_idx = jnp.arange(SSM_STATE, dtype=jnp.float32)
    G, P, H = N_SSM_GROUPS, SSM_STATE, SSM_GROUP
    inp = {}
    inp["x"] = nrm((BATCH, SEQ, D_MODEL), 1.0)
    inp["p"] = nrm((DEPTH, BATCH, SEQ, PLE_DIM), 1.0)
    inp["emb_ln_g"] = 1.0 + nrm((D_MODEL,), 0.02)
    inp["emb_ln_b"] = nrm((D_MODEL,), 0.02)
    inp["w_in"] = nrm((DEPTH, D_MODEL, D_IN), D_MODEL ** -0.5)
    inp["ssm_a_re"] = -0.5 + nrm((DEPTH, G, P), 0.01)
    inp["ssm_a_im"] = math.pi * n_idx + nrm((DEPTH, G, P), 0.01)
    inp["ssm_log_dt"] = jax.random.uniform(next(ks), (DEPTH, G), jnp.float32,
                                           math.log(DT_MIN), math.log(DT_MAX))
    inp["ssm_b_re"] = nrm((DEPTH, G, P, H), (2 * H) ** -0.5)
    inp["ssm_b_im"] = nrm((DEPTH, G, P, H), (2 * H) ** -0.5)
    inp["ssm_c_re"] = nrm((DEPTH, G, H, P), (2 * P) ** -0.5)
    inp["ssm_c_im"] = nrm((DEPTH, G, H, P), (2 * P) ** -0.5)
    inp["ssm_d"] = nrm((DEPTH, G, H), 1.0)
    inp["ssm_w_glu"] = nrm((DEPTH, D_SSM, D_SSM), D_SSM ** -0.5)
    inp["sgu_ln_g"] = 1.0 + nrm((DEPTH, D_SGU), 0.02)
    inp["sgu_ln_b"] = nrm((DEPTH, D_SGU), 0.02)
    inp["sgu_w_s"] = nrm((DEPTH, SGU_HEADS, SGU_BLOCK, SGU_BLOCK), SGU_BLOCK ** -0.5)
    inp["sgu_b_s"] = 1.0 + nrm((DEPTH, SGU_HEADS, SGU_BLOCK), 0.1)
    inp["out_g_ssm"] = 1.0 + nrm((DEPTH, D_SSM), 0.02)
    inp["out_g_sgu"] = 1.0 + nrm((DEPTH, D_SGU), 0.02)
    inp["w_out"] = nrm((DEPTH, D_MODEL, D_MODEL), BETA * D_MODEL ** -0.5)
    inp["ln1_g"] = 1.0 + nrm((DEPTH, D_MODEL), 0.02)
    inp["ln1_b"] = nrm((DEPTH, D_MODEL), 0.02)
    inp["w_ffn_gate"] = nrm((DEPTH, D_MODEL, D_FF), D_MODEL ** -0.5)
    inp["w_ffn_up"] = nrm((DEPTH, D_MODEL, D_FF), D_MODEL ** -0.5)
    inp["w_ffn_down"] = nrm((DEPTH, D_FF, D_MODEL), BETA * D_FF ** -0.5)
    inp["w_ple"] = nrm((DEPTH, PLE_DIM, D_MODEL), BETA * PLE_DIM ** -0.5)
    inp["w_ple_gate"] = nrm((DEPTH, D_MODEL, D_MODEL), D_MODEL ** -0.5)
    inp["b_ple_gate"] = nrm((DEPTH, D_MODEL), 0.02)
    inp["ln2_g"] = 1.0 + nrm((DEPTH, D_MODEL), 0.02)
    inp["ln2_b"] = nrm((DEPTH, D_MODEL), 0.02)
    return inp


def reference(x, p, emb_ln_g, emb_ln_b, w_in, ssm_a_re, ssm_a_im, ssm_log_dt,
              ssm_b_re, ssm_b_im, ssm_c_re, ssm_c_im, ssm_d, ssm_w_glu,
              sgu_ln_g, sgu_ln_b, sgu_w_s, sgu_b_s, out_g_ssm, out_g_sgu, w_out,
              ln1_g, ln1_b, w_ffn_gate, w_ffn_up, w_ffn_down, w_ple, w_ple_gate,
              b_ple_gate, ln2_g, ln2_b):
    h = layer_norm(x, emb_ln_g, emb_ln_b)
    for i in range(DEPTH):
        z = h @ w_in[i]
        y_ssm = s5_mixer(z[..., :D_SSM], ssm_a_re[i], ssm_a_im[i], ssm_log_dt[i],
                         ssm_b_re[i], ssm_b_im[i], ssm_c_re[i], ssm_c_im[i],
                         ssm_d[i], ssm_w_glu[i])
        y_sgu = sgu_mixer(z[..., D_SSM:], sgu_ln_g[i], sgu_ln_b[i], sgu_w_s[i], sgu_b_s[i])
        mix = jnp.concatenate([rms_norm(y_ssm, out_g_ssm[i]),
                               rms_norm(y_sgu, out_g_sgu[i])], axis=-1) @ w_out[i]
        h = layer_norm(ALPHA * h + mix, ln1_g[i], ln1_b[i])
        ffn = (jax.nn.silu(h @ w_ffn_gate[i]) * (h @ w_ffn_up[i])) @ w_ffn_down[i]
        ple = (p[i] @ w_ple[i]) * jax.nn.sigmoid(h @ w_ple_gate[i] + b_ple_gate[i])
        h = layer_norm(ALPHA * h + ffn + ple, ln2_g[i], ln2_b[i])
    return h
```

```python
import math
from contextlib import ExitStack
import numpy as np
import concourse.bass as bass
import concourse.mybir as mybir
from concourse.bass_utils import run_bass_kernel_spmd

F32 = mybir.dt.float32
BF16 = mybir.dt.bfloat16
ALU = mybir.AluOpType
AF = mybir.ActivationFunctionType
AX = mybir.AxisListType

D = 1024
L = 4096
DEPTH = 4
DSSM = 512
DSGU = 512
DIN = 1536
DFF = 2816
NJ = 22
PLE = 256
ALPHA = (2 * DEPTH) ** 0.25
EPS = 1e-5
UNIT = 512
NBLK = 4
TH = 64
NCH = 16
G = 32
TWO_PI = 2.0 * math.pi

GRAN = 256
ARENA = 196608
SAME_SYNC = True
RAW_ONLY = False
BATCH_LN = True


def dsize(dt):
    return 2 if dt == BF16 else 4


class Inst:
    __slots__ = ("eng", "fn", "deps", "signal", "val", "sem", "is_dma", "name", "idx")


class Tile:
    def __init__(self, ap, off, nbytes):
        self.ap = ap
        self.off = off
        self.nbytes = nbytes

    def res(self, lo=None, hi=None):
        if lo is None:
            return ("sb", self.off, self.off + self.nbytes)
        return ("sb", self.off + lo, self.off + hi)


class Prog:
    ENGS = ("pe", "act", "dve", "pool", "sp")

    def __init__(self, nc, es):
        self.nc = nc
        self.es = es
        self.streams = {e: [] for e in self.ENGS}
        self.lastw = {}
        self.readers = {}
        self.arena = es.enter_context(nc.sbuf_tensor("arena", [128, ARENA // 2], BF16))
        self.top = 0
        self.peak = 0
        self.psum = es.enter_context(nc.psum_tensor("psum", [128, 8, 512], F32))
        self.dma_slots = {}
        self.esem = {}
        self.ninst = 0

    def alloc(self, shape, dt):
        n = 1
        for s in shape:
            n *= s
        nbytes = n * dsize(dt)
        off = self.top
        self.top = (off + nbytes + GRAN - 1) // GRAN * GRAN
        self.peak = max(self.peak, self.top)
        assert self.top <= ARENA, f"SBUF arena overflow {self.top}"
        v = self.arena[:, off // 2:(off + nbytes) // 2]
        if dt != BF16:
            v = v.bitcast(dt)
        if len(shape) == 2:
            v = v.rearrange("p (a b) -> p a b", b=shape[1])
        elif len(shape) == 3:
            v = v.rearrange("p (a b c) -> p a b c", b=shape[1], c=shape[2])
        elif len(shape) == 4:
            v = v.rearrange("p (a b c d) -> p a b c d", b=shape[1], c=shape[2], d=shape[3])
        return Tile(v, off, nbytes)

    def alloc_at(self, off, shape, dt):
        top0, peak0 = self.top, self.peak
        self.top = off
        t = self.alloc(shape, dt)
        nxt = self.top
        self.top = top0
        self.peak = max(peak0, nxt)
        return t, nxt

    def mark(self):
        return self.top

    def release(self, m):
        self.top = m

    def ps(self, bank, dt=F32):
        v = self.psum[:, bank, :]
        if dt == BF16:
            v = v.bitcast(BF16)
        return v

    @staticmethod
    def psr(bank, lo=0, hi=2048, nb=1):
        return ("ps", bank * 2048 + lo, (bank + nb - 1) * 2048 + hi)

    def _gran(self, res):
        sp, lo, hi = res
        if sp in ("sb", "ps"):
            return [(sp, g) for g in range(lo // GRAN, (hi + GRAN - 1) // GRAN)]
        return [(sp, g) for g in range(lo, hi)]

    def _record(self, inst, r, w):
        deps = {}

        def add(d, raw=True):
            if d is None or d is inst:
                return
            if RAW_ONLY and (not raw) and (not d.is_dma) and d.eng == inst.eng and not inst.is_dma:
                return
            if d.is_dma:
                deps[("dma", id(d.sem))] = d if ("dma", id(d.sem)) not in deps or deps[("dma", id(d.sem))].val < d.val else deps[("dma", id(d.sem))]
            else:
                if d.eng == inst.eng and not inst.is_dma and (inst.eng == "pe" or not SAME_SYNC):
                    return
                k = ("e", d.eng)
                if k not in deps or deps[k].idx < d.idx:
                    deps[k] = d
        for res in r:
            for g in self._gran(res):
                add(self.lastw.get(g))
        for res in w:
            for g in self._gran(res):
                add(self.lastw.get(g), False)
                rd = self.readers.get(g)
                if rd:
                    for d in rd.values():
                        add(d, False)
        for res in r:
            for g in self._gran(res):
                self.readers.setdefault(g, {})[(inst.eng, inst.is_dma and id(inst.sem))] = inst
        for res in w:
            for g in self._gran(res):
                self.lastw[g] = inst
                self.readers[g] = {}
        inst.deps = list(deps.values())
        for d in inst.deps:
            d.signal = True

    def op(self, eng, fn, r=(), w=(), name=""):
        inst = _mk(eng, fn, name)
        inst.idx = len(self.streams[eng])
        self._record(inst, r, w)
        self.streams[eng].append(inst)
        self.ninst += 1
        return inst

    def dma(self, q, out, in_, r=(), w=(), key=None, **kw):
        assert key is not None
        slot = self.dma_slots.get(key)
        if slot is None:
            sem = self.es.enter_context(self.nc.semaphore("d_" + str(key)))
            slot = [sem, 0, None]
            self.dma_slots[key] = slot
        inst = _mk(q, None, "dma_" + str(key))
        inst.is_dma = True
        inst.sem = slot[0]
        slot[1] += 1
        inst.val = 16 * slot[1]
        inst.signal = True
        inst.idx = len(self.streams[q])
        inst.fn = (lambda e, o=out, i=in_, k=kw: e.dma_start(out=o, in_=i, **k))
        self._record(inst, r, w)
        if slot[2] is not None and slot[2] not in inst.deps:
            inst.deps.append(slot[2])
        slot[2] = inst
        self.streams[q].append(inst)
        self.ninst += 1
        return inst

    def emit(self):
        nc = self.nc
        for e in ("pe", "act", "dve", "pool"):
            self.esem[e] = self.es.enter_context(nc.semaphore("e_" + e))
        for e in ("pe", "act", "dve", "pool"):
            c = 0
            for inst in self.streams[e]:
                if inst.is_dma:
                    continue
                inst.sem = self.esem[e]
                if inst.signal:
                    c += 1
                    inst.val = c
        for e in self.ENGS:
            for inst in self.streams[e]:
                if not inst.is_dma and inst.signal:
                    assert e != "sp"
        block = self.es.enter_context(nc.Block())

        def run(eng_name):
            def body(e):
                waited = {}
                for inst in self.streams[eng_name]:
                    for d in inst.deps:
                        k = id(d.sem)
                        if waited.get(k, 0) < d.val:
                            e.wait_ge(d.sem, d.val)
                            waited[k] = d.val
                    if inst.fn is None:
                        continue
                    h = inst.fn(e)
                    if inst.is_dma:
                        h.then_inc(inst.sem, 16)
                    elif inst.signal:
                        h.then_inc(inst.sem, 1)
            return body
        block.tensor(run("pe"))
        block.scalar(run("act"))
        block.vector(run("dve"))
        block.gpsimd(run("pool"))
        block.sync(run("sp"))


def _mk(eng, fn, name):
    i = Inst()
    i.eng = eng
    i.fn = fn
    i.deps = []
    i.signal = False
    i.val = 0
    i.sem = None
    i.is_dma = False
    i.name = name
    return i


class Builder:
    def __init__(self, nc, cfg):
        self.nc = nc
        self.cfg = cfg
        self.es = ExitStack()
        self.P = Prog(nc, self.es)
        self.rr = 0

    def ew(self):
        self.rr += 1
        return ("act", "dve", "pool")[self.rr % 3]

    def copy(self, eng, out, in_, r, w):
        if eng == "act":
            self.P.op("act", lambda e: e.activation(out=out, in_=in_, func=AF.Copy), r, w)
        else:
            self.P.op(eng, lambda e: e.tensor_copy(out=out, in_=in_), r, w)

    def tt(self, eng, out, a, b, op, r, w):
        self.P.op(eng, lambda e: e.tensor_tensor(out=out, in0=a, in1=b, op=op), r, w)

    def mm(self, out, lhsT, rhs, start, stop, r, w):
        self.P.op("pe", lambda e: e.matmul(out, lhsT=lhsT, rhs=rhs, start=start, stop=stop), r, w)

    def tr(self, out, in_, ident, r, w):
        self.P.op("pe", lambda e: e.transpose(out=out, in_=in_, identity=ident), r, w)

    def declare(self):
        nc = self.nc
        dt = lambda n, s, d=F32, k="ExternalInput": nc.dram_tensor(n, s, d, kind=k).ap()
        I = {}
        LT = self.cfg.get("ltok", L)
        self.LT = LT
        I["x"] = dt("x", [LT, D])
        I["p"] = dt("p", [DEPTH, LT, PLE])
        I["car_in"] = dt("car_in", [128, DEPTH * 2 * G])
        self.car_out = dt("car_out", [128, DEPTH * 2 * G], F32, "ExternalOutput")
        I["emb_ln_g"] = dt("emb_ln_g", [D])
        I["emb_ln_b"] = dt("emb_ln_b", [D])
        I["w_in"] = dt("w_in", [DEPTH, D, DIN])
        I["ssm_a_re"] = dt("ssm_a_re", [DEPTH, G, 64])
        I["ssm_a_im"] = dt("ssm_a_im", [DEPTH, G, 64])
        I["ssm_log_dt"] = dt("ssm_log_dt", [DEPTH, G])
        I["ssm_b_re"] = dt("ssm_b_re", [DEPTH, G, 64, 16])
        I["ssm_b_im"] = dt("ssm_b_im", [DEPTH, G, 64, 16])
        I["ssm_c_re"] = dt("ssm_c_re", [DEPTH, G, 16, 64])
        I["ssm_c_im"] = dt("ssm_c_im", [DEPTH, G, 16, 64])
        I["ssm_d"] = dt("ssm_d", [DEPTH, G, 16])
        I["ssm_w_glu"] = dt("ssm_w_glu", [DEPTH, DSSM, DSSM])
        I["sgu_ln_g"] = dt("sgu_ln_g", [DEPTH, DSGU])
        I["sgu_ln_b"] = dt("sgu_ln_b", [DEPTH, DSGU])
        I["sgu_w_s"] = dt("sgu_w_s", [DEPTH, 8, 128, 128])
        I["sgu_b_s"] = dt("sgu_b_s", [DEPTH, 8, 128])
        I["out_g_ssm"] = dt("out_g_ssm", [DEPTH, DSSM])
        I["out_g_sgu"] = dt("out_g_sgu", [DEPTH, DSGU])
        I["w_out"] = dt("w_out", [DEPTH, D, D])
        I["ln1_g"] = dt("ln1_g", [DEPTH, D])
        I["ln1_b"] = dt("ln1_b", [DEPTH, D])
        I["w_ffn_gate"] = dt("w_ffn_gate", [DEPTH, D, DFF])
        I["w_ffn_up"] = dt("w_ffn_up", [DEPTH, D, DFF])
        I["w_ffn_down"] = dt("w_ffn_down", [DEPTH, DFF, D])
        I["w_ple"] = dt("w_ple", [DEPTH, PLE, D])
        I["w_ple_gate"] = dt("w_ple_gate", [DEPTH, D, D])
        I["b_ple_gate"] = dt("b_ple_gate", [DEPTH, D])
        I["ln2_g"] = dt("ln2_g", [DEPTH, D])
        I["ln2_b"] = dt("ln2_b", [DEPTH, D])
        self.I = I
        self.y = dt("y", [LT, D], F32, "ExternalOutput")
        sk = "ExternalOutput" if self.cfg.get("dump_scratch") else "Internal"
        S = {}
        S["win"] = dt("s_win", [DEPTH, 128, 8 * DIN], BF16, sk)
        S["wglu"] = dt("s_wglu", [DEPTH, 128, 4 * DSSM], BF16, sk)
        S["wout"] = dt("s_wout", [DEPTH, 128, 8 * D], BF16, sk)
        S["wg"] = dt("s_wg", [DEPTH, 128, NJ * 8 * 128], BF16, sk)
        S["wu"] = dt("s_wu", [DEPTH, 128, NJ * 8 * 128], BF16, sk)
        S["wd"] = dt("s_wd", [DEPTH, 128, NJ * D], BF16, sk)
        S["wple"] = dt("s_wple", [DEPTH, 128, 2 * D], BF16, sk)
        S["wpg"] = dt("s_wpg", [DEPTH, 128, 8 * D], BF16, sk)
        S["wst"] = dt("s_wst", [DEPTH, 128, 8 * 128], BF16, sk)
        S["ssm"] = dt("s_ssm", [DEPTH, 4, 128, G * 128], BF16, sk)
        S["coef"] = dt("s_coef", [DEPTH, 128, 8 * G], F32, sk)
        self.S = S

    def consts(self):
        P = self.P
        self.idb = P.alloc([128], BF16)
        self.idf = P.alloc([128], F32)
        self.onesb = P.alloc([128], BF16)
        self.msk = P.alloc([128], F32)
        self.epst = P.alloc([1], F32)
        P.op("pool", lambda e: e.memset(self.epst.ap, EPS), [], [self.epst.res()])
        idf, idb, onesb, msk = self.idf, self.idb, self.onesb, self.msk
        P.op("pool", lambda e: e.memset(idf.ap, 0.0), [], [idf.res()])
        P.op("pool", lambda e: e.affine_select(out=idf.ap, in_=idf.ap, pattern=[[-1, 128]], compare_op=ALU.not_equal,
                                               fill=1.0, base=0, channel_multiplier=1), [idf.res()], [idf.res()])
        P.op("pool", lambda e: e.tensor_copy(out=idb.ap, in_=idf.ap), [idf.res()], [idb.res()])
        P.op("pool", lambda e: e.memset(onesb.ap, 1.0 / 512.0), [], [onesb.res()])
        P.op("pool", lambda e: e.memset(msk.ap, 1.0), [], [msk.res()])
        mv = msk.ap.rearrange("p (t h) -> p t h", h=16)
        P.op("pool", lambda e: e.affine_select(out=mv, in_=mv, pattern=[[16, 8], [0, 16]], compare_op=ALU.is_ge,
                                               fill=0.0, base=15, channel_multiplier=-1), [msk.res()], [msk.res()])

    def convert_weights(self, layers):
        P = self.P
        I, S = self.I, self.S
        m0 = P.mark()
        stg = [P.alloc([DFF], F32) for _ in range(3)]
        asm = [P.alloc([NJ * 8 * 128], BF16) for _ in range(2)]
        self.cv_i = 0
        self.cv_a = 0

        def piece(src, ncols, dst_view, asm_t):
            i = self.cv_i
            self.cv_i += 1
            st = stg[i % 3]
            sv = st.ap[:, 0:ncols]
            q = "sp"
            P.dma(q, sv, src, r=[], w=[st.res()], key=("stg", i % 3))
            eng = ("act", "pool")[i % 2]
            svv = sv
            if len(dst_view.shape) == 3:
                svv = sv.rearrange("p (a b) -> p a b", b=dst_view.shape[2])
            self.copy(eng, dst_view, svv, [st.res()], [asm_t.res()])

        def whole(name, l, srcs, ncols, place):
            a = asm[self.cv_a % 2]
            self.cv_a += 1
            for idx, src in enumerate(srcs):
                piece(src, ncols, place(a, idx), a)
            tot = S[name].shape[2]
            P.dma("sp", S[name][l], a.ap[:, 0:tot], r=[a.res()], w=[("d:" + name, l, l + 1)], key=("asm", self.cv_a % 2))

        for l in layers:
            whole("win", l, [I["w_in"][l, 128 * k:128 * k + 128, :] for k in range(8)], DIN,
                  lambda a, k: a.ap[:, k * DIN:(k + 1) * DIN])
            whole("wglu", l, [I["ssm_w_glu"][l, 128 * k:128 * k + 128, :] for k in range(4)], DSSM,
                  lambda a, k: a.ap[:, k * DSSM:(k + 1) * DSSM])
            whole("wout", l, [I["w_out"][l, 128 * k:128 * k + 128, :] for k in range(8)], D,
                  lambda a, k: a.ap[:, k * D:(k + 1) * D])
            whole("wpg", l, [I["w_ple_gate"][l, 128 * k:128 * k + 128, :] for k in range(8)], D,
                  lambda a, k: a.ap[:, k * D:(k + 1) * D])
            whole("wple", l, [I["w_ple"][l, 128 * k:128 * k + 128, :] for k in range(2)], D,
                  lambda a, k: a.ap[:, k * D:(k + 1) * D])
            whole("wd", l, [I["w_ffn_down"][l, 128 * j:128 * j + 128, :] for j in range(NJ)], D,
                  lambda a, j: a.ap[:, j * D:(j + 1) * D])
            for nm, key in (("wg", "w_ffn_gate"), ("wu", "w_ffn_up")):
                whole(nm, l, [I[key][l, 128 * k:128 * k + 128, :] for k in range(8)], DFF,
                      lambda a, k: a.ap.rearrange("p (j k n) -> p j k n", k=8, n=128)[:, :, k, :])
        P.release(m0)

    def convert_wst(self, layers):
        P = self.P
        I, S = self.I, self.S
        m0 = P.mark()
        for l in layers:
            ws = P.alloc([8, 128], F32)
            wsb = P.alloc([8, 128], BF16)
            wt = P.alloc([8, 128], BF16)
            P.dma("sp", ws.ap, I["sgu_w_s"][l].rearrange("h i j -> i h j"), r=[], w=[ws.res()], key=("ws",))
            P.op("pool", lambda e, ws=ws: e.memset(ws.ap[0:64, :, 64:128], 0.0), [ws.res()], [ws.res()])
            self.copy("dve", wsb.ap, ws.ap, [ws.res()], [wsb.res()])
            pb = P.ps(0, BF16)
            for h in range(8):
                self.tr(pb[:, h * 128:(h + 1) * 128], wsb.ap[:, h, :], self.idb.ap, [wsb.res(), self.idb.res()], [P.psr(0)])
            self.copy("act", wt.ap, pb.rearrange("p (h i) -> p h i", i=128), [P.psr(0)], [wt.res()])
            P.dma("sp", S["wst"][l], wt.ap.rearrange("p h i -> p (h i)"), r=[wt.res()], w=[("d:wst", l, l + 1)], key=("wst",))
        P.release(m0)

    def ssm_prologue(self, layers):
        P = self.P
        I, S = self.I, self.S
        NP = 25
        m0 = P.mark()
        for l in layers:
            m1 = P.mark()
            ar = P.alloc([G], F32)
            ai = P.alloc([G], F32)
            dtt = P.alloc([G], F32)
            arow = P.alloc([2, 128], F32)
            for (j_, key_) in ((0, "ssm_a_re"), (1, "ssm_a_im")):
                for half in range(2):
                    P.dma("sp", arow.ap[0:G, j_, 64 * half:64 * half + 64], I[key_][l], r=[], w=[arow.res()], key=("arow",))
            for (j_, dst_) in ((0, ar), (1, ai)):
                self.tr(P.ps(1)[:, j_ * G:(j_ + 1) * G], arow.ap[0:G, j_, :], self.idf.ap[0:G, 0:G], [arow.res(), self.idf.res()], [P.psr(1)])
            self.copy("act", ar.ap, P.ps(1)[:, 0:G], [P.psr(1)], [ar.res()])
            self.copy("act", ai.ap, P.ps(1)[:, G:2 * G], [P.psr(1)], [ai.res()])
            P.dma("sp", dtt.ap, I["ssm_log_dt"][l:l + 1, :].broadcast_to([128, G]),
                  r=[], w=[dtt.res()], key=("arow",))
            P.op("act", lambda e, t=dtt: e.activation(out=t.ap, in_=t.ap, func=AF.Exp), [dtt.res()], [dtt.res()])
            ad = P.alloc([G], F32)
            th = P.alloc([G], F32)
            self.tt("dve", ad.ap, ar.ap, dtt.ap, ALU.mult, [ar.res(), dtt.res()], [ad.res()])
            self.tt("dve", th.ap, ai.ap, dtt.ap, ALU.mult, [ai.res(), dtt.res()], [th.res()])
            nn = P.alloc([G, NP], F32)
            for (lo, cnt, step, base) in ((0, 8, -1, 0), (8, 8, -1, 7), (16, 9, 1, 0)):
                P.op("pool", lambda e, t=nn, lo=lo, cnt=cnt, step=step, base=base: e.iota(
                    t.ap[:, :, lo:lo + cnt], pattern=[[0, G], [step, cnt]], base=base, channel_multiplier=0,
                    allow_small_or_imprecise_dtypes=True), [], [nn.res()])
            mg = P.alloc([G, NP], F32)
            ph = P.alloc([G, NP], F32)
            pwr = P.alloc([G, NP], F32)
            pwi = P.alloc([G, NP], F32)
            adb = ad.ap.unsqueeze(2).broadcast_to([128, G, NP])
            thb = th.ap.unsqueeze(2).broadcast_to([128, G, NP])
            self.tt("dve", mg.ap, nn.ap, adb, ALU.mult, [nn.res(), ad.res()], [mg.res()])
            P.op("act", lambda e, t=mg: e.activation(out=t.ap, in_=t.ap, func=AF.Exp), [mg.res()], [mg.res()])
            self.tt("dve", ph.ap, nn.ap, thb, ALU.mult, [nn.res(), th.res()], [ph.res()])
            tmpa = P.alloc([G, NP], F32)
            tmpf = P.alloc([G, NP], F32)
            tmpi = P.alloc([G, NP], mybir.dt.int32)
            for (dst, shift) in ((pwi, 0.0), (pwr, 0.25)):
                P.op("dve", lambda e, s=shift: e.tensor_scalar(out=tmpa.ap, in0=ph.ap, scalar1=1.0 / TWO_PI, scalar2=s + 64.0,
                                                               op0=ALU.mult, op1=ALU.add), [ph.res()], [tmpa.res()])
                P.op("dve", lambda e: e.tensor_copy(out=tmpi.ap, in_=tmpa.ap), [tmpa.res()], [tmpi.res()])
                P.op("dve", lambda e: e.tensor_copy(out=tmpf.ap, in_=tmpi.ap), [tmpi.res()], [tmpf.res()])
                self.tt("dve", tmpa.ap, tmpa.ap, tmpf.ap, ALU.subtract, [tmpa.res(), tmpf.res()], [tmpa.res()])
                P.op("dve", lambda e: e.tensor_scalar(out=tmpf.ap, in0=tmpa.ap, scalar1=0.5, scalar2=None, op0=ALU.is_gt), [tmpa.res()], [tmpf.res()])
                self.tt("dve", tmpa.ap, tmpa.ap, tmpf.ap, ALU.subtract, [tmpa.res(), tmpf.res()], [tmpa.res()])
                P.op("dve", lambda e: e.tensor_scalar(out=tmpa.ap, in0=tmpa.ap, scalar1=0.49999, scalar2=-0.49999, op0=ALU.min, op1=ALU.max),
                     [tmpa.res()], [tmpa.res()])
                P.op("act", lambda e, d=dst: e.activation(out=d.ap, in_=tmpa.ap, func=AF.Sin, scale=TWO_PI), [tmpa.res()], [dst.res()])
                self.tt("dve", dst.ap, dst.ap, mg.ap, ALU.mult, [dst.res(), mg.res()], [dst.res()])
            cf = P.alloc([8, G], F32)
            self.copy("dve", cf.ap[:, 0, :], pwr.ap[:, :, 24], [pwr.res()], [cf.res()])
            self.copy("dve", cf.ap[:, 1, :], pwi.ap[:, :, 24], [pwi.res()], [cf.res()])
            t1 = P.alloc([G], F32)
            t2 = P.alloc([G], F32)

            def cmul(o, a, b):
                self.tt("dve", t1.ap, cf.ap[:, 2 * a, :], cf.ap[:, 2 * b, :], ALU.mult, [cf.res()], [t1.res()])
                self.tt("dve", t2.ap, cf.ap[:, 2 * a + 1, :], cf.ap[:, 2 * b + 1, :], ALU.mult, [cf.res()], [t2.res()])
                self.tt("dve", cf.ap[:, 2 * o, :], t1.ap, t2.ap, ALU.subtract, [t1.res(), t2.res()], [cf.res()])
                self.tt("dve", t1.ap, cf.ap[:, 2 * a, :], cf.ap[:, 2 * b + 1, :], ALU.mult, [cf.res()], [t1.res()])
                self.tt("dve", t2.ap, cf.ap[:, 2 * a + 1, :], cf.ap[:, 2 * b, :], ALU.mult, [cf.res()], [t2.res()])
                self.tt("dve", cf.ap[:, 2 * o + 1, :], t1.ap, t2.ap, ALU.add, [t1.res(), t2.res()], [cf.res()])
            cmul(1, 0, 0)
            cmul(2, 1, 0)
            cmul(3, 1, 1)
            P.dma("sp", S["coef"][l], cf.ap.rearrange("p a g -> p (a g)"), r=[cf.res()], w=[("d:coef", l, l + 1)], key=("coef",))
            er = P.alloc([G], F32)
            den = P.alloc([G], F32)
            cr = P.alloc([G], F32)
            ci = P.alloc([G], F32)
            P.op("dve", lambda e: e.tensor_scalar(out=er.ap, in0=pwr.ap[:, :, 17], scalar1=-1.0, scalar2=None, op0=ALU.add), [pwr.res()], [er.res()])
            ei = pwi.ap[:, :, 17]
            self.tt("dve", t1.ap, ar.ap, ar.ap, ALU.mult, [ar.res()], [t1.res()])
            self.tt("dve", t2.ap, ai.ap, ai.ap, ALU.mult, [ai.res()], [t2.res()])
            self.tt("dve", den.ap, t1.ap, t2.ap, ALU.add, [t1.res(), t2.res()], [den.res()])
            P.op("dve", lambda e: e.reciprocal(out=den.ap, in_=den.ap), [den.res()], [den.res()])
            self.tt("dve", t1.ap, er.ap, ar.ap, ALU.mult, [er.res(), ar.res()], [t1.res()])
            self.tt("dve", t2.ap, ei, ai.ap, ALU.mult, [pwi.res(), ai.res()], [t2.res()])
            self.tt("dve", cr.ap, t1.ap, t2.ap, ALU.add, [t1.res(), t2.res()], [cr.res()])
            self.tt("dve", cr.ap, cr.ap, den.ap, ALU.mult, [cr.res(), den.res()], [cr.res()])
            self.tt("dve", t1.ap, ei, ar.ap, ALU.mult, [pwi.res(), ar.res()], [t1.res()])
            self.tt("dve", t2.ap, er.ap, ai.ap, ALU.mult, [er.res(), ai.res()], [t2.res()])
            self.tt("dve", ci.ap, t1.ap, t2.ap, ALU.subtract, [t1.res(), t2.res()], [ci.res()])
            self.tt("dve", ci.ap, ci.ap, den.ap, ALU.mult, [ci.res(), den.res()], [ci.res()])
            bre = P.alloc([G, 16], F32)
            bim = P.alloc([G, 16], F32)
            for (key_, dst_) in (("ssm_b_re", bre), ("ssm_b_im", bim)):
                brow = P.alloc([2, 64, 16], F32)
                for half in range(2):
                    P.dma("sp", brow.ap[0:G, half], I[key_][l], r=[], w=[brow.res()], key=("brow",))
                for h_ in range(16):
                    self.tr(P.ps(1)[:, h_ * G:(h_ + 1) * G], brow.ap[0:G, :, :, h_], self.idf.ap[0:G, 0:G], [brow.res(), self.idf.res()], [P.psr(1)])
                self.copy("act", dst_.ap.rearrange("p g h -> p h g"), P.ps(1).rearrange("p (h g) -> p h g", g=G), [P.psr(1)], [dst_.res()])
            bbr = P.alloc([G, 16], F32)
            bbi = P.alloc([G, 16], F32)
            u1 = P.alloc([G, 16], F32)
            crb = cr.ap.unsqueeze(2).broadcast_to([128, G, 16])
            cib = ci.ap.unsqueeze(2).broadcast_to([128, G, 16])
            self.tt("dve", bbr.ap, bre.ap, crb, ALU.mult, [bre.res(), cr.res()], [bbr.res()])
            self.tt("dve", u1.ap, bim.ap, cib, ALU.mult, [bim.res(), ci.res()], [u1.res()])
            self.tt("dve", bbr.ap, bbr.ap, u1.ap, ALU.subtract, [bbr.res(), u1.res()], [bbr.res()])
            self.tt("dve", bbi.ap, bim.ap, crb, ALU.mult, [bim.res(), cr.res()], [bbi.res()])
            self.tt("dve", u1.ap, bre.ap, cib, ALU.mult, [bre.res(), ci.res()], [u1.res()])
            self.tt("dve", bbi.ap, bbi.ap, u1.ap, ALU.add, [bbi.res(), u1.res()], [bbi.res()])
            cre = P.alloc([G, 16], F32)
            cim = P.alloc([G, 16], F32)
            for (src, dst, nm) in ((I["ssm_c_re"], cre, "cre"), (I["ssm_c_im"], cim, "cim")):
                cl = P.alloc([4, 128], F32)
                sv = src[l].rearrange("(t a) h p -> (a h) t p", t=4)
                P.dma("sp", cl.ap[:, :, 0:64], sv, r=[], w=[cl.res()], key=("crow",))
                P.dma("sp", cl.ap[:, :, 64:128], sv, r=[], w=[cl.res()], key=("crow",))
                for t in range(4):
                    self.tr(P.ps(1)[:, t * 128:(t + 1) * 128], cl.ap[:, t, :], self.idf.ap, [cl.res(), self.idf.res()], [P.psr(1)])
                self.copy("act", dst.ap.rearrange("p g h -> p (g h)"), P.ps(1), [P.psr(1)], [dst.res()])
            big = lambda: P.alloc([G, 8, 16], F32)
            w1 = big()
            w2 = big()

            def table(dst, xr, xi, poff, top, bot):
                pr = pwr.ap[:, :, poff:poff + 8].unsqueeze(3).broadcast_to([128, G, 8, 16])
                pi = pwi.ap[:, :, poff:poff + 8].unsqueeze(3).broadcast_to([128, G, 8, 16])
                xrb = xr.ap.unsqueeze(2).broadcast_to([128, G, 8, 16])
                xib = xi.ap.unsqueeze(2).broadcast_to([128, G, 8, 16])
                for (sl, kind) in ((slice(0, 64), top), (slice(64, 128), bot)):
                    eng = "dve"
                    rs = [pwr.res(), pwi.res(), xr.res(), xi.res()]
                    if kind == "re":
                        self.tt(eng, w1.ap[sl], pr[sl], xrb[sl], ALU.mult, rs, [w1.res()])
                        self.tt(eng, w2.ap[sl], pi[sl], xib[sl], ALU.mult, rs, [w2.res()])
                        self.tt(eng, dst.ap[sl], w1.ap[sl], w2.ap[sl], ALU.subtract, [w1.res(), w2.res()], [dst.res()])
                    else:
                        self.tt(eng, w1.ap[sl], pr[sl], xib[sl], ALU.mult, rs, [w1.res()])
                        self.tt(eng, w2.ap[sl], pi[sl], xrb[sl], ALU.mult, rs, [w2.res()])
                        self.tt(eng, dst.ap[sl], w1.ap[sl], w2.ap[sl], ALU.add, [w1.res(), w2.res()], [dst.res()])
                        if kind == "-im":
                            P.op(eng, lambda e, s=sl: e.tensor_scalar(out=dst.ap[s], in0=dst.ap[s], scalar1=-1.0, scalar2=None, op0=ALU.mult),
                                 [dst.res()], [dst.res()])
            dcol = P.alloc([G], F32)
            drow = P.alloc([8, 16], F32)
            for s_ in range(8):
                P.dma("sp", drow.ap[0:G, s_, :], I["ssm_d"][l], r=[], w=[drow.res()], key=("drow",))
            self.tr(P.ps(1)[:, 0:G], drow.ap[0:G].rearrange("p s h -> p (s h)"), self.idf.ap[0:G, 0:G], [drow.res(), self.idf.res()], [P.psr(1)])
            self.copy("act", dcol.ap, P.ps(1)[:, 0:G], [P.psr(1)], [dcol.res()])
            mats = [P.alloc([G, 128], BF16) for _ in range(4)]
            tmpm = P.alloc([128], F32)
            mk = P.mark()
            ksn = big()
            table(ksn, bbr, bbi, 0, "re", "im")
            qs = big()
            table(qs, cre, cim, 16, "re", "-im")
            for g in range(G):
                b0 = 2 + (g % 3)
                self.mm(P.ps(b0)[:, 0:128], ksn.ap[:, g].rearrange("p s h -> p (s h)"), qs.ap[:, g].rearrange("p t h -> p (t h)"),
                        True, True, [ksn.res(), qs.res()], [P.psr(b0)])
                self.tt("dve", tmpm.ap, P.ps(b0)[:, 0:128], self.msk.ap, ALU.mult, [P.psr(b0), self.msk.res()], [tmpm.res()])
                P.op("dve", lambda e, g=g: e.scalar_tensor_tensor(out=mats[0].ap[:, g, :], in0=self.idf.ap, scalar=dcol.ap[:, g:g + 1],
                                                                   in1=tmpm.ap, op0=ALU.mult, op1=ALU.add),
                     [tmpm.res(), dcol.res(), self.idf.res()], [mats[0].res()])
            P.release(mk)
            for (mi, top, bot) in ((1, "re", "im"), (2, "-im", "re")):
                ks7 = big()
                table(ks7, bbr, bbi, 8, top, bot)
                for g in range(G):
                    b0 = 5 + (g % 3)
                    self.tr(P.ps(b0)[:, 0:128], ks7.ap[:, g].rearrange("p s h -> p (s h)"), self.idf.ap, [ks7.res(), self.idf.res()], [P.psr(b0)])
                    self.copy("act", mats[mi].ap[:, g, :], P.ps(b0)[:, 0:128], [P.psr(b0)], [mats[mi].res()])
                P.release(mk)
            qs1 = big()
            table(qs1, cre, cim, 17, "re", "-im")
            self.copy("act", mats[3].ap, qs1.ap.rearrange("p g t h -> p g (t h)"), [qs1.res()], [mats[3].res()])
            P.release(mk)
            for k in range(4):
                P.dma("sp", S["ssm"][l, k], mats[k].ap.rearrange("p g c -> p (g c)"), r=[mats[k].res()],
                      w=[("d:ssm", l * 4 + k, l * 4 + k + 1)], key=("ssmst",))
            P.release(m1)
        P.release(m0)

    def bc_load(self, tile, src2d, n, key):
        self.P.dma("sp", tile.ap, src2d.broadcast_to([128, n]), r=[], w=[tile.res()], key=key)

    def tbank(self):
        self.tb = 6 + (getattr(self, "tb", 7) - 5) % 2
        return self.tb

    def statset(self):
        self.si = (getattr(self, "si", -1) + 1) % 4
        return self.stat[self.si]

    def layernorm(self, src_ap, src_res, ncols, g_t, b_t, dst_ap, dst_res, tmp, add_eng="pool"):
        self.layernorm_multi([(src_ap, src_res, ncols, dst_ap, dst_res, tmp)], g_t, b_t, add_eng)

    def layernorm_multi(self, items, g_t, b_t, add_eng="pool"):
        if len(items) > 1 and not BATCH_LN:
            for it in items:
                self.layernorm_multi([it], g_t, b_t, add_eng)
            return
        P = self.P
        sets = [self.statset() for _ in items]
        for (src_ap, src_res, ncols, dst_ap, dst_res, tmp), (st, mv, rstd, nb) in zip(items, sets):
            for c in range(ncols // 512):
                P.op("dve", lambda e, c=c, st=st, src_ap=src_ap: e.bn_stats(out=st.ap[:, c, :], in_=src_ap[:, c * 512:(c + 1) * 512]), [src_res], [st.res()])
        for (src_ap, src_res, ncols, dst_ap, dst_res, tmp), (st, mv, rstd, nb) in zip(items, sets):
            nch = ncols // 512
            P.op("dve", lambda e, st=st, mv=mv, nch=nch: e.bn_aggr(out=mv.ap, in_=st.ap[:, 0:nch, :].rearrange("p c s -> p (c s)")), [st.res()], [mv.res()])
        for it, (st, mv, rstd, nb) in zip(items, sets):
            P.op("act", lambda e, mv=mv, rstd=rstd: e.activation(out=rstd.ap, in_=mv.ap[:, 1:2], func=AF.Sqrt, bias=self.epst.ap, scale=1.0),
                 [mv.res(), self.epst.res()], [rstd.res()])
        for it, (st, mv, rstd, nb) in zip(items, sets):
            P.op("dve", lambda e, rstd=rstd: e.reciprocal(out=rstd.ap, in_=rstd.ap), [rstd.res()], [rstd.res()])
            P.op("dve", lambda e, mv=mv, rstd=rstd, nb=nb: e.scalar_tensor_tensor(out=nb.ap, in0=mv.ap[:, 0:1], scalar=-1.0, in1=rstd.ap, op0=ALU.mult, op1=ALU.mult),
                 [mv.res(), rstd.res()], [nb.res()])
        for (src_ap, src_res, ncols, dst_ap, dst_res, tmp), (st, mv, rstd, nb) in zip(items, sets):
            P.op("act", lambda e, tmp=tmp, ncols=ncols, src_ap=src_ap, nb=nb, rstd=rstd: e.activation(
                out=tmp.ap[:, 0:ncols], in_=src_ap, func=AF.Identity, bias=nb.ap, scale=rstd.ap), [src_res, nb.res(), rstd.res()], [tmp.res()])
        for (src_ap, src_res, ncols, dst_ap, dst_res, tmp), _ in zip(items, sets):
            tv = tmp.ap[:, 0:ncols]
            self.tt("dve", tv, tv, g_t.ap, ALU.mult, [tmp.res(), g_t.res()], [tmp.res()])
        for (src_ap, src_res, ncols, dst_ap, dst_res, tmp), _ in zip(items, sets):
            self.tt(add_eng, dst_ap, tmp.ap[:, 0:ncols], b_t.ap, ALU.add, [tmp.res(), b_t.res()], [dst_res])

    def htres(self, T, b):
        return [T.res(k * 2 * UNIT + b * 256, k * 2 * UNIT + (b + 1) * 256) for k in range(8)]

    def to_HT(self, b):
        self.to_HT_multi([b])

    def to_HT_multi(self, blocks):
        P = self.P
        H, HT = self.H, self.HT
        for b in blocks:
            hb = self.hb[b % 4]
            self.copy("act", hb.ap, H.ap[:, b, :], [H.res(b * 4096, (b + 1) * 4096)], [hb.res()])
        banks = {}
        for b in blocks:
            hb = self.hb[b % 4]
            bank = self.tbank()
            banks[b] = bank
            pb = P.ps(bank, BF16)
            for k in range(8):
                self.tr(pb[:, k * 128:(k + 1) * 128], hb.ap[:, k * 128:(k + 1) * 128], self.idb.ap, [hb.res(), self.idb.res()], [P.psr(bank)])
            self.copy("act", HT.ap[:, :, b * 128:(b + 1) * 128], pb.rearrange("p (k t) -> p k t", t=128), [P.psr(bank)], self.htres(HT, b))

    def main(self, units, layers):
        P = self.P
        I, S = self.I, self.S
        self.H = H = P.alloc([NBLK, D], F32)
        self.HT = HT = P.alloc([8, UNIT], BF16)
        CATT = P.alloc([8, UNIT], BF16)
        CAR = P.alloc([DEPTH, 2, G], F32)
        COEF = P.alloc([DEPTH, 8, G], F32)
        self.hb = [P.alloc([D], BF16) for _ in range(4)]
        self.stat = [(P.alloc([2, 6], F32), P.alloc([2], F32), P.alloc([1], F32), P.alloc([1], F32)) for _ in range(4)]
        TMP = [P.alloc([D], F32) for _ in range(4)]
        for l in layers:
            P.dma("sp", COEF.ap[:, l].rearrange("p a g -> p (a g)"), S["coef"][l], r=[("d:coef", l, l + 1)], w=[COEF.res()], key=("coefld",))
        nunits = len(units)
        P.dma("sp", CAR.ap.rearrange("p l a g -> p (l a g)"), I["car_in"], r=[], w=[CAR.res()], key=("coefld",))
        for ui, u in enumerate(units):
            tok0 = u * UNIT
            m0 = P.mark()
            eg = P.alloc([D], F32)
            eb = P.alloc([D], F32)
            self.bc_load(eg, I["emb_ln_g"].unsqueeze(0), D, ("egb",))
            self.bc_load(eb, I["emb_ln_b"].unsqueeze(0), D, ("egb",))
            XS = [P.alloc([D], F32) for _ in range(4)]
            items = []
            for b in range(NBLK):
                xs = XS[b]
                P.dma("sp", xs.ap, I["x"][tok0 + b * 128:tok0 + (b + 1) * 128, :], r=[], w=[xs.res()], key=("xs", b % 2))
                items.append((xs.ap, xs.res(), D, H.ap[:, b, :], H.res(b * 4096, (b + 1) * 4096), TMP[b]))
            self.layernorm_multi(items[0:2], eg, eb)
            self.to_HT_multi([0, 1])
            self.layernorm_multi(items[2:4], eg, eb)
            self.to_HT_multi([2, 3])
            P.release(m0)
            if self.cfg.get("stop") == "s0":
                self.dump_H(tok0)
                continue
            for l in layers:
                self.layer(u, ui, l, tok0, CATT, CAR, COEF, TMP, last=(l == layers[-1]))
        nblk_total = self.LT // 128
        P.dma("sp", self.car_out, CAR.ap.rearrange("p l a g -> p (l a g)"), r=[CAR.res()], w=[("d:car", 0, 1)], key=("coefld",))
        P.op("sp", None, r=[("d:y", 0, nblk_total), ("d:car", 0, 1)], w=[])

    def dump_H(self, tok0):
        P = self.P
        for b in range(NBLK):
            gb = (tok0 // 128) + b
            P.dma("sp", self.y[tok0 + b * 128:tok0 + (b + 1) * 128, :], self.H.ap[:, b, :], r=[self.H.res(b * 4096, (b + 1) * 4096)],
                  w=[("d:y", gb, gb + 1)], key=("yst", b % 2))

    def layer(self, u, ui, l, tok0, CATT, CAR, COEF, TMP, last):
        P = self.P
        I, S = self.I, self.S
        H, HT = self.H, self.HT
        idb = self.idb
        mL = P.mark()
        X0 = P.top
        XSZ = 54 * 1024
        P.top += XSZ
        xo = [X0]

        def xalloc(shape, dt):
            t, nxt = P.alloc_at(xo[0], shape, dt)
            xo[0] = nxt
            assert nxt <= X0 + XSZ, "region X overflow"
            return t
        U = P.alloc([G, 8, 16], BF16)
        UT = P.alloc([G, TH], BF16)
        MT2 = [P.alloc([G, 128], BF16) for _ in range(2)]
        MT = [MT2[0], MT2[0], MT2[1], MT2[1]]
        ZZ = P.alloc([2, G, TH], F32)
        ZS1 = Tile(ZZ.ap[:, 0], ZZ.off, ZZ.nbytes)
        ZS2 = Tile(ZZ.ap[:, 1], ZZ.off, ZZ.nbytes)
        ta2 = P.alloc([2, G, NCH], F32)
        tb2 = P.alloc([2, G, NCH], F32)
        ta = Tile(ta2.ap[:, 0], ta2.off, ta2.nbytes)
        tb_ = Tile(tb2.ap[:, 0], tb2.off, tb2.nbytes)
        EE = P.alloc([2, G, NCH + 1], F32)
        ES = Tile(EE.ap[:, 0], EE.off, EE.nbytes)
        ET = Tile(EE.ap[:, 1], EE.off, EE.nbytes)
        t1 = P.alloc([2, G], F32)
        t2 = P.alloc([2, G], F32)
        C2 = P.alloc([4, 2, G], F32)
        XB = P.alloc([G, TH], BF16)
        Win = xalloc([8, DIN], BF16)
        WsT = xalloc([8, 128], BF16)
        sg_g = xalloc([DSGU], F32)
        sg_b = xalloc([DSGU], F32)
        gsgu = xalloc([DSGU], F32)
        bst = xalloc([8], F32)
        GU = xalloc([NBLK, 512], F32)
        GV = [Tile(TMP[b_].ap[:, 512:1024], TMP[b_].off + 2048, 2048) for b_ in range(4)]
        VLN = xalloc([NBLK, 512], BF16)
        Y = [xalloc([512], F32) for _ in range(2)]
        YN = [xalloc([512], BF16) for _ in range(2)]
        junk = xalloc([512], BF16)
        P.dma("sp", Win.ap.rearrange("p k n -> p (k n)"), S["win"][l], r=[("d:win", l, l + 1)], w=[Win.res()], key=("win",))
        for k in (1, 2):
            P.dma("sp", MT[k].ap.rearrange("p g c -> p (g c)"), S["ssm"][l, k], r=[("d:ssm", l * 4 + k, l * 4 + k + 1)], w=[MT[k].res()], key=("mt", k % 2))
        P.dma("sp", WsT.ap.rearrange("p h i -> p (h i)"), S["wst"][l], r=[("d:wst", l, l + 1)], w=[WsT.res()], key=("wstld",))
        self.bc_load(sg_g, I["sgu_ln_g"][l:l + 1, :], DSGU, ("sgp",))
        self.bc_load(sg_b, I["sgu_ln_b"][l:l + 1, :], DSGU, ("sgp",))
        self.bc_load(gsgu, I["out_g_sgu"][l:l + 1, :], DSGU, ("sgp",))
        P.dma("sp", bst.ap, I["sgu_b_s"][l].rearrange("h i -> i h"), r=[], w=[bst.res()], key=("sgp",), allow_slow_non_contiguous=True)
        for r in range(8):
            bank = 4 + r % 2
            for k in range(8):
                self.mm(P.ps(bank)[0:64, :], HT.ap[:, k, r:UNIT:8], Win.ap[:, k, 0:512], k == 0, k == 7, [HT.res(), Win.res()], [P.psr(bank)])
            self.copy("act", U.ap[0:64, :, r, :], P.ps(bank)[0:64, :].rearrange("p (g h) -> p g h", h=16), [P.psr(bank)], [U.res()])
        for q in range(4):
            tb = self.tbank()
            pb = P.ps(tb, BF16)
            for gl in range(8):
                g = 8 * q + gl
                self.tr(pb[:, gl * 64:(gl + 1) * 64], U.ap[0:64, g].rearrange("p r h -> p (r h)"), idb.ap[0:64, 0:64], [U.res(), idb.res()], [P.psr(tb)])
            self.copy("act", UT.ap[:, 8 * q:8 * q + 8, :], pb[:, 0:512].rearrange("p (g t) -> p g t", t=64), [P.psr(tb)],
                      [UT.res(q * 1024, (q + 1) * 1024)])
        for bt in range(4):
            b1 = (2 * bt) % 4
            b2 = (2 * bt + 1) % 4
            for gl in range(8):
                g = 8 * bt + gl
                self.mm(P.ps(b1)[:, gl * 64:(gl + 1) * 64], MT[1].ap[:, g, :], UT.ap[:, g, :], True, True, [MT[1].res(), UT.res()], [P.psr(b1)])
                self.mm(P.ps(b2)[:, gl * 64:(gl + 1) * 64], MT[2].ap[:, g, :], UT.ap[:, g, :], True, True, [MT[2].res(), UT.res()], [P.psr(b2)])
            self.copy("act", ZS1.ap[:, 8 * bt:8 * bt + 8, :], P.ps(b1).rearrange("p (g t) -> p g t", t=64), [P.psr(b1)], [ZS1.res(bt * 2048, (bt + 1) * 2048)])
            self.copy("dve", ZS2.ap[:, 8 * bt:8 * bt + 8, :], P.ps(b2).rearrange("p (g t) -> p g t", t=64), [P.psr(b2)], [ZS2.res(bt * 2048, (bt + 1) * 2048)])
        SE = "pool"
        z1 = ZS1.ap.rearrange("p g (c s) -> p g c s", s=4)
        z2 = ZS2.ap.rearrange("p g (c s) -> p g c s", s=4)
        zz = ZZ.ap.rearrange("p a g (c s) -> p a g c s", s=4)
        zzs = ZZ.ap[:, ::-1].rearrange("p a g (c s) -> p a g c s", s=4)
        cres = [COEF.res()]
        cf = lambda i: COEF.ap[:, l, i, :]
        cfb = lambda i: COEF.ap[:, l, i, :].unsqueeze(2).broadcast_to([128, G, NCH])
        for (ci_, src_i, sgn) in ((0, 0, 1.0), (2, 6, 1.0)):
            self.copy(SE, C2.ap[:, ci_, 0, :], cf(src_i), cres, [C2.res()])
            self.copy(SE, C2.ap[:, ci_, 1, :], cf(src_i), cres, [C2.res()])
            self.copy(SE, C2.ap[:, ci_ + 1, 0, :], cf(src_i + 1), cres, [C2.res()])
            P.op(SE, lambda e, ci_=ci_, src_i=src_i: e.tensor_scalar(out=C2.ap[:, ci_ + 1, 1, :], in0=cf(src_i + 1), scalar1=-1.0, scalar2=None, op0=ALU.mult),
                 cres, [C2.res()])
        c2b = lambda i: C2.ap[:, i].unsqueeze(3).broadcast_to([128, 2, G, NCH])
        for s_ in range(1, 4):
            self.tt(SE, ta2.ap, zz[:, :, :, :, s_ - 1], c2b(0), ALU.mult, [ZZ.res(), C2.res()], [ta2.res()])
            self.tt(SE, tb2.ap, zzs[:, :, :, :, s_ - 1], c2b(1), ALU.mult, [ZZ.res(), C2.res()], [tb2.res()])
            self.tt(SE, ta2.ap, ta2.ap, tb2.ap, ALU.add, [ta2.res(), tb2.res()], [ta2.res()])
            self.tt(SE, zz[:, :, :, :, s_], zz[:, :, :, :, s_], ta2.ap, ALU.add, [ZZ.res(), ta2.res()], [ZZ.res()])
        self.copy(SE, EE.ap[:, :, :, 0], CAR.ap[:, l], [CAR.res()], [EE.res()])
        ees = EE.ap[:, ::-1]
        for c in range(NCH):
            self.tt(SE, t1.ap, EE.ap[:, :, :, c], C2.ap[:, 2], ALU.mult, [EE.res(), C2.res()], [t1.res()])
            self.tt(SE, t2.ap, ees[:, :, :, c], C2.ap[:, 3], ALU.mult, [EE.res(), C2.res()], [t2.res()])
            self.tt(SE, t1.ap, t1.ap, t2.ap, ALU.add, [t1.res(), t2.res()], [t1.res()])
            self.tt(SE, EE.ap[:, :, :, c + 1], t1.ap, zz[:, :, :, c, 3], ALU.add, [t1.res(), ZZ.res()], [EE.res()])
        self.copy(SE, CAR.ap[:, l], EE.ap[:, :, :, NCH], [EE.res()], [CAR.res()])
        items = []
        for b in range(NBLK):
            bA, bB = (0, 1) if b % 2 == 0 else (2, 3)
            for (bank, c0) in ((bA, 512), (bB, 1024)):
                for k in range(8):
                    self.mm(P.ps(bank), HT.ap[:, k, b * 128:(b + 1) * 128], Win.ap[:, k, c0:c0 + 512], k == 0, k == 7,
                            [HT.res(), Win.res()], [P.psr(bank)])
            gv = GV[b]
            P.op("act", lambda e, b=b, bA=bA: e.activation(out=GU.ap[:, b, :], in_=P.ps(bA), func=AF.Gelu_apprx_tanh),
                 [P.psr(bA)], [GU.res(b * 2048, (b + 1) * 2048)])
            P.op("act", lambda e, gv=gv, bB=bB: e.activation(out=gv.ap, in_=P.ps(bB), func=AF.Gelu_apprx_tanh), [P.psr(bB)], [gv.res()])
            items.append((gv.ap, gv.res(), 512, VLN.ap[:, b, :], VLN.res(b * 1024, (b + 1) * 1024), TMP[b]))
        self.layernorm_multi(items[0:2], sg_g, sg_b, add_eng="dve")
        self.layernorm_multi(items[2:4], sg_g, sg_b, add_eng="dve")
        for b in range(NBLK):
            bank = 4 + b % 2
            for h in range(8):
                self.mm(P.ps(bank)[:, 64 * h:64 * h + 64], WsT.ap[:, h, :], VLN.ap[:, b, 64 * h:64 * h + 64], True, True,
                        [WsT.res(), VLN.res(b * 1024, (b + 1) * 1024)], [P.psr(bank)])
            y = Y[b % 2]
            yn = YN[b % 2]
            for h in range(8):
                P.op("dve", lambda e, h=h, y=y, bank=bank, b=b: e.scalar_tensor_tensor(
                    out=y.ap[:, 64 * h:64 * h + 64], in0=P.ps(bank)[:, 64 * h:64 * h + 64], scalar=bst.ap[:, h:h + 1],
                    in1=GU.ap[:, b, 64 * h:64 * h + 64], op0=ALU.add, op1=ALU.mult),
                    [P.psr(bank), bst.res(), GU.res(b * 2048, (b + 1) * 2048)], [y.res()])
            st, mv, rstd, nb = self.statset()
            P.op("act", lambda e, y=y, nb=nb: e.activation(out=junk.ap, in_=y.ap, func=AF.Square, accum_out=nb.ap), [y.res()], [junk.res(), nb.res()])
            P.op("act", lambda e, nb=nb, rstd=rstd: e.activation(out=rstd.ap, in_=nb.ap, func=AF.Sqrt, bias=self.epst.ap, scale=1.0 / 512.0),
                 [nb.res(), self.epst.res()], [rstd.res()])
            P.op("dve", lambda e, rstd=rstd: e.reciprocal(out=rstd.ap, in_=rstd.ap), [rstd.res()], [rstd.res()])
            P.op("dve", lambda e, y=y, yn=yn, rstd=rstd: e.scalar_tensor_tensor(out=yn.ap, in0=y.ap, scalar=rstd.ap, in1=gsgu.ap, op0=ALU.mult, op1=ALU.mult),
                 [y.res(), rstd.res(), gsgu.res()], [yn.res()])
            tb = self.tbank()
            pb = P.ps(tb, BF16)
            for q in range(4):
                self.tr(pb[:, q * 128:(q + 1) * 128], yn.ap[:, q * 128:(q + 1) * 128], idb.ap, [yn.res(), idb.res()], [P.psr(tb)])
            self.copy("act", CATT.ap[:, 4:8, b * 128:(b + 1) * 128], pb[:, 0:512].rearrange("p (q t) -> p q t", t=128), [P.psr(tb)],
                      self.htres(CATT, b)[4:8])
        xb = XB.ap.rearrange("p g (c s) -> p g c s", s=4)
        self.copy("dve", xb[:, :, :, 0], ES.ap[:, :, 0:NCH], [ES.res()], [XB.res()])
        for s_ in range(1, 4):
            self.tt("dve", ta.ap, ES.ap[:, :, 0:NCH], cfb(2 * (s_ - 1)), ALU.mult, [ES.res()] + cres, [ta.res()])
            self.tt("dve", tb_.ap, ET.ap[:, :, 0:NCH], cfb(2 * (s_ - 1) + 1), ALU.mult, [ET.res()] + cres, [tb_.res()])
            self.tt("dve", ta.ap, ta.ap, tb_.ap, ALU.add, [ta.res(), tb_.res()], [ta.res()])
            self.tt("dve", xb[:, :, :, s_], ta.ap, z1[:, :, :, s_ - 1], ALU.add, [ta.res(), ZS1.res()], [XB.res()])
        xo[0] = X0
        YG = xalloc([G, TH], BF16)
        YGE = xalloc([8, G, 16], BF16)
        YGT = xalloc([4, UNIT], BF16)
        Wglu = xalloc([4, DSSM], BF16)
        gssm = xalloc([4], F32)
        PT = xalloc([4, UNIT], F32)
        SQ = xalloc([4, UNIT], BF16)
        SG = [xalloc([UNIT], F32) for _ in range(2)]
        RS = xalloc([UNIT], F32)
        for k in (0, 3):
            P.dma("sp", MT[k].ap.rearrange("p g c -> p (g c)"), S["ssm"][l, k], r=[("d:ssm", l * 4 + k, l * 4 + k + 1)], w=[MT[k].res()], key=("mt", k % 2))
        P.dma("sp", Wglu.ap.rearrange("p k n -> p (k n)"), S["wglu"][l], r=[("d:wglu", l, l + 1)], w=[Wglu.res()], key=("wglu",))
        P.dma("sp", gssm.ap, I["out_g_ssm"][l].rearrange("(q p) -> p q", p=128), r=[], w=[gssm.res()], key=("wglu",), allow_slow_non_contiguous=True)
        for bt in range(4):
            bank = bt
            for gl in range(8):
                g = 8 * bt + gl
                o = P.ps(bank)[:, gl * 64:(gl + 1) * 64]
                self.mm(o, MT[0].ap[:, g, :], UT.ap[:, g, :], True, False, [MT[0].res(), UT.res()], [P.psr(bank)])
                self.mm(o, MT[3].ap[:, g, :], XB.ap[:, g, :], False, True, [MT[3].res(), XB.res()], [P.psr(bank)])
            self.copy("act", YG.ap[:, 8 * bt:8 * bt + 8, :], P.ps(bank).rearrange("p (g t) -> p g t", t=64), [P.psr(bank)],
                      [YG.res(bt * 1024, (bt + 1) * 1024)])
        for q in range(4):
            tb = self.tbank()
            pb = P.ps(tb, BF16)
            for gl in range(8):
                g = 8 * q + gl
                self.tr(pb[0:64, gl * 128:(gl + 1) * 128], YG.ap[:, g, :], idb.ap, [YG.res(), idb.res()], [P.psr(tb)])
            P.op("act", lambda e, q=q, pb=pb: e.activation(out=YGE.ap[0:64, :, 8 * q:8 * q + 8, :], in_=pb[0:64, :].rearrange("p (g r h) -> p r g h", r=8, h=16),
                                                            func=AF.Gelu_apprx_tanh), [P.psr(tb)], [YGE.res()])
        for q in range(4):
            tb = self.tbank()
            pb = P.ps(tb, BF16)
            for r in range(8):
                self.tr(pb[:, r * 64:(r + 1) * 64], YGE.ap[0:64, r, 8 * q:8 * q + 8, :].rearrange("p g h -> p (g h)"), idb.ap[0:64, 0:64], [YGE.res(), idb.res()], [P.psr(tb)])
            self.copy("dve", YGT.ap[:, q, :].rearrange("p (t r) -> p r t", r=8), pb[:, 0:512].rearrange("p (r t) -> p r t", t=64), [P.psr(tb)],
                      [YGT.res(q * 1024, (q + 1) * 1024)])
        for co in range(4):
            for k in range(4):
                self.mm(P.ps(co), Wglu.ap[:, k, co * 128:(co + 1) * 128], YGT.ap[:, k, :], k == 0, k == 3, [Wglu.res(), YGT.res()], [P.psr(co)])
            sg = SG[co % 2]
            P.op("act", lambda e, sg=sg, co=co: e.activation(out=sg.ap, in_=P.ps(co), func=AF.Sigmoid), [P.psr(co)], [sg.res()])
            self.tt("dve", PT.ap[:, co, :], sg.ap, YGT.ap[:, co, :], ALU.mult, [sg.res(), YGT.res(co * 1024, (co + 1) * 1024)], [PT.res(co * 2048, (co + 1) * 2048)])
            P.op("act", lambda e, co=co: e.activation(out=SQ.ap[:, co, :], in_=PT.ap[:, co, :], func=AF.Square),
                 [PT.res(co * 2048, (co + 1) * 2048)], [SQ.res(co * 1024, (co + 1) * 1024)])
        for co in range(4):
            self.mm(P.ps(4), self.onesb.ap, SQ.ap[:, co, :], co == 0, co == 3, [self.onesb.res(), SQ.res()], [P.psr(4)])
        P.op("act", lambda e: e.activation(out=RS.ap, in_=P.ps(4), func=AF.Sqrt, bias=self.epst.ap, scale=1.0), [P.psr(4), self.epst.res()], [RS.res()])
        P.op("dve", lambda e: e.reciprocal(out=RS.ap, in_=RS.ap), [RS.res()], [RS.res()])
        for co in range(4):
            P.op("dve", lambda e, co=co: e.scalar_tensor_tensor(out=CATT.ap[:, co, :], in0=PT.ap[:, co, :], scalar=gssm.ap[:, co:co + 1], in1=RS.ap,
                                                                 op0=ALU.mult, op1=ALU.mult),
                 [PT.res(), gssm.res(), RS.res()], [CATT.res(co * 1024, (co + 1) * 1024)])
        P.release(mL)
        mL = P.mark()
        Wout = P.alloc([8, D], BF16)
        P.dma("sp", Wout.ap.rearrange("p k n -> p (k n)"), S["wout"][l], r=[("d:wout", l, l + 1)], w=[Wout.res()], key=("wout",))
        g1 = P.alloc([D], F32)
        b1 = P.alloc([D], F32)
        self.bc_load(g1, I["ln1_g"][l:l + 1, :], D, ("lnp",))
        self.bc_load(b1, I["ln1_b"][l:l + 1, :], D, ("lnp",))
        items = []
        for b in range(NBLK):
            banks = (0, 1) if b % 2 == 0 else (2, 3)
            tmp = TMP[b]
            hres = H.res(b * 4096, (b + 1) * 4096)
            for hf in range(2):
                for k in range(8):
                    self.mm(P.ps(banks[hf]), CATT.ap[:, k, b * 128:(b + 1) * 128], Wout.ap[:, k, hf * 512:(hf + 1) * 512], k == 0, k == 7,
                            [CATT.res(), Wout.res()], [P.psr(banks[hf])])
                P.op("dve", lambda e, b=b, hf=hf, tmp=tmp, bk=banks[hf]: e.scalar_tensor_tensor(
                    out=tmp.ap[:, hf * 512:(hf + 1) * 512], in0=H.ap[:, b, hf * 512:(hf + 1) * 512], scalar=ALPHA, in1=P.ps(bk),
                    op0=ALU.mult, op1=ALU.add), [hres, P.psr(banks[hf])], [tmp.res()])
            items.append((tmp.ap, tmp.res(), D, H.ap[:, b, :], hres, tmp))
        self.layernorm_multi(items[0:2], g1, b1)
        self.to_HT_multi([0, 1])
        self.layernorm_multi(items[2:4], g1, b1)
        self.to_HT_multi([2, 3])
        P.release(mL)
        if self.cfg.get("stop") == "s4":
            self.dump_H(tok0)
            return
        mL = P.mark()
        Wd = P.alloc([NJ, D], BF16)
        ACTT = P.alloc([NJ, UNIT], BF16)
        Wpg = P.alloc([8, D], BF16)
        Wple = P.alloc([2, D], BF16)
        WG = [P.alloc([8, 128], BF16) for _ in range(3)]
        WU = [P.alloc([8, 128], BF16) for _ in range(3)]
        SIL = [P.alloc([UNIT], F32) for _ in range(2)]
        g2 = P.alloc([D], F32)
        b2 = P.alloc([D], F32)
        bpg = P.alloc([D], F32)
        PF = [P.alloc([PLE], F32) for _ in range(4)]
        PB = [P.alloc([PLE], BF16) for _ in range(4)]
        PT2 = [P.alloc([2, 128], BF16) for _ in range(4)]
        for b in range(NBLK):
            pf, pbt = PF[b], PB[b]
            P.dma("sp", pf.ap, I["p"][l, tok0 + b * 128:tok0 + (b + 1) * 128, :], r=[], w=[pf.res()], key=("pf", b % 2))
            self.copy("pool", pbt.ap, pf.ap, [pf.res()], [pbt.res()])
        for b in range(NBLK):
            pbt, pt2 = PB[b], PT2[b]
            tb = self.tbank()
            pb = P.ps(tb, BF16)
            for kk in range(2):
                self.tr(pb[:, kk * 128:(kk + 1) * 128], pbt.ap[:, kk * 128:(kk + 1) * 128], idb.ap, [pbt.res(), idb.res()], [P.psr(tb)])
            self.copy("act", pt2.ap, pb[:, 0:256].rearrange("p (k t) -> p k t", t=128), [P.psr(tb)], [pt2.res()])
        def ring_load(j):
            sl = j % 3
            P.dma("sp", WG[sl].ap.rearrange("p k n -> p (k n)"), S["wg"][l][:, j * 1024:(j + 1) * 1024], r=[("d:wg", l, l + 1)], w=[WG[sl].res()], key=("wg", sl))
            P.dma("sp", WU[sl].ap.rearrange("p k n -> p (k n)"), S["wu"][l][:, j * 1024:(j + 1) * 1024], r=[("d:wu", l, l + 1)], w=[WU[sl].res()], key=("wu", sl))
        for j in range(NJ):
            sl = j % 3
            if j == 0:
                ring_load(0)
                ring_load(1)
                P.dma("sp", Wpg.ap.rearrange("p k n -> p (k n)"), S["wpg"][l], r=[("d:wpg", l, l + 1)], w=[Wpg.res()], key=("wpg",))
                P.dma("sp", Wple.ap.rearrange("p k n -> p (k n)"), S["wple"][l], r=[("d:wple", l, l + 1)], w=[Wple.res()], key=("wpg",))
                P.dma("sp", Wd.ap.rearrange("p j n -> p (j n)"), S["wd"][l], r=[("d:wd", l, l + 1)], w=[Wd.res()], key=("wd",))
                P.dma("sp", g2.ap, I["ln2_g"][l:l + 1, :].broadcast_to([128, D]), r=[], w=[g2.res()], key=("lnp2",))
                P.dma("sp", b2.ap, I["ln2_b"][l:l + 1, :].broadcast_to([128, D]), r=[], w=[b2.res()], key=("lnp2",))
                P.dma("sp", bpg.ap, I["b_ple_gate"][l:l + 1, :].broadcast_to([128, D]), r=[], w=[bpg.res()], key=("lnp2",))
            if j + 2 < NJ:
                ring_load(j + 2)
            bG, bU = (0, 1) if j % 2 == 0 else (2, 3)
            for k in range(8):
                self.mm(P.ps(bG), WG[sl].ap[:, k, :], HT.ap[:, k, :], k == 0, k == 7, [WG[sl].res(), HT.res()], [P.psr(bG)])
            for k in range(8):
                self.mm(P.ps(bU), WU[sl].ap[:, k, :], HT.ap[:, k, :], k == 0, k == 7, [WU[sl].res(), HT.res()], [P.psr(bU)])
            sil = SIL[j % 2]
            P.op("act", lambda e, sil=sil, bG=bG: e.activation(out=sil.ap, in_=P.ps(bG), func=AF.Silu), [P.psr(bG)], [sil.res()])
            self.tt("dve", ACTT.ap[:, j, :], sil.ap, P.ps(bU), ALU.mult, [sil.res(), P.psr(bU)], [ACTT.res(j * 1024, (j + 1) * 1024)])
        pending = None
        for b in range(NBLK):
            hres = H.res(b * 4096, (b + 1) * 4096)
            tmp = TMP[b]
            TQ = TMP[(b + 1) % 4]
            PLEV = TMP[(b + 2) % 4]
            pt2 = PT2[b]
            for hf in range(2):
                for j in range(NJ):
                    self.mm(P.ps(hf), ACTT.ap[:, j, b * 128:(b + 1) * 128], Wd.ap[:, j, hf * 512:(hf + 1) * 512], j == 0, j == NJ - 1,
                            [ACTT.res(), Wd.res()], [P.psr(hf)])
                for kk in range(2):
                    self.mm(P.ps(2 + hf), pt2.ap[:, kk, :], Wple.ap[:, kk, hf * 512:(hf + 1) * 512], kk == 0, kk == 1, [pt2.res(), Wple.res()], [P.psr(2 + hf)])
                for k in range(8):
                    self.mm(P.ps(4 + hf), HT.ap[:, k, b * 128:(b + 1) * 128], Wpg.ap[:, k, hf * 512:(hf + 1) * 512], k == 0, k == 7,
                            self.htres(HT, b) + [Wpg.res()], [P.psr(4 + hf)])
                if hf == 1 and pending is not None:
                    self.to_HT(pending)
                    pending = None
                cs = slice(hf * 512, (hf + 1) * 512)
                self.tt("dve", TQ.ap[:, cs], P.ps(4 + hf), bpg.ap[:, cs], ALU.add, [P.psr(4 + hf), bpg.res()], [TQ.res(hf * 2048, (hf + 1) * 2048)])
                P.op("act", lambda e, cs=cs, TQ=TQ: e.activation(out=TQ.ap[:, cs], in_=TQ.ap[:, cs], func=AF.Sigmoid),
                     [TQ.res(hf * 2048, (hf + 1) * 2048)], [TQ.res(hf * 2048, (hf + 1) * 2048)])
                self.tt("dve", PLEV.ap[:, cs], TQ.ap[:, cs], P.ps(2 + hf), ALU.mult, [TQ.res(hf * 2048, (hf + 1) * 2048), P.psr(2 + hf)],
                        [PLEV.res(hf * 2048, (hf + 1) * 2048)])
                P.op("dve", lambda e, b=b, cs=cs, tmp=tmp, hf=hf: e.scalar_tensor_tensor(
                    out=tmp.ap[:, cs], in0=H.ap[:, b, cs], scalar=ALPHA, in1=P.ps(hf), op0=ALU.mult, op1=ALU.add),
                    [hres, P.psr(hf)], [tmp.res()])
            self.tt("pool", tmp.ap, tmp.ap, PLEV.ap, ALU.add, [tmp.res(), PLEV.res()], [tmp.res()])
            self.layernorm(tmp.ap, tmp.res(), D, g2, b2, H.ap[:, b, :], hres, tmp)
            if last:
                gb = (tok0 // 128) + b
                P.dma("sp", self.y[tok0 + b * 128:tok0 + (b + 1) * 128, :], H.ap[:, b, :], r=[hres], w=[("d:y", gb, gb + 1)], key=("yst", b % 2))
            else:
                pending = b
        if pending is not None:
            self.to_HT(pending)
        P.release(mL)


def build(cfg):
    nc = bass.Bass("TRN2", target_bir_lowering=False)
    B = Builder(nc, cfg)
    with B.es:
        B.declare()
        B.consts()
        layers = cfg.get("layers", list(range(DEPTH)))
        units = cfg.get("units", list(range(cfg.get("ltok", L) // UNIT)))
        if cfg.get("prologue", True):
            if cfg.get("p_ssm", True):
                B.ssm_prologue(layers)
            if cfg.get("p_wst", True):
                B.convert_wst(layers)
            if cfg.get("p_w", True):
                B.convert_weights(layers)
        if cfg.get("main", True):
            B.main(units, layers)
        B.P.emit()
    return nc, B


_CACHE = {}
LTOK = 4096


def kernel(**inputs):
    cfg = {"ltok": LTOK}
    if "nc" not in _CACHE:
        _CACHE["nc"] = build(cfg)[0]
    nc = _CACHE["nc"]
    wnames = [k for k in inputs if k not in ("x", "p")]
    shared = {k: np.ascontiguousarray(inputs[k], dtype=np.float32) for k in wnames}
    car = [np.zeros((128, DEPTH * 2 * G), np.float32) for _ in range(8)]
    outs = [[] for _ in range(8)]
    for k in range(L // LTOK):
        in_maps = []
        for c in range(8):
            m = dict(shared)
            m["x"] = np.ascontiguousarray(inputs["x"][c, k * LTOK:(k + 1) * LTOK], dtype=np.float32)
            m["p"] = np.ascontiguousarray(inputs["p"][:, c, k * LTOK:(k + 1) * LTOK], dtype=np.float32)
            m["car_in"] = car[c]
            in_maps.append(m)
        res = run_bass_kernel_spmd(nc, in_maps, core_ids=list(range(8)))
        for c in range(8):
            outs[c].append(np.asarray(res.results[c]["y"], dtype=np.float32))
            car[c] = np.ascontiguousarray(np.asarray(res.results[c]["car_out"], dtype=np.float32))
    return np.stack([np.concatenate(o, axis=0) for o in outs], axis=0)
```

```python
import math
from contextlib import ExitStack
import numpy as np
import concourse.bass as bass
import concourse.mybir as mybir
from concourse.bass_utils import run_bass_kernel_spmd

F32 = mybir.dt.float32
BF16 = mybir.dt.bfloat16
ALU = mybir.AluOpType
AF = mybir.ActivationFunctionType
AX = mybir.AxisListType

D = 1024
L = 4096
DEPTH = 4
DSSM = 512
DSGU = 512
DIN = 1536
DFF = 2816
NJ = 22
PLE = 256
ALPHA = (2 * DEPTH) ** 0.25
EPS = 1e-5
UNIT = 512
NBLK = 4
TH = 64
NCH = 16
G = 32
TWO_PI = 2.0 * math.pi

GRAN = 256
ARENA = 196608
SAME_SYNC = True
RAW_ONLY = False
BATCH_LN = True


def dsize(dt):
    return 2 if dt == BF16 else 4


class Inst:
    __slots__ = ("eng", "fn", "deps", "signal", "val", "sem", "is_dma", "name", "idx")


class Tile:
    def __init__(self, ap, off, nbytes):
        self.ap = ap
        self.off = off
        self.nbytes = nbytes

    def res(self, lo=None, hi=None):
        if lo is None:
            return ("sb", self.off, self.off + self.nbytes)
        return ("sb", self.off + lo, self.off + hi)


class Prog:
    ENGS = ("pe", "act", "dve", "pool", "sp")

    def __init__(self, nc, es):
        self.nc = nc
        self.es = es
        self.streams = {e: [] for e in self.ENGS}
        self.lastw = {}
        self.readers = {}
        self.arena = es.enter_context(nc.sbuf_tensor("arena", [128, ARENA // 2], BF16))
        self.top = 0
        self.peak = 0
        self.psum = es.enter_context(nc.psum_tensor("psum", [128, 8, 512], F32))
        self.dma_slots = {}
        self.esem = {}
        self.ninst = 0

    def alloc(self, shape, dt):
        n = 1
        for s in shape:
            n *= s
        nbytes = n * dsize(dt)
        off = self.top
        self.top = (off + nbytes + GRAN - 1) // GRAN * GRAN
        self.peak = max(self.peak, self.top)
        assert self.top <= ARENA, f"SBUF arena overflow {self.top}"
        v = self.arena[:, off // 2:(off + nbytes) // 2]
        if dt != BF16:
            v = v.bitcast(dt)
        if len(shape) == 2:
            v = v.rearrange("p (a b) -> p a b", b=shape[1])
        elif len(shape) == 3:
            v = v.rearrange("p (a b c) -> p a b c", b=shape[1], c=shape[2])
        elif len(shape) == 4:
            v = v.rearrange("p (a b c d) -> p a b c d", b=shape[1], c=shape[2], d=shape[3])
        return Tile(v, off, nbytes)

    def alloc_at(self, off, shape, dt):
        top0, peak0 = self.top, self.peak
        self.top = off
        t = self.alloc(shape, dt)
        nxt = self.top
        self.top = top0
        self.peak = max(peak0, nxt)
        return t, nxt

    def mark(self):
        return self.top

    def release(self, m):
        self.top = m

    def ps(self, bank, dt=F32):
        v = self.psum[:, bank, :]
        if dt == BF16:
            v = v.bitcast(BF16)
        return v

    @staticmethod
    def psr(bank, lo=0, hi=2048, nb=1):
        return ("ps", bank * 2048 + lo, (bank + nb - 1) * 2048 + hi)

    def _gran(self, res):
        sp, lo, hi = res
        if sp in ("sb", "ps"):
            return [(sp, g) for g in range(lo // GRAN, (hi + GRAN - 1) // GRAN)]
        return [(sp, g) for g in range(lo, hi)]

    def _record(self, inst, r, w):
        deps = {}

        def add(d, raw=True):
            if d is None or d is inst:
                return
            if RAW_ONLY and (not raw) and (not d.is_dma) and d.eng == inst.eng and not inst.is_dma:
                return
            if d.is_dma:
                deps[("dma", id(d.sem))] = d if ("dma", id(d.sem)) not in deps or deps[("dma", id(d.sem))].val < d.val else deps[("dma", id(d.sem))]
            else:
                if d.eng == inst.eng and not inst.is_dma and (inst.eng == "pe" or not SAME_SYNC):
                    return
                k = ("e", d.eng)
                if k not in deps or deps[k].idx < d.idx:
                    deps[k] = d
        for res in r:
            for g in self._gran(res):
                add(self.lastw.get(g))
        for res in w:
            for g in self._gran(res):
                add(self.lastw.get(g), False)
                rd = self.readers.get(g)
                if rd:
                    for d in rd.values():
                        add(d, False)
        for res in r:
            for g in self._gran(res):
                self.readers.setdefault(g, {})[(inst.eng, inst.is_dma and id(inst.sem))] = inst
        for res in w:
            for g in self._gran(res):
                self.lastw[g] = inst
                self.readers[g] = {}
        inst.deps = list(deps.values())
        for d in inst.deps:
            d.signal = True

    def op(self, eng, fn, r=(), w=(), name=""):
        inst = _mk(eng, fn, name)
        inst.idx = len(self.streams[eng])
        self._record(inst, r, w)
        self.streams[eng].append(inst)
        self.ninst += 1
        return inst

    def dma(self, q, out, in_, r=(), w=(), key=None, **kw):
        assert key is not None
        slot = self.dma_slots.get(key)
        if slot is None:
            sem = self.es.enter_context(self.nc.semaphore("d_" + str(key)))
            slot = [sem, 0, None]
            self.dma_slots[key] = slot
        inst = _mk(q, None, "dma_" + str(key))
        inst.is_dma = True
        inst.sem = slot[0]
        slot[1] += 1
        inst.val = 16 * slot[1]
        inst.signal = True
        inst.idx = len(self.streams[q])
        inst.fn = (lambda e, o=out, i=in_, k=kw: e.dma_start(out=o, in_=i, **k))
        self._record(inst, r, w)
        if slot[2] is not None and slot[2] not in inst.deps:
            inst.deps.append(slot[2])
        slot[2] = inst
        self.streams[q].append(inst)
        self.ninst += 1
        return inst

    def emit(self):
        nc = self.nc
        for e in ("pe", "act", "dve", "pool"):
            self.esem[e] = self.es.enter_context(nc.semaphore("e_" + e))
        for e in ("pe", "act", "dve", "pool"):
            c = 0
            for inst in self.streams[e]:
                if inst.is_dma:
                    continue
                inst.sem = self.esem[e]
                if inst.signal:
                    c += 1
                    inst.val = c
        for e in self.ENGS:
            for inst in self.streams[e]:
                if not inst.is_dma and inst.signal:
                    assert e != "sp"
        block = self.es.enter_context(nc.Block())

        def run(eng_name):
            def body(e):
                waited = {}
                for inst in self.streams[eng_name]:
                    for d in inst.deps:
                        k = id(d.sem)
                        if waited.get(k, 0) < d.val:
                            e.wait_ge(d.sem, d.val)
                            waited[k] = d.val
                    if inst.fn is None:
                        continue
                    h = inst.fn(e)
                    if inst.is_dma:
                        h.then_inc(inst.sem, 16)
                    elif inst.signal:
                        h.then_inc(inst.sem, 1)
            return body
        block.tensor(run("pe"))
        block.scalar(run("act"))
        block.vector(run("dve"))
        block.gpsimd(run("pool"))
        block.sync(run("sp"))


def _mk(eng, fn, name):
    i = Inst()
    i.eng = eng
    i.fn = fn
    i.deps = []
    i.signal = False
    i.val = 0
    i.sem = None
    i.is_dma = False
    i.name = name
    return i


class Builder:
    def __init__(self, nc, cfg):
        self.nc = nc
        self.cfg = cfg
        self.es = ExitStack()
        self.P = Prog(nc, self.es)
        self.rr = 0

    def ew(self):
        self.rr += 1
        return ("act", "dve", "pool")[self.rr % 3]

    def copy(self, eng, out, in_, r, w):
        if eng == "act":
            self.P.op("act", lambda e: e.activation(out=out, in_=in_, func=AF.Copy), r, w)
        else:
            self.P.op(eng, lambda e: e.tensor_copy(out=out, in_=in_), r, w)

    def tt(self, eng, out, a, b, op, r, w):
        self.P.op(eng, lambda e: e.tensor_tensor(out=out, in0=a, in1=b, op=op), r, w)

    def mm(self, out, lhsT, rhs, start, stop, r, w):
        self.P.op("pe", lambda e: e.matmul(out, lhsT=lhsT, rhs=rhs, start=start, stop=stop), r, w)

    def tr(self, out, in_, ident, r, w):
        self.P.op("pe", lambda e: e.transpose(out=out, in_=in_, identity=ident), r, w)

    def declare(self):
        nc = self.nc
        dt = lambda n, s, d=F32, k="ExternalInput": nc.dram_tensor(n, s, d, kind=k).ap()
        I = {}
        LT = self.cfg.get("ltok", L)
        self.LT = LT
        I["x"] = dt("x", [LT, D])
        I["p"] = dt("p", [DEPTH, LT, PLE])
        I["car_in"] = dt("car_in", [128, DEPTH * 2 * G])
        self.car_out = dt("car_out", [128, DEPTH * 2 * G], F32, "ExternalOutput")
        I["emb_ln_g"] = dt("emb_ln_g", [D])
        I["emb_ln_b"] = dt("emb_ln_b", [D])
        I["w_in"] = dt("w_in", [DEPTH, D, DIN])
        I["ssm_a_re"] = dt("ssm_a_re", [DEPTH, G, 64])
        I["ssm_a_im"] = dt("ssm_a_im", [DEPTH, G, 64])
        I["ssm_log_dt"] = dt("ssm_log_dt", [DEPTH, G])
        I["ssm_b_re"] = dt("ssm_b_re", [DEPTH, G, 64, 16])
        I["ssm_b_im"] = dt("ssm_b_im", [DEPTH, G, 64, 16])
        I["ssm_c_re"] = dt("ssm_c_re", [DEPTH, G, 16, 64])
        I["ssm_c_im"] = dt("ssm_c_im", [DEPTH, G, 16, 64])
        I["ssm_d"] = dt("ssm_d", [DEPTH, G, 16])
        I["ssm_w_glu"] = dt("ssm_w_glu", [DEPTH, DSSM, DSSM])
        I["sgu_ln_g"] = dt("sgu_ln_g", [DEPTH, DSGU])
        I["sgu_ln_b"] = dt("sgu_ln_b", [DEPTH, DSGU])
        I["sgu_w_s"] = dt("sgu_w_s", [DEPTH, 8, 128, 128])
        I["sgu_b_s"] = dt("sgu_b_s", [DEPTH, 8, 128])
        I["out_g_ssm"] = dt("out_g_ssm", [DEPTH, DSSM])
        I["out_g_sgu"] = dt("out_g_sgu", [DEPTH, DSGU])
        I["w_out"] = dt("w_out", [DEPTH, D, D])
        I["ln1_g"] = dt("ln1_g", [DEPTH, D])
        I["ln1_b"] = dt("ln1_b", [DEPTH, D])
        I["w_ffn_gate"] = dt("w_ffn_gate", [DEPTH, D, DFF])
        I["w_ffn_up"] = dt("w_ffn_up", [DEPTH, D, DFF])
        I["w_ffn_down"] = dt("w_ffn_down", [DEPTH, DFF, D])
        I["w_ple"] = dt("w_ple", [DEPTH, PLE, D])
        I["w_ple_gate"] = dt("w_ple_gate", [DEPTH, D, D])
        I["b_ple_gate"] = dt("b_ple_gate", [DEPTH, D])
        I["ln2_g"] = dt("ln2_g", [DEPTH, D])
        I["ln2_b"] = dt("ln2_b", [DEPTH, D])
        self.I = I
        self.y = dt("y", [LT, D], F32, "ExternalOutput")
        sk = "ExternalOutput" if self.cfg.get("dump_scratch") else "Internal"
        S = {}
        S["win"] = dt("s_win", [DEPTH, 128, 8 * DIN], BF16, sk)
        S["wglu"] = dt("s_wglu", [DEPTH, 128, 4 * DSSM], BF16, sk)
        S["wout"] = dt("s_wout", [DEPTH, 128, 8 * D], BF16, sk)
        S["wg"] = dt("s_wg", [DEPTH, 128, NJ * 8 * 128], BF16, sk)
        S["wu"] = dt("s_wu", [DEPTH, 128, NJ * 8 * 128], BF16, sk)
        S["wd"] = dt("s_wd", [DEPTH, 128, NJ * D], BF16, sk)
        S["wple"] = dt("s_wple", [DEPTH, 128, 2 * D], BF16, sk)
        S["wpg"] = dt("s_wpg", [DEPTH, 128, 8 * D], BF16, sk)
        S["wst"] = dt("s_wst", [DEPTH, 128, 8 * 128], BF16, sk)
        S["ssm"] = dt("s_ssm", [DEPTH, 4, 128, G * 128], BF16, sk)
        S["coef"] = dt("s_coef", [DEPTH, 128, 8 * G], F32, sk)
        self.S = S

    def consts(self):
        P = self.P
        self.idb = P.alloc([128], BF16)
        self.idf = P.alloc([128], F32)
        self.onesb = P.alloc([128], BF16)
        self.msk = P.alloc([128], F32)
        self.epst = P.alloc([1], F32)
        P.op("pool", lambda e: e.memset(self.epst.ap, EPS), [], [self.epst.res()])
        idf, idb, onesb, msk = self.idf, self.idb, self.onesb, self.msk
        P.op("pool", lambda e: e.memset(idf.ap, 0.0), [], [idf.res()])
        P.op("pool", lambda e: e.affine_select(out=idf.ap, in_=idf.ap, pattern=[[-1, 128]], compare_op=ALU.not_equal,
                                               fill=1.0, base=0, channel_multiplier=1), [idf.res()], [idf.res()])
        P.op("pool", lambda e: e.tensor_copy(out=idb.ap, in_=idf.ap), [idf.res()], [idb.res()])
        P.op("pool", lambda e: e.memset(onesb.ap, 1.0 / 512.0), [], [onesb.res()])
        P.op("pool", lambda e: e.memset(msk.ap, 1.0), [], [msk.res()])
        mv = msk.ap.rearrange("p (t h) -> p t h", h=16)
        P.op("pool", lambda e: e.affine_select(out=mv, in_=mv, pattern=[[16, 8], [0, 16]], compare_op=ALU.is_ge,
                                               fill=0.0, base=15, channel_multiplier=-1), [msk.res()], [msk.res()])

    def convert_weights(self, layers):
        P = self.P
        I, S = self.I, self.S
        m0 = P.mark()
        NSTG = 6
        stg = [P.alloc([DFF], F32) for _ in range(NSTG)]
        asm = [P.alloc([NJ * 8 * 128], BF16) for _ in range(2)]
        self.cv_i = 0
        self.cv_a = 0
        pending = []

        def flush():
            while pending:
                pending.pop(0)()

        def piece(src, ncols, dst_view, asm_t):
            i = self.cv_i
            self.cv_i += 1
            st = stg[i % NSTG]
            sv = st.ap[:, 0:ncols]
            q = "sp"
            P.dma(q, sv, src, r=[], w=[st.res()], key=("stg", i % NSTG))
            eng = ("act", "pool")[i % 2]
            svv = sv
            if len(dst_view.shape) == 3:
                svv = sv.rearrange("p (a b) -> p a b", b=dst_view.shape[2])
            self.copy(eng, dst_view, svv, [st.res()], [asm_t.res()])

        def whole(name, l, srcs, ncols, place):
            a = asm[self.cv_a % 2]
            self.cv_a += 1
            for idx, src in enumerate(srcs):
                piece(src, ncols, place(a, idx), a)
                if idx == 2:
                    flush()
            flush()
            tot = S[name].shape[2]
            ak = self.cv_a % 2
            pending.append(lambda name=name, l=l, a=a, tot=tot, ak=ak: P.dma(
                "sp", S[name][l], a.ap[:, 0:tot], r=[a.res()], w=[("d:" + name, l, l + 1)], key=("asm", ak)))

        for l in layers:
            whole("win", l, [I["w_in"][l, 128 * k:128 * k + 128, :] for k in range(8)], DIN,
                  lambda a, k: a.ap[:, k * DIN:(k + 1) * DIN])
            whole("wglu", l, [I["ssm_w_glu"][l, 128 * k:128 * k + 128, :] for k in range(4)], DSSM,
                  lambda a, k: a.ap[:, k * DSSM:(k + 1) * DSSM])
            whole("wout", l, [I["w_out"][l, 128 * k:128 * k + 128, :] for k in range(8)], D,
                  lambda a, k: a.ap[:, k * D:(k + 1) * D])
            whole("wpg", l, [I["w_ple_gate"][l, 128 * k:128 * k + 128, :] for k in range(8)], D,
                  lambda a, k: a.ap[:, k * D:(k + 1) * D])
            whole("wple", l, [I["w_ple"][l, 128 * k:128 * k + 128, :] for k in range(2)], D,
                  lambda a, k: a.ap[:, k * D:(k + 1) * D])
            whole("wd", l, [I["w_ffn_down"][l, 128 * j:128 * j + 128, :] for j in range(NJ)], D,
                  lambda a, j: a.ap[:, j * D:(j + 1) * D])
            for nm, key in (("wg", "w_ffn_gate"), ("wu", "w_ffn_up")):
                whole(nm, l, [I[key][l, 128 * k:128 * k + 128, :] for k in range(8)], DFF,
                      lambda a, k: a.ap.rearrange("p (j k n) -> p j k n", k=8, n=128)[:, :, k, :])
        flush()
        P.release(m0)

    def convert_wst(self, layers):
        P = self.P
        I, S = self.I, self.S
        m0 = P.mark()
        for l in layers:
            ws = P.alloc([8, 128], F32)
            wsb = P.alloc([8, 128], BF16)
            wt = P.alloc([8, 128], BF16)
            P.dma("sp", ws.ap, I["sgu_w_s"][l].rearrange("h i j -> i h j"), r=[], w=[ws.res()], key=("ws",))
            P.op("pool", lambda e, ws=ws: e.memset(ws.ap[0:64, :, 64:128], 0.0), [ws.res()], [ws.res()])
            self.copy("dve", wsb.ap, ws.ap, [ws.res()], [wsb.res()])
            pb = P.ps(0, BF16)
            for h in range(8):
                self.tr(pb[:, h * 128:(h + 1) * 128], wsb.ap[:, h, :], self.idb.ap, [wsb.res(), self.idb.res()], [P.psr(0)])
            self.copy("act", wt.ap, pb.rearrange("p (h i) -> p h i", i=128), [P.psr(0)], [wt.res()])
            P.dma("sp", S["wst"][l], wt.ap.rearrange("p h i -> p (h i)"), r=[wt.res()], w=[("d:wst", l, l + 1)], key=("wst",))
        P.release(m0)

    def ssm_prologue(self, layers):
        P = self.P
        I, S = self.I, self.S
        NP = 25
        m0 = P.mark()
        for l in layers:
            m1 = P.mark()
            ar = P.alloc([G], F32)
            ai = P.alloc([G], F32)
            dtt = P.alloc([G], F32)
            arow = P.alloc([2, 128], F32)
            for (j_, key_) in ((0, "ssm_a_re"), (1, "ssm_a_im")):
                for half in range(2):
                    P.dma("sp", arow.ap[0:G, j_, 64 * half:64 * half + 64], I[key_][l], r=[], w=[arow.res()], key=("arow",))
            for (j_, dst_) in ((0, ar), (1, ai)):
                self.tr(P.ps(1)[:, j_ * G:(j_ + 1) * G], arow.ap[0:G, j_, :], self.idf.ap[0:G, 0:G], [arow.res(), self.idf.res()], [P.psr(1)])
            self.copy("act", ar.ap, P.ps(1)[:, 0:G], [P.psr(1)], [ar.res()])
            self.copy("act", ai.ap, P.ps(1)[:, G:2 * G], [P.psr(1)], [ai.res()])
            P.dma("sp", dtt.ap, I["ssm_log_dt"][l:l + 1, :].broadcast_to([128, G]),
                  r=[], w=[dtt.res()], key=("arow",))
            P.op("act", lambda e, t=dtt: e.activation(out=t.ap, in_=t.ap, func=AF.Exp), [dtt.res()], [dtt.res()])
            ad = P.alloc([G], F32)
            th = P.alloc([G], F32)
            self.tt("dve", ad.ap, ar.ap, dtt.ap, ALU.mult, [ar.res(), dtt.res()], [ad.res()])
            self.tt("dve", th.ap, ai.ap, dtt.ap, ALU.mult, [ai.res(), dtt.res()], [th.res()])
            nn = P.alloc([G, NP], F32)
            for (lo, cnt, step, base) in ((0, 8, -1, 0), (8, 8, -1, 7), (16, 9, 1, 0)):
                P.op("pool", lambda e, t=nn, lo=lo, cnt=cnt, step=step, base=base: e.iota(
                    t.ap[:, :, lo:lo + cnt], pattern=[[0, G], [step, cnt]], base=base, channel_multiplier=0,
                    allow_small_or_imprecise_dtypes=True), [], [nn.res()])
            mg = P.alloc([G, NP], F32)
            ph = P.alloc([G, NP], F32)
            pwr = P.alloc([G, NP], F32)
            pwi = P.alloc([G, NP], F32)
            adb = ad.ap.unsqueeze(2).broadcast_to([128, G, NP])
            thb = th.ap.unsqueeze(2).broadcast_to([128, G, NP])
            self.tt("dve", mg.ap, nn.ap, adb, ALU.mult, [nn.res(), ad.res()], [mg.res()])
            P.op("act", lambda e, t=mg: e.activation(out=t.ap, in_=t.ap, func=AF.Exp), [mg.res()], [mg.res()])
            self.tt("dve", ph.ap, nn.ap, thb, ALU.mult, [nn.res(), th.res()], [ph.res()])
            tmpa = P.alloc([G, NP], F32)
            tmpf = P.alloc([G, NP], F32)
            tmpi = P.alloc([G, NP], mybir.dt.int32)
            for (dst, shift) in ((pwi, 0.0), (pwr, 0.25)):
                P.op("dve", lambda e, s=shift: e.tensor_scalar(out=tmpa.ap, in0=ph.ap, scalar1=1.0 / TWO_PI, scalar2=s + 64.0,
                                                               op0=ALU.mult, op1=ALU.add), [ph.res()], [tmpa.res()])
                P.op("dve", lambda e: e.tensor_copy(out=tmpi.ap, in_=tmpa.ap), [tmpa.res()], [tmpi.res()])
                P.op("dve", lambda e: e.tensor_copy(out=tmpf.ap, in_=tmpi.ap), [tmpi.res()], [tmpf.res()])
                self.tt("dve", tmpa.ap, tmpa.ap, tmpf.ap, ALU.subtract, [tmpa.res(), tmpf.res()], [tmpa.res()])
                P.op("dve", lambda e: e.tensor_scalar(out=tmpf.ap, in0=tmpa.ap, scalar1=0.5, scalar2=None, op0=ALU.is_gt), [tmpa.res()], [tmpf.res()])
                self.tt("dve", tmpa.ap, tmpa.ap, tmpf.ap, ALU.subtract, [tmpa.res(), tmpf.res()], [tmpa.res()])
                P.op("dve", lambda e: e.tensor_scalar(out=tmpa.ap, in0=tmpa.ap, scalar1=0.49999, scalar2=-0.49999, op0=ALU.min, op1=ALU.max),
                     [tmpa.res()], [tmpa.res()])
                P.op("act", lambda e, d=dst: e.activation(out=d.ap, in_=tmpa.ap, func=AF.Sin, scale=TWO_PI), [tmpa.res()], [dst.res()])
                self.tt("dve", dst.ap, dst.ap, mg.ap, ALU.mult, [dst.res(), mg.res()], [dst.res()])
            cf = P.alloc([8, G], F32)
            self.copy("dve", cf.ap[:, 0, :], pwr.ap[:, :, 24], [pwr.res()], [cf.res()])
            self.copy("dve", cf.ap[:, 1, :], pwi.ap[:, :, 24], [pwi.res()], [cf.res()])
            t1 = P.alloc([G], F32)
            t2 = P.alloc([G], F32)

            def cmul(o, a, b):
                self.tt("dve", t1.ap, cf.ap[:, 2 * a, :], cf.ap[:, 2 * b, :], ALU.mult, [cf.res()], [t1.res()])
                self.tt("dve", t2.ap, cf.ap[:, 2 * a + 1, :], cf.ap[:, 2 * b + 1, :], ALU.mult, [cf.res()], [t2.res()])
                self.tt("dve", cf.ap[:, 2 * o, :], t1.ap, t2.ap, ALU.subtract, [t1.res(), t2.res()], [cf.res()])
                self.tt("dve", t1.ap, cf.ap[:, 2 * a, :], cf.ap[:, 2 * b + 1, :], ALU.mult, [cf.res()], [t1.res()])
                self.tt("dve", t2.ap, cf.ap[:, 2 * a + 1, :], cf.ap[:, 2 * b, :], ALU.mult, [cf.res()], [t2.res()])
                self.tt("dve", cf.ap[:, 2 * o + 1, :], t1.ap, t2.ap, ALU.add, [t1.res(), t2.res()], [cf.res()])
            cmul(1, 0, 0)
            cmul(2, 1, 0)
            cmul(3, 1, 1)
            P.dma("sp", S["coef"][l], cf.ap.rearrange("p a g -> p (a g)"), r=[cf.res()], w=[("d:coef", l, l + 1)], key=("coef",))
            er = P.alloc([G], F32)
            den = P.alloc([G], F32)
            cr = P.alloc([G], F32)
            ci = P.alloc([G], F32)
            P.op("dve", lambda e: e.tensor_scalar(out=er.ap, in0=pwr.ap[:, :, 17], scalar1=-1.0, scalar2=None, op0=ALU.add), [pwr.res()], [er.res()])
            ei = pwi.ap[:, :, 17]
            self.tt("dve", t1.ap, ar.ap, ar.ap, ALU.mult, [ar.res()], [t1.res()])
            self.tt("dve", t2.ap, ai.ap, ai.ap, ALU.mult, [ai.res()], [t2.res()])
            self.tt("dve", den.ap, t1.ap, t2.ap, ALU.add, [t1.res(), t2.res()], [den.res()])
            P.op("dve", lambda e: e.reciprocal(out=den.ap, in_=den.ap), [den.res()], [den.res()])
            self.tt("dve", t1.ap, er.ap, ar.ap, ALU.mult, [er.res(), ar.res()], [t1.res()])
            self.tt("dve", t2.ap, ei, ai.ap, ALU.mult, [pwi.res(), ai.res()], [t2.res()])
            self.tt("dve", cr.ap, t1.ap, t2.ap, ALU.add, [t1.res(), t2.res()], [cr.res()])
            self.tt("dve", cr.ap, cr.ap, den.ap, ALU.mult, [cr.res(), den.res()], [cr.res()])
            self.tt("dve", t1.ap, ei, ar.ap, ALU.mult, [pwi.res(), ar.res()], [t1.res()])
            self.tt("dve", t2.ap, er.ap, ai.ap, ALU.mult, [er.res(), ai.res()], [t2.res()])
            self.tt("dve", ci.ap, t1.ap, t2.ap, ALU.subtract, [t1.res(), t2.res()], [ci.res()])
            self.tt("dve", ci.ap, ci.ap, den.ap, ALU.mult, [ci.res(), den.res()], [ci.res()])
            bre = P.alloc([G, 16], F32)
            bim = P.alloc([G, 16], F32)
            for (key_, dst_) in (("ssm_b_re", bre), ("ssm_b_im", bim)):
                brow = P.alloc([2, 64, 16], F32)
                for half in range(2):
                    P.dma("sp", brow.ap[0:G, half], I[key_][l], r=[], w=[brow.res()], key=("brow",))
                for h_ in range(16):
                    self.tr(P.ps(1)[:, h_ * G:(h_ + 1) * G], brow.ap[0:G, :, :, h_], self.idf.ap[0:G, 0:G], [brow.res(), self.idf.res()], [P.psr(1)])
                self.copy("act", dst_.ap.rearrange("p g h -> p h g"), P.ps(1).rearrange("p (h g) -> p h g", g=G), [P.psr(1)], [dst_.res()])
            bbr = P.alloc([G, 16], F32)
            bbi = P.alloc([G, 16], F32)
            u1 = P.alloc([G, 16], F32)
            crb = cr.ap.unsqueeze(2).broadcast_to([128, G, 16])
            cib = ci.ap.unsqueeze(2).broadcast_to([128, G, 16])
            self.tt("dve", bbr.ap, bre.ap, crb, ALU.mult, [bre.res(), cr.res()], [bbr.res()])
            self.tt("dve", u1.ap, bim.ap, cib, ALU.mult, [bim.res(), ci.res()], [u1.res()])
            self.tt("dve", bbr.ap, bbr.ap, u1.ap, ALU.subtract, [bbr.res(), u1.res()], [bbr.res()])
            self.tt("dve", bbi.ap, bim.ap, crb, ALU.mult, [bim.res(), cr.res()], [bbi.res()])
            self.tt("dve", u1.ap, bre.ap, cib, ALU.mult, [bre.res(), ci.res()], [u1.res()])
            self.tt("dve", bbi.ap, bbi.ap, u1.ap, ALU.add, [bbi.res(), u1.res()], [bbi.res()])
            cre = P.alloc([G, 16], F32)
            cim = P.alloc([G, 16], F32)
            for (src, dst, nm) in ((I["ssm_c_re"], cre, "cre"), (I["ssm_c_im"], cim, "cim")):
                cl = P.alloc([4, 128], F32)
                sv = src[l].rearrange("(t a) h p -> (a h) t p", t=4)
                P.dma("sp", cl.ap[:, :, 0:64], sv, r=[], w=[cl.res()], key=("crow",))
                P.dma("sp", cl.ap[:, :, 64:128], sv, r=[], w=[cl.res()], key=("crow",))
                for t in range(4):
                    self.tr(P.ps(1)[:, t * 128:(t + 1) * 128], cl.ap[:, t, :], self.idf.ap, [cl.res(), self.idf.res()], [P.psr(1)])
                self.copy("act", dst.ap.rearrange("p g h -> p (g h)"), P.ps(1), [P.psr(1)], [dst.res()])
            big = lambda: P.alloc([G, 8, 16], F32)
            w1 = big()
            w2 = big()

            def table(dst, xr, xi, poff, top, bot):
                pr = pwr.ap[:, :, poff:poff + 8].unsqueeze(3).broadcast_to([128, G, 8, 16])
                pi = pwi.ap[:, :, poff:poff + 8].unsqueeze(3).broadcast_to([128, G, 8, 16])
                xrb = xr.ap.unsqueeze(2).broadcast_to([128, G, 8, 16])
                xib = xi.ap.unsqueeze(2).broadcast_to([128, G, 8, 16])
                for (sl, kind) in ((slice(0, 64), top), (slice(64, 128), bot)):
                    eng = "dve"
                    rs = [pwr.res(), pwi.res(), xr.res(), xi.res()]
                    if kind == "re":
                        self.tt(eng, w1.ap[sl], pr[sl], xrb[sl], ALU.mult, rs, [w1.res()])
                        self.tt(eng, w2.ap[sl], pi[sl], xib[sl], ALU.mult, rs, [w2.res()])
                        self.tt(eng, dst.ap[sl], w1.ap[sl], w2.ap[sl], ALU.subtract, [w1.res(), w2.res()], [dst.res()])
                    else:
                        self.tt(eng, w1.ap[sl], pr[sl], xib[sl], ALU.mult, rs, [w1.res()])
                        self.tt(eng, w2.ap[sl], pi[sl], xrb[sl], ALU.mult, rs, [w2.res()])
                        self.tt(eng, dst.ap[sl], w1.ap[sl], w2.ap[sl], ALU.add, [w1.res(), w2.res()], [dst.res()])
                        if kind == "-im":
                            P.op(eng, lambda e, s=sl: e.tensor_scalar(out=dst.ap[s], in0=dst.ap[s], scalar1=-1.0, scalar2=None, op0=ALU.mult),
                                 [dst.res()], [dst.res()])
            dcol = P.alloc([G], F32)
            drow = P.alloc([8, 16], F32)
            for s_ in range(8):
                P.dma("sp", drow.ap[0:G, s_, :], I["ssm_d"][l], r=[], w=[drow.res()], key=("drow",))
            self.tr(P.ps(1)[:, 0:G], drow.ap[0:G].rearrange("p s h -> p (s h)"), self.idf.ap[0:G, 0:G], [drow.res(), self.idf.res()], [P.psr(1)])
            self.copy("act", dcol.ap, P.ps(1)[:, 0:G], [P.psr(1)], [dcol.res()])
            mats = [P.alloc([G, 128], BF16) for _ in range(4)]
            tmpm = P.alloc([128], F32)
            mk = P.mark()
            ksn = big()
            table(ksn, bbr, bbi, 0, "re", "im")
            qs = big()
            table(qs, cre, cim, 16, "re", "-im")
            for g in range(G):
                b0 = 2 + (g % 3)
                self.mm(P.ps(b0)[:, 0:128], ksn.ap[:, g].rearrange("p s h -> p (s h)"), qs.ap[:, g].rearrange("p t h -> p (t h)"),
                        True, True, [ksn.res(), qs.res()], [P.psr(b0)])
                self.tt("dve", tmpm.ap, P.ps(b0)[:, 0:128], self.msk.ap, ALU.mult, [P.psr(b0), self.msk.res()], [tmpm.res()])
                P.op("dve", lambda e, g=g: e.scalar_tensor_tensor(out=mats[0].ap[:, g, :], in0=self.idf.ap, scalar=dcol.ap[:, g:g + 1],
                                                                   in1=tmpm.ap, op0=ALU.mult, op1=ALU.add),
                     [tmpm.res(), dcol.res(), self.idf.res()], [mats[0].res()])
            P.release(mk)
            for (mi, top, bot) in ((1, "re", "im"), (2, "-im", "re")):
                ks7 = big()
                table(ks7, bbr, bbi, 8, top, bot)
                for g in range(G):
                    b0 = 5 + (g % 3)
                    self.tr(P.ps(b0)[:, 0:128], ks7.ap[:, g].rearrange("p s h -> p (s h)"), self.idf.ap, [ks7.res(), self.idf.res()], [P.psr(b0)])
                    self.copy("act", mats[mi].ap[:, g, :], P.ps(b0)[:, 0:128], [P.psr(b0)], [mats[mi].res()])
                P.release(mk)
            qs1 = big()
            table(qs1, cre, cim, 17, "re", "-im")
            self.copy("act", mats[3].ap, qs1.ap.rearrange("p g t h -> p g (t h)"), [qs1.res()], [mats[3].res()])
            P.release(mk)
            for k in range(4):
                P.dma("sp", S["ssm"][l, k], mats[k].ap.rearrange("p g c -> p (g c)"), r=[mats[k].res()],
                      w=[("d:ssm", l * 4 + k, l * 4 + k + 1)], key=("ssmst",))
            P.release(m1)
        P.release(m0)

    def bc_load(self, tile, src2d, n, key):
        self.P.dma("sp", tile.ap, src2d.broadcast_to([128, n]), r=[], w=[tile.res()], key=key)

    def tbank(self):
        self.tb = 6 + (getattr(self, "tb", 7) - 5) % 2
        return self.tb

    def statset(self):
        self.si = (getattr(self, "si", -1) + 1) % 4
        return self.stat[self.si]

    def layernorm(self, src_ap, src_res, ncols, g_t, b_t, dst_ap, dst_res, tmp, add_eng="pool"):
        self.layernorm_multi([(src_ap, src_res, ncols, dst_ap, dst_res, tmp)], g_t, b_t, add_eng)

    def layernorm_multi(self, items, g_t, b_t, add_eng="pool"):
        if len(items) > 1 and not BATCH_LN:
            for it in items:
                self.layernorm_multi([it], g_t, b_t, add_eng)
            return
        P = self.P
        sets = [self.statset() for _ in items]
        for (src_ap, src_res, ncols, dst_ap, dst_res, tmp), (st, mv, rstd, nb) in zip(items, sets):
            for c in range(ncols // 512):
                P.op("dve", lambda e, c=c, st=st, src_ap=src_ap: e.bn_stats(out=st.ap[:, c, :], in_=src_ap[:, c * 512:(c + 1) * 512]), [src_res], [st.res()])
        for (src_ap, src_res, ncols, dst_ap, dst_res, tmp), (st, mv, rstd, nb) in zip(items, sets):
            nch = ncols // 512
            P.op("dve", lambda e, st=st, mv=mv, nch=nch: e.bn_aggr(out=mv.ap, in_=st.ap[:, 0:nch, :].rearrange("p c s -> p (c s)")), [st.res()], [mv.res()])
        for it, (st, mv, rstd, nb) in zip(items, sets):
            P.op("act", lambda e, mv=mv, rstd=rstd: e.activation(out=rstd.ap, in_=mv.ap[:, 1:2], func=AF.Sqrt, bias=self.epst.ap, scale=1.0),
                 [mv.res(), self.epst.res()], [rstd.res()])
        for it, (st, mv, rstd, nb) in zip(items, sets):
            P.op("dve", lambda e, rstd=rstd: e.reciprocal(out=rstd.ap, in_=rstd.ap), [rstd.res()], [rstd.res()])
            P.op("dve", lambda e, mv=mv, rstd=rstd, nb=nb: e.scalar_tensor_tensor(out=nb.ap, in0=mv.ap[:, 0:1], scalar=-1.0, in1=rstd.ap, op0=ALU.mult, op1=ALU.mult),
                 [mv.res(), rstd.res()], [nb.res()])
        for (src_ap, src_res, ncols, dst_ap, dst_res, tmp), (st, mv, rstd, nb) in zip(items, sets):
            P.op("act", lambda e, tmp=tmp, ncols=ncols, src_ap=src_ap, nb=nb, rstd=rstd: e.activation(
                out=tmp.ap[:, 0:ncols], in_=src_ap, func=AF.Identity, bias=nb.ap, scale=rstd.ap), [src_res, nb.res(), rstd.res()], [tmp.res()])
        for (src_ap, src_res, ncols, dst_ap, dst_res, tmp), _ in zip(items, sets):
            tv = tmp.ap[:, 0:ncols]
            self.tt("dve", tv, tv, g_t.ap, ALU.mult, [tmp.res(), g_t.res()], [tmp.res()])
        for (src_ap, src_res, ncols, dst_ap, dst_res, tmp), _ in zip(items, sets):
            self.tt(add_eng, dst_ap, tmp.ap[:, 0:ncols], b_t.ap, ALU.add, [tmp.res(), b_t.res()], [dst_res])

    def htres(self, T, b):
        return [T.res(k * 2 * UNIT + b * 256, k * 2 * UNIT + (b + 1) * 256) for k in range(8)]

    def to_HT(self, b):
        self.to_HT_multi([b])

    def to_HT_multi(self, blocks):
        P = self.P
        H, HT = self.H, self.HT
        for b in blocks:
            hb = self.hb[b % 4]
            self.copy("act", hb.ap, H.ap[:, b, :], [H.res(b * 4096, (b + 1) * 4096)], [hb.res()])
        banks = {}
        for b in blocks:
            hb = self.hb[b % 4]
            bank = self.tbank()
            banks[b] = bank
            pb = P.ps(bank, BF16)
            for k in range(8):
                self.tr(pb[:, k * 128:(k + 1) * 128], hb.ap[:, k * 128:(k + 1) * 128], self.idb.ap, [hb.res(), self.idb.res()], [P.psr(bank)])
            self.copy("act", HT.ap[:, :, b * 128:(b + 1) * 128], pb.rearrange("p (k t) -> p k t", t=128), [P.psr(bank)], self.htres(HT, b))

    def main(self, units, layers):
        P = self.P
        I, S = self.I, self.S
        self.H = H = P.alloc([NBLK, D], F32)
        self.HT = HT = P.alloc([8, UNIT], BF16)
        CATT = P.alloc([8, UNIT], BF16)
        CAR = P.alloc([DEPTH, 2, G], F32)
        COEF = P.alloc([DEPTH, 8, G], F32)
        self.hb = [P.alloc([D], BF16) for _ in range(4)]
        self.stat = [(P.alloc([2, 6], F32), P.alloc([2], F32), P.alloc([1], F32), P.alloc([1], F32)) for _ in range(4)]
        TMP = [P.alloc([D], F32) for _ in range(4)]
        for l in layers:
            P.dma("sp", COEF.ap[:, l].rearrange("p a g -> p (a g)"), S["coef"][l], r=[("d:coef", l, l + 1)], w=[COEF.res()], key=("coefld",))
        nunits = len(units)
        P.dma("sp", CAR.ap.rearrange("p l a g -> p (l a g)"), I["car_in"], r=[], w=[CAR.res()], key=("coefld",))
        for ui, u in enumerate(units):
            tok0 = u * UNIT
            m0 = P.mark()
            eg = P.alloc([D], F32)
            eb = P.alloc([D], F32)
            self.bc_load(eg, I["emb_ln_g"].unsqueeze(0), D, ("egb",))
            self.bc_load(eb, I["emb_ln_b"].unsqueeze(0), D, ("egb",))
            XS = [P.alloc([D], F32) for _ in range(4)]
            items = []
            for b in range(NBLK):
                xs = XS[b]
                P.dma("sp", xs.ap, I["x"][tok0 + b * 128:tok0 + (b + 1) * 128, :], r=[], w=[xs.res()], key=("xs", b % 2))
                items.append((xs.ap, xs.res(), D, H.ap[:, b, :], H.res(b * 4096, (b + 1) * 4096), TMP[b]))
            self.layernorm_multi(items[0:2], eg, eb)
            self.to_HT_multi([0, 1])
            self.layernorm_multi(items[2:4], eg, eb)
            self.to_HT_multi([2, 3])
            P.release(m0)
            if self.cfg.get("stop") == "s0":
                self.dump_H(tok0)
                continue
            for l in layers:
                self.layer(u, ui, l, tok0, CATT, CAR, COEF, TMP, last=(l == layers[-1]))
        nblk_total = self.LT // 128
        P.dma("sp", self.car_out, CAR.ap.rearrange("p l a g -> p (l a g)"), r=[CAR.res()], w=[("d:car", 0, 1)], key=("coefld",))
        P.op("sp", None, r=[("d:y", 0, nblk_total), ("d:car", 0, 1)], w=[])

    def dump_H(self, tok0):
        P = self.P
        for b in range(NBLK):
            gb = (tok0 // 128) + b
            P.dma("sp", self.y[tok0 + b * 128:tok0 + (b + 1) * 128, :], self.H.ap[:, b, :], r=[self.H.res(b * 4096, (b + 1) * 4096)],
                  w=[("d:y", gb, gb + 1)], key=("yst", b % 2))

    def layer(self, u, ui, l, tok0, CATT, CAR, COEF, TMP, last):
        P = self.P
        I, S = self.I, self.S
        H, HT = self.H, self.HT
        idb = self.idb
        mL = P.mark()
        X0 = P.top
        XSZ = 54 * 1024
        P.top += XSZ
        xo = [X0]

        def xalloc(shape, dt):
            t, nxt = P.alloc_at(xo[0], shape, dt)
            xo[0] = nxt
            assert nxt <= X0 + XSZ, "region X overflow"
            return t
        U = P.alloc([G, 8, 16], BF16)
        UT = P.alloc([G, TH], BF16)
        MT2 = [P.alloc([G, 128], BF16) for _ in range(2)]
        MT = [MT2[0], MT2[0], MT2[1], MT2[1]]
        ZZ = P.alloc([2, G, TH], F32)
        ZS1 = Tile(ZZ.ap[:, 0], ZZ.off, ZZ.nbytes)
        ZS2 = Tile(ZZ.ap[:, 1], ZZ.off, ZZ.nbytes)
        ta2 = P.alloc([2, G, NCH], F32)
        tb2 = P.alloc([2, G, NCH], F32)
        ta = Tile(ta2.ap[:, 0], ta2.off, ta2.nbytes)
        tb_ = Tile(tb2.ap[:, 0], tb2.off, tb2.nbytes)
        EE = P.alloc([2, G, NCH + 1], F32)
        ES = Tile(EE.ap[:, 0], EE.off, EE.nbytes)
        ET = Tile(EE.ap[:, 1], EE.off, EE.nbytes)
        t1 = P.alloc([2, G], F32)
        t2 = P.alloc([2, G], F32)
        C2 = P.alloc([4, 2, G], F32)
        XB = P.alloc([G, TH], BF16)
        Win = xalloc([8, DIN], BF16)
        WsT = xalloc([8, 128], BF16)
        sg_g = xalloc([DSGU], F32)
        sg_b = xalloc([DSGU], F32)
        gsgu = xalloc([DSGU], F32)
        bst = xalloc([8], F32)
        GU = xalloc([NBLK, 512], F32)
        GV = [Tile(TMP[b_].ap[:, 512:1024], TMP[b_].off + 2048, 2048) for b_ in range(4)]
        VLN = xalloc([NBLK, 512], BF16)
        Y = [xalloc([512], F32) for _ in range(2)]
        YN = [xalloc([512], BF16) for _ in range(2)]
        junk = xalloc([512], BF16)
        P.dma("sp", Win.ap.rearrange("p k n -> p (k n)"), S["win"][l], r=[("d:win", l, l + 1)], w=[Win.res()], key=("win",))
        for k in (1, 2):
            P.dma("sp", MT[k].ap.rearrange("p g c -> p (g c)"), S["ssm"][l, k], r=[("d:ssm", l * 4 + k, l * 4 + k + 1)], w=[MT[k].res()], key=("mt", k % 2))
        P.dma("sp", WsT.ap.rearrange("p h i -> p (h i)"), S["wst"][l], r=[("d:wst", l, l + 1)], w=[WsT.res()], key=("wstld",))
        self.bc_load(sg_g, I["sgu_ln_g"][l:l + 1, :], DSGU, ("sgp",))
        self.bc_load(sg_b, I["sgu_ln_b"][l:l + 1, :], DSGU, ("sgp",))
        self.bc_load(gsgu, I["out_g_sgu"][l:l + 1, :], DSGU, ("sgp",))
        P.dma("sp", bst.ap, I["sgu_b_s"][l].rearrange("h i -> i h"), r=[], w=[bst.res()], key=("sgp",), allow_slow_non_contiguous=True)
        for r in range(8):
            bank = 4 + r % 2
            for k in range(8):
                self.mm(P.ps(bank)[0:64, :], HT.ap[:, k, r:UNIT:8], Win.ap[:, k, 0:512], k == 0, k == 7, [HT.res(), Win.res()], [P.psr(bank)])
            self.copy("act", U.ap[0:64, :, r, :], P.ps(bank)[0:64, :].rearrange("p (g h) -> p g h", h=16), [P.psr(bank)], [U.res()])
        for q in range(4):
            tb = self.tbank()
            pb = P.ps(tb, BF16)
            for gl in range(8):
                g = 8 * q + gl
                self.tr(pb[:, gl * 64:(gl + 1) * 64], U.ap[0:64, g].rearrange("p r h -> p (r h)"), idb.ap[0:64, 0:64], [U.res(), idb.res()], [P.psr(tb)])
            self.copy("act", UT.ap[:, 8 * q:8 * q + 8, :], pb[:, 0:512].rearrange("p (g t) -> p g t", t=64), [P.psr(tb)],
                      [UT.res(q * 1024, (q + 1) * 1024)])
        for bt in range(4):
            b1 = (2 * bt) % 4
            b2 = (2 * bt + 1) % 4
            for gl in range(8):
                g = 8 * bt + gl
                self.mm(P.ps(b1)[:, gl * 64:(gl + 1) * 64], MT[1].ap[:, g, :], UT.ap[:, g, :], True, True, [MT[1].res(), UT.res()], [P.psr(b1)])
                self.mm(P.ps(b2)[:, gl * 64:(gl + 1) * 64], MT[2].ap[:, g, :], UT.ap[:, g, :], True, True, [MT[2].res(), UT.res()], [P.psr(b2)])
            self.copy("act", ZS1.ap[:, 8 * bt:8 * bt + 8, :], P.ps(b1).rearrange("p (g t) -> p g t", t=64), [P.psr(b1)], [ZS1.res(bt * 2048, (bt + 1) * 2048)])
            self.copy("dve", ZS2.ap[:, 8 * bt:8 * bt + 8, :], P.ps(b2).rearrange("p (g t) -> p g t", t=64), [P.psr(b2)], [ZS2.res(bt * 2048, (bt + 1) * 2048)])
        SE = "pool"
        z1 = ZS1.ap.rearrange("p g (c s) -> p g c s", s=4)
        z2 = ZS2.ap.rearrange("p g (c s) -> p g c s", s=4)
        zz = ZZ.ap.rearrange("p a g (c s) -> p a g c s", s=4)
        zzs = ZZ.ap[:, ::-1].rearrange("p a g (c s) -> p a g c s", s=4)
        cres = [COEF.res()]
        cf = lambda i: COEF.ap[:, l, i, :]
        cfb = lambda i: COEF.ap[:, l, i, :].unsqueeze(2).broadcast_to([128, G, NCH])
        for (ci_, src_i, sgn) in ((0, 0, 1.0), (2, 6, 1.0)):
            self.copy(SE, C2.ap[:, ci_, 0, :], cf(src_i), cres, [C2.res()])
            self.copy(SE, C2.ap[:, ci_, 1, :], cf(src_i), cres, [C2.res()])
            self.copy(SE, C2.ap[:, ci_ + 1, 0, :], cf(src_i + 1), cres, [C2.res()])
            P.op(SE, lambda e, ci_=ci_, src_i=src_i: e.tensor_scalar(out=C2.ap[:, ci_ + 1, 1, :], in0=cf(src_i + 1), scalar1=-1.0, scalar2=None, op0=ALU.mult),
                 cres, [C2.res()])
        c2b = lambda i: C2.ap[:, i].unsqueeze(3).broadcast_to([128, 2, G, NCH])
        for s_ in range(1, 4):
            self.tt(SE, ta2.ap, zz[:, :, :, :, s_ - 1], c2b(0), ALU.mult, [ZZ.res(), C2.res()], [ta2.res()])
            self.tt(SE, tb2.ap, zzs[:, :, :, :, s_ - 1], c2b(1), ALU.mult, [ZZ.res(), C2.res()], [tb2.res()])
            self.tt(SE, ta2.ap, ta2.ap, tb2.ap, ALU.add, [ta2.res(), tb2.res()], [ta2.res()])
            self.tt(SE, zz[:, :, :, :, s_], zz[:, :, :, :, s_], ta2.ap, ALU.add, [ZZ.res(), ta2.res()], [ZZ.res()])
        self.copy(SE, EE.ap[:, :, :, 0], CAR.ap[:, l], [CAR.res()], [EE.res()])
        ees = EE.ap[:, ::-1]
        for c in range(NCH):
            self.tt(SE, t1.ap, EE.ap[:, :, :, c], C2.ap[:, 2], ALU.mult, [EE.res(), C2.res()], [t1.res()])
            self.tt(SE, t2.ap, ees[:, :, :, c], C2.ap[:, 3], ALU.mult, [EE.res(), C2.res()], [t2.res()])
            self.tt(SE, t1.ap, t1.ap, t2.ap, ALU.add, [t1.res(), t2.res()], [t1.res()])
            self.tt(SE, EE.ap[:, :, :, c + 1], t1.ap, zz[:, :, :, c, 3], ALU.add, [t1.res(), ZZ.res()], [EE.res()])
        self.copy(SE, CAR.ap[:, l], EE.ap[:, :, :, NCH], [EE.res()], [CAR.res()])
        items = []
        for b in range(NBLK):
            bA, bB = (0, 1) if b % 2 == 0 else (2, 3)
            for (bank, c0) in ((bA, 512), (bB, 1024)):
                for k in range(8):
                    self.mm(P.ps(bank), HT.ap[:, k, b * 128:(b + 1) * 128], Win.ap[:, k, c0:c0 + 512], k == 0, k == 7,
                            [HT.res(), Win.res()], [P.psr(bank)])
            gv = GV[b]
            P.op("act", lambda e, b=b, bA=bA: e.activation(out=GU.ap[:, b, :], in_=P.ps(bA), func=AF.Gelu_apprx_tanh),
                 [P.psr(bA)], [GU.res(b * 2048, (b + 1) * 2048)])
            P.op("act", lambda e, gv=gv, bB=bB: e.activation(out=gv.ap, in_=P.ps(bB), func=AF.Gelu_apprx_tanh), [P.psr(bB)], [gv.res()])
            items.append((gv.ap, gv.res(), 512, VLN.ap[:, b, :], VLN.res(b * 1024, (b + 1) * 1024), TMP[b]))
        self.layernorm_multi(items[0:2], sg_g, sg_b, add_eng="dve")
        self.layernorm_multi(items[2:4], sg_g, sg_b, add_eng="dve")
        for b in range(NBLK):
            bank = 4 + b % 2
            for h in range(8):
                self.mm(P.ps(bank)[:, 64 * h:64 * h + 64], WsT.ap[:, h, :], VLN.ap[:, b, 64 * h:64 * h + 64], True, True,
                        [WsT.res(), VLN.res(b * 1024, (b + 1) * 1024)], [P.psr(bank)])
            y = Y[b % 2]
            yn = YN[b % 2]
            for h in range(8):
                P.op("dve", lambda e, h=h, y=y, bank=bank, b=b: e.scalar_tensor_tensor(
                    out=y.ap[:, 64 * h:64 * h + 64], in0=P.ps(bank)[:, 64 * h:64 * h + 64], scalar=bst.ap[:, h:h + 1],
                    in1=GU.ap[:, b, 64 * h:64 * h + 64], op0=ALU.add, op1=ALU.mult),
                    [P.psr(bank), bst.res(), GU.res(b * 2048, (b + 1) * 2048)], [y.res()])
            st, mv, rstd, nb = self.statset()
            P.op("act", lambda e, y=y, nb=nb: e.activation(out=junk.ap, in_=y.ap, func=AF.Square, accum_out=nb.ap), [y.res()], [junk.res(), nb.res()])
            P.op("act", lambda e, nb=nb, rstd=rstd: e.activation(out=rstd.ap, in_=nb.ap, func=AF.Sqrt, bias=self.epst.ap, scale=1.0 / 512.0),
                 [nb.res(), self.epst.res()], [rstd.res()])
            P.op("dve", lambda e, rstd=rstd: e.reciprocal(out=rstd.ap, in_=rstd.ap), [rstd.res()], [rstd.res()])
            P.op("dve", lambda e, y=y, yn=yn, rstd=rstd: e.scalar_tensor_tensor(out=yn.ap, in0=y.ap, scalar=rstd.ap, in1=gsgu.ap, op0=ALU.mult, op1=ALU.mult),
                 [y.res(), rstd.res(), gsgu.res()], [yn.res()])
            tb = self.tbank()
            pb = P.ps(tb, BF16)
            for q in range(4):
                self.tr(pb[:, q * 128:(q + 1) * 128], yn.ap[:, q * 128:(q + 1) * 128], idb.ap, [yn.res(), idb.res()], [P.psr(tb)])
            self.copy("act", CATT.ap[:, 4:8, b * 128:(b + 1) * 128], pb[:, 0:512].rearrange("p (q t) -> p q t", t=128), [P.psr(tb)],
                      self.htres(CATT, b)[4:8])
        xb = XB.ap.rearrange("p g (c s) -> p g c s", s=4)
        self.copy("dve", xb[:, :, :, 0], ES.ap[:, :, 0:NCH], [ES.res()], [XB.res()])
        for s_ in range(1, 4):
            self.tt("dve", ta.ap, ES.ap[:, :, 0:NCH], cfb(2 * (s_ - 1)), ALU.mult, [ES.res()] + cres, [ta.res()])
            self.tt("dve", tb_.ap, ET.ap[:, :, 0:NCH], cfb(2 * (s_ - 1) + 1), ALU.mult, [ET.res()] + cres, [tb_.res()])
            self.tt("dve", ta.ap, ta.ap, tb_.ap, ALU.add, [ta.res(), tb_.res()], [ta.res()])
            self.tt("dve", xb[:, :, :, s_], ta.ap, z1[:, :, :, s_ - 1], ALU.add, [ta.res(), ZS1.res()], [XB.res()])
        xo[0] = X0
        YG = xalloc([G, TH], BF16)
        YGE = xalloc([8, G, 16], BF16)
        YGT = xalloc([4, UNIT], BF16)
        Wglu = xalloc([4, DSSM], BF16)
        gssm = xalloc([4], F32)
        PT = xalloc([4, UNIT], F32)
        SQ = xalloc([4, UNIT], BF16)
        SG = [xalloc([UNIT], F32) for _ in range(2)]
        RS = xalloc([UNIT], F32)
        for k in (0, 3):
            P.dma("sp", MT[k].ap.rearrange("p g c -> p (g c)"), S["ssm"][l, k], r=[("d:ssm", l * 4 + k, l * 4 + k + 1)], w=[MT[k].res()], key=("mt", k % 2))
        P.dma("sp", Wglu.ap.rearrange("p k n -> p (k n)"), S["wglu"][l], r=[("d:wglu", l, l + 1)], w=[Wglu.res()], key=("wglu",))
        P.dma("sp", gssm.ap, I["out_g_ssm"][l].rearrange("(q p) -> p q", p=128), r=[], w=[gssm.res()], key=("wglu",), allow_slow_non_contiguous=True)
        for bt in range(4):
            bank = bt
            for gl in range(8):
                g = 8 * bt + gl
                o = P.ps(bank)[:, gl * 64:(gl + 1) * 64]
                self.mm(o, MT[0].ap[:, g, :], UT.ap[:, g, :], True, False, [MT[0].res(), UT.res()], [P.psr(bank)])
                self.mm(o, MT[3].ap[:, g, :], XB.ap[:, g, :], False, True, [MT[3].res(), XB.res()], [P.psr(bank)])
            self.copy("act", YG.ap[:, 8 * bt:8 * bt + 8, :], P.ps(bank).rearrange("p (g t) -> p g t", t=64), [P.psr(bank)],
                      [YG.res(bt * 1024, (bt + 1) * 1024)])
        for q in range(4):
            tb = self.tbank()
            pb = P.ps(tb, BF16)
            for gl in range(8):
                g = 8 * q + gl
                self.tr(pb[0:64, gl * 128:(gl + 1) * 128], YG.ap[:, g, :], idb.ap, [YG.res(), idb.res()], [P.psr(tb)])
            P.op("act", lambda e, q=q, pb=pb: e.activation(out=YGE.ap[0:64, :, 8 * q:8 * q + 8, :], in_=pb[0:64, :].rearrange("p (g r h) -> p r g h", r=8, h=16),
                                                            func=AF.Gelu_apprx_tanh), [P.psr(tb)], [YGE.res()])
        for q in range(4):
            tb = self.tbank()
            pb = P.ps(tb, BF16)
            for r in range(8):
                self.tr(pb[:, r * 64:(r + 1) * 64], YGE.ap[0:64, r, 8 * q:8 * q + 8, :].rearrange("p g h -> p (g h)"), idb.ap[0:64, 0:64], [YGE.res(), idb.res()], [P.psr(tb)])
            self.copy("dve", YGT.ap[:, q, :].rearrange("p (t r) -> p r t", r=8), pb[:, 0:512].rearrange("p (r t) -> p r t", t=64), [P.psr(tb)],
                      [YGT.res(q * 1024, (q + 1) * 1024)])
        for co in range(4):
            for k in range(4):
                self.mm(P.ps(co), Wglu.ap[:, k, co * 128:(co + 1) * 128], YGT.ap[:, k, :], k == 0, k == 3, [Wglu.res(), YGT.res()], [P.psr(co)])
            sg = SG[co % 2]
            P.op("act", lambda e, sg=sg, co=co: e.activation(out=sg.ap, in_=P.ps(co), func=AF.Sigmoid), [P.psr(co)], [sg.res()])
            self.tt("dve", PT.ap[:, co, :], sg.ap, YGT.ap[:, co, :], ALU.mult, [sg.res(), YGT.res(co * 1024, (co + 1) * 1024)], [PT.res(co * 2048, (co + 1) * 2048)])
            P.op("act", lambda e, co=co: e.activation(out=SQ.ap[:, co, :], in_=PT.ap[:, co, :], func=AF.Square),
                 [PT.res(co * 2048, (co + 1) * 2048)], [SQ.res(co * 1024, (co + 1) * 1024)])
        for co in range(4):
            self.mm(P.ps(4), self.onesb.ap, SQ.ap[:, co, :], co == 0, co == 3, [self.onesb.res(), SQ.res()], [P.psr(4)])
        P.op("act", lambda e: e.activation(out=RS.ap, in_=P.ps(4), func=AF.Sqrt, bias=self.epst.ap, scale=1.0), [P.psr(4), self.epst.res()], [RS.res()])
        P.op("dve", lambda e: e.reciprocal(out=RS.ap, in_=RS.ap), [RS.res()], [RS.res()])
        for co in range(4):
            P.op("dve", lambda e, co=co: e.scalar_tensor_tensor(out=CATT.ap[:, co, :], in0=PT.ap[:, co, :], scalar=gssm.ap[:, co:co + 1], in1=RS.ap,
                                                                 op0=ALU.mult, op1=ALU.mult),
                 [PT.res(), gssm.res(), RS.res()], [CATT.res(co * 1024, (co + 1) * 1024)])
        P.release(mL)
        mL = P.mark()
        Wout = P.alloc([8, D], BF16)
        P.dma("sp", Wout.ap.rearrange("p k n -> p (k n)"), S["wout"][l], r=[("d:wout", l, l + 1)], w=[Wout.res()], key=("wout",))
        g1 = P.alloc([D], F32)
        b1 = P.alloc([D], F32)
        self.bc_load(g1, I["ln1_g"][l:l + 1, :], D, ("lnp",))
        self.bc_load(b1, I["ln1_b"][l:l + 1, :], D, ("lnp",))
        items = []
        for b in range(NBLK):
            banks = (0, 1) if b % 2 == 0 else (2, 3)
            tmp = TMP[b]
            hres = H.res(b * 4096, (b + 1) * 4096)
            for hf in range(2):
                for k in range(8):
                    self.mm(P.ps(banks[hf]), CATT.ap[:, k, b * 128:(b + 1) * 128], Wout.ap[:, k, hf * 512:(hf + 1) * 512], k == 0, k == 7,
                            [CATT.res(), Wout.res()], [P.psr(banks[hf])])
                P.op("dve", lambda e, b=b, hf=hf, tmp=tmp, bk=banks[hf]: e.scalar_tensor_tensor(
                    out=tmp.ap[:, hf * 512:(hf + 1) * 512], in0=H.ap[:, b, hf * 512:(hf + 1) * 512], scalar=ALPHA, in1=P.ps(bk),
                    op0=ALU.mult, op1=ALU.add), [hres, P.psr(banks[hf])], [tmp.res()])
            items.append((tmp.ap, tmp.res(), D, H.ap[:, b, :], hres, tmp))
        self.layernorm_multi(items[0:2], g1, b1)
        self.to_HT_multi([0, 1])
        self.layernorm_multi(items[2:4], g1, b1)
        self.to_HT_multi([2, 3])
        P.release(mL)
        if self.cfg.get("stop") == "s4":
            self.dump_H(tok0)
            return
        mL = P.mark()
        Wd = P.alloc([NJ, D], BF16)
        ACTT = P.alloc([NJ, UNIT], BF16)
        Wpg = P.alloc([8, D], BF16)
        Wple = P.alloc([2, D], BF16)
        WG = [P.alloc([8, 128], BF16) for _ in range(3)]
        WU = [P.alloc([8, 128], BF16) for _ in range(3)]
        SIL = [P.alloc([UNIT], F32) for _ in range(2)]
        g2 = P.alloc([D], F32)
        b2 = P.alloc([D], F32)
        bpg = P.alloc([D], F32)
        PF = [P.alloc([PLE], F32) for _ in range(4)]
        PB = [P.alloc([PLE], BF16) for _ in range(4)]
        PT2 = [P.alloc([2, 128], BF16) for _ in range(4)]
        for b in range(NBLK):
            pf, pbt = PF[b], PB[b]
            P.dma("sp", pf.ap, I["p"][l, tok0 + b * 128:tok0 + (b + 1) * 128, :], r=[], w=[pf.res()], key=("pf", b % 2))
            self.copy("pool", pbt.ap, pf.ap, [pf.res()], [pbt.res()])
        for b in range(NBLK):
            pbt, pt2 = PB[b], PT2[b]
            tb = self.tbank()
            pb = P.ps(tb, BF16)
            for kk in range(2):
                self.tr(pb[:, kk * 128:(kk + 1) * 128], pbt.ap[:, kk * 128:(kk + 1) * 128], idb.ap, [pbt.res(), idb.res()], [P.psr(tb)])
            self.copy("act", pt2.ap, pb[:, 0:256].rearrange("p (k t) -> p k t", t=128), [P.psr(tb)], [pt2.res()])
        def ring_load(j):
            sl = j % 3
            P.dma("sp", WG[sl].ap.rearrange("p k n -> p (k n)"), S["wg"][l][:, j * 1024:(j + 1) * 1024], r=[("d:wg", l, l + 1)], w=[WG[sl].res()], key=("wg", sl))
            P.dma("sp", WU[sl].ap.rearrange("p k n -> p (k n)"), S["wu"][l][:, j * 1024:(j + 1) * 1024], r=[("d:wu", l, l + 1)], w=[WU[sl].res()], key=("wu", sl))
        for j in range(NJ):
            sl = j % 3
            if j == 0:
                ring_load(0)
                ring_load(1)
                P.dma("sp", Wpg.ap.rearrange("p k n -> p (k n)"), S["wpg"][l], r=[("d:wpg", l, l + 1)], w=[Wpg.res()], key=("wpg",))
                P.dma("sp", Wple.ap.rearrange("p k n -> p (k n)"), S["wple"][l], r=[("d:wple", l, l + 1)], w=[Wple.res()], key=("wpg",))
                P.dma("sp", Wd.ap.rearrange("p j n -> p (j n)"), S["wd"][l], r=[("d:wd", l, l + 1)], w=[Wd.res()], key=("wd",))
                P.dma("sp", g2.ap, I["ln2_g"][l:l + 1, :].broadcast_to([128, D]), r=[], w=[g2.res()], key=("lnp2",))
                P.dma("sp", b2.ap, I["ln2_b"][l:l + 1, :].broadcast_to([128, D]), r=[], w=[b2.res()], key=("lnp2",))
                P.dma("sp", bpg.ap, I["b_ple_gate"][l:l + 1, :].broadcast_to([128, D]), r=[], w=[bpg.res()], key=("lnp2",))
            if j + 2 < NJ:
                ring_load(j + 2)
            bG, bU = (0, 1) if j % 2 == 0 else (2, 3)
            for k in range(8):
                self.mm(P.ps(bG), WG[sl].ap[:, k, :], HT.ap[:, k, :], k == 0, k == 7, [WG[sl].res(), HT.res()], [P.psr(bG)])
            for k in range(8):
                self.mm(P.ps(bU), WU[sl].ap[:, k, :], HT.ap[:, k, :], k == 0, k == 7, [WU[sl].res(), HT.res()], [P.psr(bU)])
            sil = SIL[j % 2]
            P.op("act", lambda e, sil=sil, bG=bG: e.activation(out=sil.ap, in_=P.ps(bG), func=AF.Silu), [P.psr(bG)], [sil.res()])
            self.tt("dve", ACTT.ap[:, j, :], sil.ap, P.ps(bU), ALU.mult, [sil.res(), P.psr(bU)], [ACTT.res(j * 1024, (j + 1) * 1024)])
        pending = None
        for b in range(NBLK):
            hres = H.res(b * 4096, (b + 1) * 4096)
            tmp = TMP[b]
            TQ = TMP[(b + 1) % 4]
            PLEV = TMP[(b + 2) % 4]
            pt2 = PT2[b]
            for hf in range(2):
                for j in range(NJ):
                    self.mm(P.ps(hf), ACTT.ap[:, j, b * 128:(b + 1) * 128], Wd.ap[:, j, hf * 512:(hf + 1) * 512], j == 0, j == NJ - 1,
                            [ACTT.res(), Wd.res()], [P.psr(hf)])
                for kk in range(2):
                    self.mm(P.ps(2 + hf), pt2.ap[:, kk, :], Wple.ap[:, kk, hf * 512:(hf + 1) * 512], kk == 0, kk == 1, [pt2.res(), Wple.res()], [P.psr(2 + hf)])
                for k in range(8):
                    self.mm(P.ps(4 + hf), HT.ap[:, k, b * 128:(b + 1) * 128], Wpg.ap[:, k, hf * 512:(hf + 1) * 512], k == 0, k == 7,
                            self.htres(HT, b) + [Wpg.res()], [P.psr(4 + hf)])
                if hf == 1 and pending is not None:
                    self.to_HT(pending)
                    pending = None
                cs = slice(hf * 512, (hf + 1) * 512)
                self.tt("dve", TQ.ap[:, cs], P.ps(4 + hf), bpg.ap[:, cs], ALU.add, [P.psr(4 + hf), bpg.res()], [TQ.res(hf * 2048, (hf + 1) * 2048)])
                P.op("act", lambda e, cs=cs, TQ=TQ: e.activation(out=TQ.ap[:, cs], in_=TQ.ap[:, cs], func=AF.Sigmoid),
                     [TQ.res(hf * 2048, (hf + 1) * 2048)], [TQ.res(hf * 2048, (hf + 1) * 2048)])
                self.tt("dve", PLEV.ap[:, cs], TQ.ap[:, cs], P.ps(2 + hf), ALU.mult, [TQ.res(hf * 2048, (hf + 1) * 2048), P.psr(2 + hf)],
                        [PLEV.res(hf * 2048, (hf + 1) * 2048)])
                P.op("dve", lambda e, b=b, cs=cs, tmp=tmp, hf=hf: e.scalar_tensor_tensor(
                    out=tmp.ap[:, cs], in0=H.ap[:, b, cs], scalar=ALPHA, in1=P.ps(hf), op0=ALU.mult, op1=ALU.add),
                    [hres, P.psr(hf)], [tmp.res()])
            self.tt("pool", tmp.ap, tmp.ap, PLEV.ap, ALU.add, [tmp.res(), PLEV.res()], [tmp.res()])
            self.layernorm(tmp.ap, tmp.res(), D, g2, b2, H.ap[:, b, :], hres, tmp)
            if last:
                gb = (tok0 // 128) + b
                P.dma("sp", self.y[tok0 + b * 128:tok0 + (b + 1) * 128, :], H.ap[:, b, :], r=[hres], w=[("d:y", gb, gb + 1)], key=("yst", b % 2))
            else:
                pending = b
        if pending is not None:
            self.to_HT(pending)
        P.release(mL)


def build(cfg):
    nc = bass.Bass("TRN2", target_bir_lowering=False)
    B = Builder(nc, cfg)
    with B.es:
        B.declare()
        B.consts()
        layers = cfg.get("layers", list(range(DEPTH)))
        units = cfg.get("units", list(range(cfg.get("ltok", L) // UNIT)))
        if cfg.get("prologue", True):
            if cfg.get("p_ssm", True):
                B.ssm_prologue(layers)
            if cfg.get("p_wst", True):
                B.convert_wst(layers)
            if cfg.get("p_w", True):
                B.convert_weights(layers)
        if cfg.get("main", True):
            B.main(units, layers)
        B.P.emit()
    return nc, B


_CACHE = {}
LTOK = 4096


def kernel(**inputs):
    cfg = {"ltok": LTOK}
    if "nc" not in _CACHE:
        _CACHE["nc"] = build(cfg)[0]
    nc = _CACHE["nc"]
    wnames = [k for k in inputs if k not in ("x", "p")]
    shared = {k: np.ascontiguousarray(inputs[k], dtype=np.float32) for k in wnames}
    car = [np.zeros((128, DEPTH * 2 * G), np.float32) for _ in range(8)]
    outs = [[] for _ in range(8)]
    for k in range(L // LTOK):
        in_maps = []
        for c in range(8):
            m = dict(shared)
            m["x"] = np.ascontiguousarray(inputs["x"][c, k * LTOK:(k + 1) * LTOK], dtype=np.float32)
            m["p"] = np.ascontiguousarray(inputs["p"][:, c, k * LTOK:(k + 1) * LTOK], dtype=np.float32)
            m["car_in"] = car[c]
            in_maps.append(m)
        res = run_bass_kernel_spmd(nc, in_maps, core_ids=list(range(8)))
        for c in range(8):
            outs[c].append(np.asarray(res.results[c]["y"], dtype=np.float32))
            car[c] = np.ascontiguousarray(np.asarray(res.results[c]["car_out"], dtype=np.float32))
    return np.stack([np.concatenate(o, axis=0) for o in outs], axis=0)
```

```python
import math
from contextlib import ExitStack
import numpy as np
import concourse.bass as bass
import concourse.mybir as mybir
from concourse.bass_utils import run_bass_kernel_spmd

F32 = mybir.dt.float32
BF16 = mybir.dt.bfloat16
ALU = mybir.AluOpType
AF = mybir.ActivationFunctionType
AX = mybir.AxisListType

D = 1024
L = 4096
DEPTH = 4
DSSM = 512
DSGU = 512
DIN = 1536
DFF = 2816
NJ = 22
PLE = 256
ALPHA = (2 * DEPTH) ** 0.25
EPS = 1e-5
UNIT = 512
NBLK = 4
TH = 64
NCH = 16
G = 32
TWO_PI = 2.0 * math.pi

GRAN = 256
ARENA = 196608
SAME_SYNC = True
RAW_ONLY = False
BATCH_LN = True


def dsize(dt):
    return 2 if dt == BF16 else 4


class Inst:
    __slots__ = ("eng", "fn", "deps", "signal", "val", "sem", "is_dma", "name", "idx")


class Tile:
    def __init__(self, ap, off, nbytes):
        self.ap = ap
        self.off = off
        self.nbytes = nbytes

    def res(self, lo=None, hi=None):
        if lo is None:
            return ("sb", self.off, self.off + self.nbytes)
        return ("sb", self.off + lo, self.off + hi)


class Prog:
    ENGS = ("pe", "act", "dve", "pool", "sp")

    def __init__(self, nc, es):
        self.nc = nc
        self.es = es
        self.streams = {e: [] for e in self.ENGS}
        self.lastw = {}
        self.readers = {}
        self.arena = es.enter_context(nc.sbuf_tensor("arena", [128, ARENA // 2], BF16))
        self.top = 0
        self.peak = 0
        self.psum = es.enter_context(nc.psum_tensor("psum", [128, 8, 512], F32))
        self.dma_slots = {}
        self.esem = {}
        self.ninst = 0

    def alloc(self, shape, dt):
        n = 1
        for s in shape:
            n *= s
        nbytes = n * dsize(dt)
        off = self.top
        self.top = (off + nbytes + GRAN - 1) // GRAN * GRAN
        self.peak = max(self.peak, self.top)
        assert self.top <= ARENA, f"SBUF arena overflow {self.top}"
        v = self.arena[:, off // 2:(off + nbytes) // 2]
        if dt != BF16:
            v = v.bitcast(dt)
        if len(shape) == 2:
            v = v.rearrange("p (a b) -> p a b", b=shape[1])
        elif len(shape) == 3:
            v = v.rearrange("p (a b c) -> p a b c", b=shape[1], c=shape[2])
        elif len(shape) == 4:
            v = v.rearrange("p (a b c d) -> p a b c d", b=shape[1], c=shape[2], d=shape[3])
        return Tile(v, off, nbytes)

    def alloc_at(self, off, shape, dt):
        top0, peak0 = self.top, self.peak
        self.top = off
        t = self.alloc(shape, dt)
        nxt = self.top
        self.top = top0
        self.peak = max(peak0, nxt)
        return t, nxt

    def mark(self):
        return self.top

    def release(self, m):
        self.top = m

    def ps(self, bank, dt=F32):
        v = self.psum[:, bank, :]
        if dt == BF16:
            v = v.bitcast(BF16)
        return v

    @staticmethod
    def psr(bank, lo=0, hi=2048, nb=1):
        return ("ps", bank * 2048 + lo, (bank + nb - 1) * 2048 + hi)

    def _gran(self, res):
        sp, lo, hi = res
        if sp in ("sb", "ps"):
            return [(sp, g) for g in range(lo // GRAN, (hi + GRAN - 1) // GRAN)]
        return [(sp, g) for g in range(lo, hi)]

    def _record(self, inst, r, w):
        deps = {}

        def add(d, raw=True):
            if d is None or d is inst:
                return
            if RAW_ONLY and (not raw) and (not d.is_dma) and d.eng == inst.eng and not inst.is_dma:
                return
            if d.is_dma:
                deps[("dma", id(d.sem))] = d if ("dma", id(d.sem)) not in deps or deps[("dma", id(d.sem))].val < d.val else deps[("dma", id(d.sem))]
            else:
                if d.eng == inst.eng and not inst.is_dma and (inst.eng == "pe" or not SAME_SYNC):
                    return
                k = ("e", d.eng)
                if k not in deps or deps[k].idx < d.idx:
                    deps[k] = d
        for res in r:
            for g in self._gran(res):
                add(self.lastw.get(g))
        for res in w:
            for g in self._gran(res):
                add(self.lastw.get(g), False)
                rd = self.readers.get(g)
                if rd:
                    for d in rd.values():
                        add(d, False)
        for res in r:
            for g in self._gran(res):
                self.readers.setdefault(g, {})[(inst.eng, inst.is_dma and id(inst.sem))] = inst
        for res in w:
            for g in self._gran(res):
                self.lastw[g] = inst
                self.readers[g] = {}
        inst.deps = list(deps.values())
        for d in inst.deps:
            d.signal = True

    def op(self, eng, fn, r=(), w=(), name=""):
        inst = _mk(eng, fn, name)
        inst.idx = len(self.streams[eng])
        self._record(inst, r, w)
        self.streams[eng].append(inst)
        self.ninst += 1
        return inst

    def dma(self, q, out, in_, r=(), w=(), key=None, **kw):
        assert key is not None
        slot = self.dma_slots.get(key)
        if slot is None:
            sem = self.es.enter_context(self.nc.semaphore("d_" + str(key)))
            slot = [sem, 0, None]
            self.dma_slots[key] = slot
        inst = _mk(q, None, "dma_" + str(key))
        inst.is_dma = True
        inst.sem = slot[0]
        slot[1] += 1
        inst.val = 16 * slot[1]
        inst.signal = True
        inst.idx = len(self.streams[q])
        inst.fn = (lambda e, o=out, i=in_, k=kw: e.dma_start(out=o, in_=i, **k))
        self._record(inst, r, w)
        if slot[2] is not None and slot[2] not in inst.deps:
            inst.deps.append(slot[2])
        slot[2] = inst
        self.streams[q].append(inst)
        self.ninst += 1
        return inst

    def emit(self):
        nc = self.nc
        for e in ("pe", "act", "dve", "pool"):
            self.esem[e] = self.es.enter_context(nc.semaphore("e_" + e))
        for e in ("pe", "act", "dve", "pool"):
            c = 0
            for inst in self.streams[e]:
                if inst.is_dma:
                    continue
                inst.sem = self.esem[e]
                if inst.signal:
                    c += 1
                    inst.val = c
        for e in self.ENGS:
            for inst in self.streams[e]:
                if not inst.is_dma and inst.signal:
                    assert e != "sp"
        block = self.es.enter_context(nc.Block())

        def run(eng_name):
            def body(e):
                waited = {}
                for inst in self.streams[eng_name]:
                    for d in inst.deps:
                        k = id(d.sem)
                        if waited.get(k, 0) < d.val:
                            e.wait_ge(d.sem, d.val)
                            waited[k] = d.val
                    if inst.fn is None:
                        continue
                    h = inst.fn(e)
                    if inst.is_dma:
                        h.then_inc(inst.sem, 16)
                    elif inst.signal:
                        h.then_inc(inst.sem, 1)
            return body
        block.tensor(run("pe"))
        block.scalar(run("act"))
        block.vector(run("dve"))
        block.gpsimd(run("pool"))
        block.sync(run("sp"))


def _mk(eng, fn, name):
    i = Inst()
    i.eng = eng
    i.fn = fn
    i.deps = []
    i.signal = False
    i.val = 0
    i.sem = None
    i.is_dma = False
    i.name = name
    return i


class Builder:
    def __init__(self, nc, cfg):
        self.nc = nc
        self.cfg = cfg
        self.es = ExitStack()
        self.P = Prog(nc, self.es)
        self.rr = 0

    def ew(self):
        self.rr += 1
        return ("act", "dve", "pool")[self.rr % 3]

    def copy(self, eng, out, in_, r, w):
        if eng == "act":
            self.P.op("act", lambda e: e.activation(out=out, in_=in_, func=AF.Copy), r, w)
        else:
            self.P.op(eng, lambda e: e.tensor_copy(out=out, in_=in_), r, w)

    def tt(self, eng, out, a, b, op, r, w):
        self.P.op(eng, lambda e: e.tensor_tensor(out=out, in0=a, in1=b, op=op), r, w)

    def mm(self, out, lhsT, rhs, start, stop, r, w):
        self.P.op("pe", lambda e: e.matmul(out, lhsT=lhsT, rhs=rhs, start=start, stop=stop), r, w)

    def tr(self, out, in_, ident, r, w):
        self.P.op("pe", lambda e: e.transpose(out=out, in_=in_, identity=ident), r, w)

    def declare(self):
        nc = self.nc
        dt = lambda n, s, d=F32, k="ExternalInput": nc.dram_tensor(n, s, d, kind=k).ap()
        I = {}
        LT = self.cfg.get("ltok", L)
        self.LT = LT
        I["x"] = dt("x", [LT, D])
        I["p"] = dt("p", [DEPTH, LT, PLE])
        I["car_in"] = dt("car_in", [128, DEPTH * 2 * G])
        self.car_out = dt("car_out", [128, DEPTH * 2 * G], F32, "ExternalOutput")
        I["emb_ln_g"] = dt("emb_ln_g", [D])
        I["emb_ln_b"] = dt("emb_ln_b", [D])
        I["w_in"] = dt("w_in", [DEPTH, D, DIN])
        I["ssm_a_re"] = dt("ssm_a_re", [DEPTH, G, 64])
        I["ssm_a_im"] = dt("ssm_a_im", [DEPTH, G, 64])
        I["ssm_log_dt"] = dt("ssm_log_dt", [DEPTH, G])
        I["ssm_b_re"] = dt("ssm_b_re", [DEPTH, G, 64, 16])
        I["ssm_b_im"] = dt("ssm_b_im", [DEPTH, G, 64, 16])
        I["ssm_c_re"] = dt("ssm_c_re", [DEPTH, G, 16, 64])
        I["ssm_c_im"] = dt("ssm_c_im", [DEPTH, G, 16, 64])
        I["ssm_d"] = dt("ssm_d", [DEPTH, G, 16])
        I["ssm_w_glu"] = dt("ssm_w_glu", [DEPTH, DSSM, DSSM])
        I["sgu_ln_g"] = dt("sgu_ln_g", [DEPTH, DSGU])
        I["sgu_ln_b"] = dt("sgu_ln_b", [DEPTH, DSGU])
        I["sgu_w_s"] = dt("sgu_w_s", [DEPTH, 8, 128, 128])
        I["sgu_b_s"] = dt("sgu_b_s", [DEPTH, 8, 128])
        I["out_g_ssm"] = dt("out_g_ssm", [DEPTH, DSSM])
        I["out_g_sgu"] = dt("out_g_sgu", [DEPTH, DSGU])
        I["w_out"] = dt("w_out", [DEPTH, D, D])
        I["ln1_g"] = dt("ln1_g", [DEPTH, D])
        I["ln1_b"] = dt("ln1_b", [DEPTH, D])
        I["w_ffn_gate"] = dt("w_ffn_gate", [DEPTH, D, DFF])
        I["w_ffn_up"] = dt("w_ffn_up", [DEPTH, D, DFF])
        I["w_ffn_down"] = dt("w_ffn_down", [DEPTH, DFF, D])
        I["w_ple"] = dt("w_ple", [DEPTH, PLE, D])
        I["w_ple_gate"] = dt("w_ple_gate", [DEPTH, D, D])
        I["b_ple_gate"] = dt("b_ple_gate", [DEPTH, D])
        I["ln2_g"] = dt("ln2_g", [DEPTH, D])
        I["ln2_b"] = dt("ln2_b", [DEPTH, D])
        self.I = I
        self.y = dt("y", [LT, D], F32, "ExternalOutput")
        sk = "ExternalOutput" if self.cfg.get("dump_scratch") else "Internal"
        S = {}
        S["win"] = dt("s_win", [DEPTH, 128, 8 * DIN], BF16, sk)
        S["wglu"] = dt("s_wglu", [DEPTH, 128, 4 * DSSM], BF16, sk)
        S["wout"] = dt("s_wout", [DEPTH, 128, 8 * D], BF16, sk)
        S["wg"] = dt("s_wg", [DEPTH, 128, NJ * 8 * 128], BF16, sk)
        S["wu"] = dt("s_wu", [DEPTH, 128, NJ * 8 * 128], BF16, sk)
        S["wd"] = dt("s_wd", [DEPTH, 128, NJ * D], BF16, sk)
        S["wple"] = dt("s_wple", [DEPTH, 128, 2 * D], BF16, sk)
        S["wpg"] = dt("s_wpg", [DEPTH, 128, 8 * D], BF16, sk)
        S["wst"] = dt("s_wst", [DEPTH, 128, 8 * 128], BF16, sk)
        S["ssm"] = dt("s_ssm", [DEPTH, 4, 128, G * 128], BF16, sk)
        S["coef"] = dt("s_coef", [DEPTH, 128, 8 * G], F32, sk)
        self.S = S

    def consts(self):
        P = self.P
        self.idb = P.alloc([128], BF16)
        self.idf = P.alloc([128], F32)
        self.onesb = P.alloc([128], BF16)
        self.msk = P.alloc([128], F32)
        self.epst = P.alloc([1], F32)
        P.op("pool", lambda e: e.memset(self.epst.ap, EPS), [], [self.epst.res()])
        idf, idb, onesb, msk = self.idf, self.idb, self.onesb, self.msk
        P.op("pool", lambda e: e.memset(idf.ap, 0.0), [], [idf.res()])
        P.op("pool", lambda e: e.affine_select(out=idf.ap, in_=idf.ap, pattern=[[-1, 128]], compare_op=ALU.not_equal,
                                               fill=1.0, base=0, channel_multiplier=1), [idf.res()], [idf.res()])
        P.op("pool", lambda e: e.tensor_copy(out=idb.ap, in_=idf.ap), [idf.res()], [idb.res()])
        P.op("pool", lambda e: e.memset(onesb.ap, 1.0 / 512.0), [], [onesb.res()])
        P.op("pool", lambda e: e.memset(msk.ap, 1.0), [], [msk.res()])
        mv = msk.ap.rearrange("p (t h) -> p t h", h=16)
        P.op("pool", lambda e: e.affine_select(out=mv, in_=mv, pattern=[[16, 8], [0, 16]], compare_op=ALU.is_ge,
                                               fill=0.0, base=15, channel_multiplier=-1), [msk.res()], [msk.res()])

    def convert_weights(self, layers):
        P = self.P
        I, S = self.I, self.S
        m0 = P.mark()
        NSTG = 6
        stg = [P.alloc([DFF], F32) for _ in range(NSTG)]
        asm = [P.alloc([NJ * 8 * 128], BF16) for _ in range(2)]
        self.cv_i = 0
        self.cv_a = 0
        pending = []

        def flush():
            while pending:
                pending.pop(0)()

        def piece(src, ncols, dst_view, asm_t):
            i = self.cv_i
            self.cv_i += 1
            st = stg[i % NSTG]
            sv = st.ap[:, 0:ncols]
            q = "sp"
            P.dma(q, sv, src, r=[], w=[st.res()], key=("stg", i % NSTG))
            eng = ("act", "pool")[i % 2]
            svv = sv
            if len(dst_view.shape) == 3:
                svv = sv.rearrange("p (a b) -> p a b", b=dst_view.shape[2])
            self.copy(eng, dst_view, svv, [st.res()], [asm_t.res()])

        def whole(name, l, srcs, ncols, place):
            a = asm[self.cv_a % 2]
            self.cv_a += 1
            for idx, src in enumerate(srcs):
                piece(src, ncols, place(a, idx), a)
                if idx == 2:
                    flush()
            flush()
            tot = S[name].shape[2]
            ak = self.cv_a % 2
            pending.append(lambda name=name, l=l, a=a, tot=tot, ak=ak: P.dma(
                "sp", S[name][l], a.ap[:, 0:tot], r=[a.res()], w=[("d:" + name, l, l + 1)], key=("asm", ak)))

        for l in layers:
            whole("win", l, [I["w_in"][l, 128 * k:128 * k + 128, :] for k in range(8)], DIN,
                  lambda a, k: a.ap[:, k * DIN:(k + 1) * DIN])
            whole("wglu", l, [I["ssm_w_glu"][l, 128 * k:128 * k + 128, :] for k in range(4)], DSSM,
                  lambda a, k: a.ap[:, k * DSSM:(k + 1) * DSSM])
            whole("wout", l, [I["w_out"][l, 128 * k:128 * k + 128, :] for k in range(8)], D,
                  lambda a, k: a.ap[:, k * D:(k + 1) * D])
            whole("wpg", l, [I["w_ple_gate"][l, 128 * k:128 * k + 128, :] for k in range(8)], D,
                  lambda a, k: a.ap[:, k * D:(k + 1) * D])
            whole("wple", l, [I["w_ple"][l, 128 * k:128 * k + 128, :] for k in range(2)], D,
                  lambda a, k: a.ap[:, k * D:(k + 1) * D])
            whole("wd", l, [I["w_ffn_down"][l, 128 * j:128 * j + 128, :] for j in range(NJ)], D,
                  lambda a, j: a.ap[:, j * D:(j + 1) * D])
            for nm, key in (("wg", "w_ffn_gate"), ("wu", "w_ffn_up")):
                whole(nm, l, [I[key][l, 128 * k:128 * k + 128, :] for k in range(8)], DFF,
                      lambda a, k: a.ap.rearrange("p (j k n) -> p j k n", k=8, n=128)[:, :, k, :])
        flush()
        P.release(m0)

    def convert_wst(self, layers):
        P = self.P
        I, S = self.I, self.S
        m0 = P.mark()
        for l in layers:
            ws = P.alloc([8, 128], F32)
            wsb = P.alloc([8, 128], BF16)
            wt = P.alloc([8, 128], BF16)
            P.dma("sp", ws.ap, I["sgu_w_s"][l].rearrange("h i j -> i h j"), r=[], w=[ws.res()], key=("ws",))
            P.op("pool", lambda e, ws=ws: e.memset(ws.ap[0:64, :, 64:128], 0.0), [ws.res()], [ws.res()])
            self.copy("dve", wsb.ap, ws.ap, [ws.res()], [wsb.res()])
            pb = P.ps(0, BF16)
            for h in range(8):
                self.tr(pb[:, h * 128:(h + 1) * 128], wsb.ap[:, h, :], self.idb.ap, [wsb.res(), self.idb.res()], [P.psr(0)])
            self.copy("act", wt.ap, pb.rearrange("p (h i) -> p h i", i=128), [P.psr(0)], [wt.res()])
            P.dma("sp", S["wst"][l], wt.ap.rearrange("p h i -> p (h i)"), r=[wt.res()], w=[("d:wst", l, l + 1)], key=("wst",))
        P.release(m0)

    def ssm_prologue(self, layers):
        P = self.P
        I, S = self.I, self.S
        NP = 25
        m0 = P.mark()
        for l in layers:
            m1 = P.mark()
            ar = P.alloc([G], F32)
            ai = P.alloc([G], F32)
            dtt = P.alloc([G], F32)
            arow = P.alloc([2, 128], F32)
            for (j_, key_) in ((0, "ssm_a_re"), (1, "ssm_a_im")):
                for half in range(2):
                    P.dma("sp", arow.ap[0:G, j_, 64 * half:64 * half + 64], I[key_][l], r=[], w=[arow.res()], key=("arow",))
            for (j_, dst_) in ((0, ar), (1, ai)):
                self.tr(P.ps(1)[:, j_ * G:(j_ + 1) * G], arow.ap[0:G, j_, :], self.idf.ap[0:G, 0:G], [arow.res(), self.idf.res()], [P.psr(1)])
            self.copy("act", ar.ap, P.ps(1)[:, 0:G], [P.psr(1)], [ar.res()])
            self.copy("act", ai.ap, P.ps(1)[:, G:2 * G], [P.psr(1)], [ai.res()])
            P.dma("sp", dtt.ap, I["ssm_log_dt"][l:l + 1, :].broadcast_to([128, G]),
                  r=[], w=[dtt.res()], key=("arow",))
            P.op("act", lambda e, t=dtt: e.activation(out=t.ap, in_=t.ap, func=AF.Exp), [dtt.res()], [dtt.res()])
            ad = P.alloc([G], F32)
            th = P.alloc([G], F32)
            self.tt("dve", ad.ap, ar.ap, dtt.ap, ALU.mult, [ar.res(), dtt.res()], [ad.res()])
            self.tt("dve", th.ap, ai.ap, dtt.ap, ALU.mult, [ai.res(), dtt.res()], [th.res()])
            nn = P.alloc([G, NP], F32)
            for (lo, cnt, step, base) in ((0, 8, -1, 0), (8, 8, -1, 7), (16, 9, 1, 0)):
                P.op("pool", lambda e, t=nn, lo=lo, cnt=cnt, step=step, base=base: e.iota(
                    t.ap[:, :, lo:lo + cnt], pattern=[[0, G], [step, cnt]], base=base, channel_multiplier=0,
                    allow_small_or_imprecise_dtypes=True), [], [nn.res()])
            mg = P.alloc([G, NP], F32)
            ph = P.alloc([G, NP], F32)
            pwr = P.alloc([G, NP], F32)
            pwi = P.alloc([G, NP], F32)
            adb = ad.ap.unsqueeze(2).broadcast_to([128, G, NP])
            thb = th.ap.unsqueeze(2).broadcast_to([128, G, NP])
            self.tt("dve", mg.ap, nn.ap, adb, ALU.mult, [nn.res(), ad.res()], [mg.res()])
            P.op("act", lambda e, t=mg: e.activation(out=t.ap, in_=t.ap, func=AF.Exp), [mg.res()], [mg.res()])
            self.tt("dve", ph.ap, nn.ap, thb, ALU.mult, [nn.res(), th.res()], [ph.res()])
            tmpa = P.alloc([G, NP], F32)
            tmpf = P.alloc([G, NP], F32)
            tmpi = P.alloc([G, NP], mybir.dt.int32)
            for (dst, shift) in ((pwi, 0.0), (pwr, 0.25)):
                P.op("dve", lambda e, s=shift: e.tensor_scalar(out=tmpa.ap, in0=ph.ap, scalar1=1.0 / TWO_PI, scalar2=s + 64.0,
                                                               op0=ALU.mult, op1=ALU.add), [ph.res()], [tmpa.res()])
                P.op("dve", lambda e: e.tensor_copy(out=tmpi.ap, in_=tmpa.ap), [tmpa.res()], [tmpi.res()])
                P.op("dve", lambda e: e.tensor_copy(out=tmpf.ap, in_=tmpi.ap), [tmpi.res()], [tmpf.res()])
                self.tt("dve", tmpa.ap, tmpa.ap, tmpf.ap, ALU.subtract, [tmpa.res(), tmpf.res()], [tmpa.res()])
                P.op("dve", lambda e: e.tensor_scalar(out=tmpf.ap, in0=tmpa.ap, scalar1=0.5, scalar2=None, op0=ALU.is_gt), [tmpa.res()], [tmpf.res()])
                self.tt("dve", tmpa.ap, tmpa.ap, tmpf.ap, ALU.subtract, [tmpa.res(), tmpf.res()], [tmpa.res()])
                P.op("dve", lambda e: e.tensor_scalar(out=tmpa.ap, in0=tmpa.ap, scalar1=0.49999, scalar2=-0.49999, op0=ALU.min, op1=ALU.max),
                     [tmpa.res()], [tmpa.res()])
                P.op("act", lambda e, d=dst: e.activation(out=d.ap, in_=tmpa.ap, func=AF.Sin, scale=TWO_PI), [tmpa.res()], [dst.res()])
                self.tt("dve", dst.ap, dst.ap, mg.ap, ALU.mult, [dst.res(), mg.res()], [dst.res()])
            cf = P.alloc([8, G], F32)
            self.copy("dve", cf.ap[:, 0, :], pwr.ap[:, :, 24], [pwr.res()], [cf.res()])
            self.copy("dve", cf.ap[:, 1, :], pwi.ap[:, :, 24], [pwi.res()], [cf.res()])
            t1 = P.alloc([G], F32)
            t2 = P.alloc([G], F32)

            def cmul(o, a, b):
                self.tt("dve", t1.ap, cf.ap[:, 2 * a, :], cf.ap[:, 2 * b, :], ALU.mult, [cf.res()], [t1.res()])
                self.tt("dve", t2.ap, cf.ap[:, 2 * a + 1, :], cf.ap[:, 2 * b + 1, :], ALU.mult, [cf.res()], [t2.res()])
                self.tt("dve", cf.ap[:, 2 * o, :], t1.ap, t2.ap, ALU.subtract, [t1.res(), t2.res()], [cf.res()])
                self.tt("dve", t1.ap, cf.ap[:, 2 * a, :], cf.ap[:, 2 * b + 1, :], ALU.mult, [cf.res()], [t1.res()])
                self.tt("dve", t2.ap, cf.ap[:, 2 * a + 1, :], cf.ap[:, 2 * b, :], ALU.mult, [cf.res()], [t2.res()])
                self.tt("dve", cf.ap[:, 2 * o + 1, :], t1.ap, t2.ap, ALU.add, [t1.res(), t2.res()], [cf.res()])
            cmul(1, 0, 0)
            cmul(2, 1, 0)
            cmul(3, 1, 1)
            P.dma("sp", S["coef"][l], cf.ap.rearrange("p a g -> p (a g)"), r=[cf.res()], w=[("d:coef", l, l + 1)], key=("coef",))
            er = P.alloc([G], F32)
            den = P.alloc([G], F32)
            cr = P.alloc([G], F32)
            ci = P.alloc([G], F32)
            P.op("dve", lambda e: e.tensor_scalar(out=er.ap, in0=pwr.ap[:, :, 17], scalar1=-1.0, scalar2=None, op0=ALU.add), [pwr.res()], [er.res()])
            ei = pwi.ap[:, :, 17]
            self.tt("dve", t1.ap, ar.ap, ar.ap, ALU.mult, [ar.res()], [t1.res()])
            self.tt("dve", t2.ap, ai.ap, ai.ap, ALU.mult, [ai.res()], [t2.res()])
            self.tt("dve", den.ap, t1.ap, t2.ap, ALU.add, [t1.res(), t2.res()], [den.res()])
            P.op("dve", lambda e: e.reciprocal(out=den.ap, in_=den.ap), [den.res()], [den.res()])
            self.tt("dve", t1.ap, er.ap, ar.ap, ALU.mult, [er.res(), ar.res()], [t1.res()])
            self.tt("dve", t2.ap, ei, ai.ap, ALU.mult, [pwi.res(), ai.res()], [t2.res()])
            self.tt("dve", cr.ap, t1.ap, t2.ap, ALU.add, [t1.res(), t2.res()], [cr.res()])
            self.tt("dve", cr.ap, cr.ap, den.ap, ALU.mult, [cr.res(), den.res()], [cr.res()])
            self.tt("dve", t1.ap, ei, ar.ap, ALU.mult, [pwi.res(), ar.res()], [t1.res()])
            self.tt("dve", t2.ap, er.ap, ai.ap, ALU.mult, [er.res(), ai.res()], [t2.res()])
            self.tt("dve", ci.ap, t1.ap, t2.ap, ALU.subtract, [t1.res(), t2.res()], [ci.res()])
            self.tt("dve", ci.ap, ci.ap, den.ap, ALU.mult, [ci.res(), den.res()], [ci.res()])
            bre = P.alloc([G, 16], F32)
            bim = P.alloc([G, 16], F32)
            for (key_, dst_) in (("ssm_b_re", bre), ("ssm_b_im", bim)):
                brow = P.alloc([2, 64, 16], F32)
                for half in range(2):
                    P.dma("sp", brow.ap[0:G, half], I[key_][l], r=[], w=[brow.res()], key=("brow",))
                for h_ in range(16):
                    self.tr(P.ps(1)[:, h_ * G:(h_ + 1) * G], brow.ap[0:G, :, :, h_], self.idf.ap[0:G, 0:G], [brow.res(), self.idf.res()], [P.psr(1)])
                self.copy("act", dst_.ap.rearrange("p g h -> p h g"), P.ps(1).rearrange("p (h g) -> p h g", g=G), [P.psr(1)], [dst_.res()])
            bbr = P.alloc([G, 16], F32)
            bbi = P.alloc([G, 16], F32)
            u1 = P.alloc([G, 16], F32)
            crb = cr.ap.unsqueeze(2).broadcast_to([128, G, 16])
            cib = ci.ap.unsqueeze(2).broadcast_to([128, G, 16])
            self.tt("dve", bbr.ap, bre.ap, crb, ALU.mult, [bre.res(), cr.res()], [bbr.res()])
            self.tt("dve", u1.ap, bim.ap, cib, ALU.mult, [bim.res(), ci.res()], [u1.res()])
            self.tt("dve", bbr.ap, bbr.ap, u1.ap, ALU.subtract, [bbr.res(), u1.res()], [bbr.res()])
            self.tt("dve", bbi.ap, bim.ap, crb, ALU.mult, [bim.res(), cr.res()], [bbi.res()])
            self.tt("dve", u1.ap, bre.ap, cib, ALU.mult, [bre.res(), ci.res()], [u1.res()])
            self.tt("dve", bbi.ap, bbi.ap, u1.ap, ALU.add, [bbi.res(), u1.res()], [bbi.res()])
            cre = P.alloc([G, 16], F32)
            cim = P.alloc([G, 16], F32)
            for (src, dst, nm) in ((I["ssm_c_re"], cre, "cre"), (I["ssm_c_im"], cim, "cim")):
                cl = P.alloc([4, 128], F32)
                sv = src[l].rearrange("(t a) h p -> (a h) t p", t=4)
                P.dma("sp", cl.ap[:, :, 0:64], sv, r=[], w=[cl.res()], key=("crow",))
                P.dma("sp", cl.ap[:, :, 64:128], sv, r=[], w=[cl.res()], key=("crow",))
                for t in range(4):
                    self.tr(P.ps(1)[:, t * 128:(t + 1) * 128], cl.ap[:, t, :], self.idf.ap, [cl.res(), self.idf.res()], [P.psr(1)])
                self.copy("act", dst.ap.rearrange("p g h -> p (g h)"), P.ps(1), [P.psr(1)], [dst.res()])
            big = lambda: P.alloc([G, 8, 16], F32)
            w1 = big()
            w2 = big()

            def table(dst, xr, xi, poff, top, bot):
                pr = pwr.ap[:, :, poff:poff + 8].unsqueeze(3).broadcast_to([128, G, 8, 16])
                pi = pwi.ap[:, :, poff:poff + 8].unsqueeze(3).broadcast_to([128, G, 8, 16])
                xrb = xr.ap.unsqueeze(2).broadcast_to([128, G, 8, 16])
                xib = xi.ap.unsqueeze(2).broadcast_to([128, G, 8, 16])
                for (sl, kind) in ((slice(0, 64), top), (slice(64, 128), bot)):
                    eng = "dve"
                    rs = [pwr.res(), pwi.res(), xr.res(), xi.res()]
                    if kind == "re":
                        self.tt(eng, w1.ap[sl], pr[sl], xrb[sl], ALU.mult, rs, [w1.res()])
                        self.tt(eng, w2.ap[sl], pi[sl], xib[sl], ALU.mult, rs, [w2.res()])
                        self.tt(eng, dst.ap[sl], w1.ap[sl], w2.ap[sl], ALU.subtract, [w1.res(), w2.res()], [dst.res()])
                    else:
                        self.tt(eng, w1.ap[sl], pr[sl], xib[sl], ALU.mult, rs, [w1.res()])
                        self.tt(eng, w2.ap[sl], pi[sl], xrb[sl], ALU.mult, rs, [w2.res()])
                        self.tt(eng, dst.ap[sl], w1.ap[sl], w2.ap[sl], ALU.add, [w1.res(), w2.res()], [dst.res()])
                        if kind == "-im":
                            P.op(eng, lambda e, s=sl: e.tensor_scalar(out=dst.ap[s], in0=dst.ap[s], scalar1=-1.0, scalar2=None, op0=ALU.mult),
                                 [dst.res()], [dst.res()])
            dcol = P.alloc([G], F32)
            drow = P.alloc([8, 16], F32)
            for s_ in range(8):
                P.dma("sp", drow.ap[0:G, s_, :], I["ssm_d"][l], r=[], w=[drow.res()], key=("drow",))
            self.tr(P.ps(1)[:, 0:G], drow.ap[0:G].rearrange("p s h -> p (s h)"), self.idf.ap[0:G, 0:G], [drow.res(), self.idf.res()], [P.psr(1)])
            self.copy("act", dcol.ap, P.ps(1)[:, 0:G], [P.psr(1)], [dcol.res()])
            mats = [P.alloc([G, 128], BF16) for _ in range(4)]
            tmpm = P.alloc([128], F32)
            mk = P.mark()
            ksn = big()
            table(ksn, bbr, bbi, 0, "re", "im")
            qs = big()
            table(qs, cre, cim, 16, "re", "-im")
            for g in range(G):
                b0 = 2 + (g % 3)
                self.mm(P.ps(b0)[:, 0:128], ksn.ap[:, g].rearrange("p s h -> p (s h)"), qs.ap[:, g].rearrange("p t h -> p (t h)"),
                        True, True, [ksn.res(), qs.res()], [P.psr(b0)])
                self.tt("dve", tmpm.ap, P.ps(b0)[:, 0:128], self.msk.ap, ALU.mult, [P.psr(b0), self.msk.res()], [tmpm.res()])
                P.op("dve", lambda e, g=g: e.scalar_tensor_tensor(out=mats[0].ap[:, g, :], in0=self.idf.ap, scalar=dcol.ap[:, g:g + 1],
                                                                   in1=tmpm.ap, op0=ALU.mult, op1=ALU.add),
                     [tmpm.res(), dcol.res(), self.idf.res()], [mats[0].res()])
            P.release(mk)
            for (mi, top, bot) in ((1, "re", "im"), (2, "-im", "re")):
                ks7 = big()
                table(ks7, bbr, bbi, 8, top, bot)
                for g in range(G):
                    b0 = 5 + (g % 3)
                    self.tr(P.ps(b0)[:, 0:128], ks7.ap[:, g].rearrange("p s h -> p (s h)"), self.idf.ap, [ks7.res(), self.idf.res()], [P.psr(b0)])
                    self.copy("act", mats[mi].ap[:, g, :], P.ps(b0)[:, 0:128], [P.psr(b0)], [mats[mi].res()])
                P.release(mk)
            qs1 = big()
            table(qs1, cre, cim, 17, "re", "-im")
            self.copy("act", mats[3].ap, qs1.ap.rearrange("p g t h -> p g (t h)"), [qs1.res()], [mats[3].res()])
            P.release(mk)
            for k in range(4):
                P.dma("sp", S["ssm"][l, k], mats[k].ap.rearrange("p g c -> p (g c)"), r=[mats[k].res()],
                      w=[("d:ssm", l * 4 + k, l * 4 + k + 1)], key=("ssmst",))
            P.release(m1)
        P.release(m0)

    def bc_load(self, tile, src2d, n, key):
        self.P.dma("sp", tile.ap, src2d.broadcast_to([128, n]), r=[], w=[tile.res()], key=key)

    def tbank(self):
        self.tb = 6 + (getattr(self, "tb", 7) - 5) % 2
        return self.tb

    def statset(self):
        self.si = (getattr(self, "si", -1) + 1) % 4
        return self.stat[self.si]

    def layernorm(self, src_ap, src_res, ncols, g_t, b_t, dst_ap, dst_res, tmp, add_eng="pool"):
        self.layernorm_multi([(src_ap, src_res, ncols, dst_ap, dst_res, tmp)], g_t, b_t, add_eng)

    def layernorm_multi(self, items, g_t, b_t, add_eng="pool"):
        if len(items) > 1 and not BATCH_LN:
            for it in items:
                self.layernorm_multi([it], g_t, b_t, add_eng)
            return
        P = self.P
        sets = [self.statset() for _ in items]
        for (src_ap, src_res, ncols, dst_ap, dst_res, tmp), (st, mv, rstd, nb) in zip(items, sets):
            for c in range(ncols // 512):
                P.op("dve", lambda e, c=c, st=st, src_ap=src_ap: e.bn_stats(out=st.ap[:, c, :], in_=src_ap[:, c * 512:(c + 1) * 512]), [src_res], [st.res()])
        for (src_ap, src_res, ncols, dst_ap, dst_res, tmp), (st, mv, rstd, nb) in zip(items, sets):
            nch = ncols // 512
            P.op("dve", lambda e, st=st, mv=mv, nch=nch: e.bn_aggr(out=mv.ap, in_=st.ap[:, 0:nch, :].rearrange("p c s -> p (c s)")), [st.res()], [mv.res()])
        for it, (st, mv, rstd, nb) in zip(items, sets):
            P.op("act", lambda e, mv=mv, rstd=rstd: e.activation(out=rstd.ap, in_=mv.ap[:, 1:2], func=AF.Sqrt, bias=self.epst.ap, scale=1.0),
                 [mv.res(), self.epst.res()], [rstd.res()])
        for it, (st, mv, rstd, nb) in zip(items, sets):
            P.op("dve", lambda e, rstd=rstd: e.reciprocal(out=rstd.ap, in_=rstd.ap), [rstd.res()], [rstd.res()])
            P.op("dve", lambda e, mv=mv, rstd=rstd, nb=nb: e.scalar_tensor_tensor(out=nb.ap, in0=mv.ap[:, 0:1], scalar=-1.0, in1=rstd.ap, op0=ALU.mult, op1=ALU.mult),
                 [mv.res(), rstd.res()], [nb.res()])
        for (src_ap, src_res, ncols, dst_ap, dst_res, tmp), (st, mv, rstd, nb) in zip(items, sets):
            P.op("act", lambda e, tmp=tmp, ncols=ncols, src_ap=src_ap, nb=nb, rstd=rstd: e.activation(
                out=tmp.ap[:, 0:ncols], in_=src_ap, func=AF.Identity, bias=nb.ap, scale=rstd.ap), [src_res, nb.res(), rstd.res()], [tmp.res()])
        for (src_ap, src_res, ncols, dst_ap, dst_res, tmp), _ in zip(items, sets):
            tv = tmp.ap[:, 0:ncols]
            self.tt("dve", tv, tv, g_t.ap, ALU.mult, [tmp.res(), g_t.res()], [tmp.res()])
        for (src_ap, src_res, ncols, dst_ap, dst_res, tmp), _ in zip(items, sets):
            self.tt(add_eng, dst_ap, tmp.ap[:, 0:ncols], b_t.ap, ALU.add, [tmp.res(), b_t.res()], [dst_res])

    def htres(self, T, b):
        return [T.res(k * 2 * UNIT + b * 256, k * 2 * UNIT + (b + 1) * 256) for k in range(8)]

    def to_HT(self, b):
        self.to_HT_multi([b])

    def to_HT_multi(self, blocks):
        P = self.P
        H, HT = self.H, self.HT
        for b in blocks:
            hb = self.hb[b % 4]
            self.copy("act", hb.ap, H.ap[:, b, :], [H.res(b * 4096, (b + 1) * 4096)], [hb.res()])
        banks = {}
        for b in blocks:
            hb = self.hb[b % 4]
            bank = self.tbank()
            banks[b] = bank
            pb = P.ps(bank, BF16)
            for k in range(8):
                self.tr(pb[:, k * 128:(k + 1) * 128], hb.ap[:, k * 128:(k + 1) * 128], self.idb.ap, [hb.res(), self.idb.res()], [P.psr(bank)])
            self.copy("act", HT.ap[:, :, b * 128:(b + 1) * 128], pb.rearrange("p (k t) -> p k t", t=128), [P.psr(bank)], self.htres(HT, b))

    def main(self, units, layers):
        P = self.P
        I, S = self.I, self.S
        self.H = H = P.alloc([NBLK, D], F32)
        self.HT = HT = P.alloc([8, UNIT], BF16)
        CATT = P.alloc([8, UNIT], BF16)
        CAR = P.alloc([DEPTH, 2, G], F32)
        COEF = P.alloc([DEPTH, 8, G], F32)
        self.hb = [P.alloc([D], BF16) for _ in range(4)]
        self.stat = [(P.alloc([2, 6], F32), P.alloc([2], F32), P.alloc([1], F32), P.alloc([1], F32)) for _ in range(4)]
        TMP = [P.alloc([D], F32) for _ in range(4)]
        for l in layers:
            P.dma("sp", COEF.ap[:, l].rearrange("p a g -> p (a g)"), S["coef"][l], r=[("d:coef", l, l + 1)], w=[COEF.res()], key=("coefld",))
        nunits = len(units)
        P.dma("sp", CAR.ap.rearrange("p l a g -> p (l a g)"), I["car_in"], r=[], w=[CAR.res()], key=("coefld",))
        for ui, u in enumerate(units):
            tok0 = u * UNIT
            m0 = P.mark()
            eg = P.alloc([D], F32)
            eb = P.alloc([D], F32)
            self.bc_load(eg, I["emb_ln_g"].unsqueeze(0), D, ("egb",))
            self.bc_load(eb, I["emb_ln_b"].unsqueeze(0), D, ("egb",))
            XS = [P.alloc([D], F32) for _ in range(4)]
            items = []
            for b in range(NBLK):
                xs = XS[b]
                P.dma("sp", xs.ap, I["x"][tok0 + b * 128:tok0 + (b + 1) * 128, :], r=[], w=[xs.res()], key=("xs", b % 2))
                items.append((xs.ap, xs.res(), D, H.ap[:, b, :], H.res(b * 4096, (b + 1) * 4096), TMP[b]))
            self.layernorm_multi(items[0:2], eg, eb)
            self.to_HT_multi([0, 1])
            self.layernorm_multi(items[2:4], eg, eb)
            self.to_HT_multi([2, 3])
            P.release(m0)
            if self.cfg.get("stop") == "s0":
                self.dump_H(tok0)
                continue
            for l in layers:
                self.layer(u, ui, l, tok0, CATT, CAR, COEF, TMP, last=(l == layers[-1]))
        nblk_total = self.LT // 128
        P.dma("sp", self.car_out, CAR.ap.rearrange("p l a g -> p (l a g)"), r=[CAR.res()], w=[("d:car", 0, 1)], key=("coefld",))
        P.op("sp", None, r=[("d:y", 0, nblk_total), ("d:car", 0, 1)], w=[])

    def dump_H(self, tok0):
        P = self.P
        for b in range(NBLK):
            gb = (tok0 // 128) + b
            P.dma("sp", self.y[tok0 + b * 128:tok0 + (b + 1) * 128, :], self.H.ap[:, b, :], r=[self.H.res(b * 4096, (b + 1) * 4096)],
                  w=[("d:y", gb, gb + 1)], key=("yst", b % 2))

    def layer(self, u, ui, l, tok0, CATT, CAR, COEF, TMP, last):
        P = self.P
        I, S = self.I, self.S
        H, HT = self.H, self.HT
        idb = self.idb
        mL = P.mark()
        X0 = P.top
        XSZ = 54 * 1024
        P.top += XSZ
        xo = [X0]

        def xalloc(shape, dt):
            t, nxt = P.alloc_at(xo[0], shape, dt)
            xo[0] = nxt
            assert nxt <= X0 + XSZ, "region X overflow"
            return t
        U = P.alloc([G, 8, 16], BF16)
        UT = P.alloc([G, TH], BF16)
        MT2 = [P.alloc([G, 128], BF16) for _ in range(2)]
        MT = [MT2[0], MT2[0], MT2[1], MT2[1]]
        ZZ = P.alloc([2, G, TH], F32)
        ZS1 = Tile(ZZ.ap[:, 0], ZZ.off, ZZ.nbytes)
        ZS2 = Tile(ZZ.ap[:, 1], ZZ.off, ZZ.nbytes)
        ta2 = P.alloc([2, G, NCH], F32)
        tb2 = P.alloc([2, G, NCH], F32)
        ta = Tile(ta2.ap[:, 0], ta2.off, ta2.nbytes)
        tb_ = Tile(tb2.ap[:, 0], tb2.off, tb2.nbytes)
        EE = P.alloc([2, G, NCH + 1], F32)
        ES = Tile(EE.ap[:, 0], EE.off, EE.nbytes)
        ET = Tile(EE.ap[:, 1], EE.off, EE.nbytes)
        t1 = P.alloc([2, G], F32)
        t2 = P.alloc([2, G], F32)
        C2 = P.alloc([4, 2, G], F32)
        XB = P.alloc([G, TH], BF16)
        Win = xalloc([8, DIN], BF16)
        WsT = xalloc([8, 128], BF16)
        sg_g = xalloc([DSGU], F32)
        sg_b = xalloc([DSGU], F32)
        gsgu = xalloc([DSGU], F32)
        bst = xalloc([8], F32)
        GU = xalloc([NBLK, 512], F32)
        GV = [Tile(TMP[b_].ap[:, 512:1024], TMP[b_].off + 2048, 2048) for b_ in range(4)]
        VLN = xalloc([NBLK, 512], BF16)
        Y = [xalloc([512], F32) for _ in range(2)]
        YN = [xalloc([512], BF16) for _ in range(2)]
        junk = xalloc([512], BF16)
        P.dma("sp", Win.ap.rearrange("p k n -> p (k n)"), S["win"][l], r=[("d:win", l, l + 1)], w=[Win.res()], key=("win",))
        for k in (1, 2):
            P.dma("sp", MT[k].ap.rearrange("p g c -> p (g c)"), S["ssm"][l, k], r=[("d:ssm", l * 4 + k, l * 4 + k + 1)], w=[MT[k].res()], key=("mt", k % 2))
        P.dma("sp", WsT.ap.rearrange("p h i -> p (h i)"), S["wst"][l], r=[("d:wst", l, l + 1)], w=[WsT.res()], key=("wstld",))
        self.bc_load(sg_g, I["sgu_ln_g"][l:l + 1, :], DSGU, ("sgp",))
        self.bc_load(sg_b, I["sgu_ln_b"][l:l + 1, :], DSGU, ("sgp",))
        self.bc_load(gsgu, I["out_g_sgu"][l:l + 1, :], DSGU, ("sgp",))
        P.dma("sp", bst.ap, I["sgu_b_s"][l].rearrange("h i -> i h"), r=[], w=[bst.res()], key=("sgp",), allow_slow_non_contiguous=True)
        for r in range(8):
            bank = 4 + r % 2
            for k in range(8):
                self.mm(P.ps(bank)[0:64, :], HT.ap[:, k, r:UNIT:8], Win.ap[:, k, 0:512], k == 0, k == 7, [HT.res(), Win.res()], [P.psr(bank)])
            self.copy("act", U.ap[0:64, :, r, :], P.ps(bank)[0:64, :].rearrange("p (g h) -> p g h", h=16), [P.psr(bank)], [U.res()])
        for q in range(4):
            tb = self.tbank()
            pb = P.ps(tb, BF16)
            for gl in range(8):
                g = 8 * q + gl
                self.tr(pb[:, gl * 64:(gl + 1) * 64], U.ap[0:64, g].rearrange("p r h -> p (r h)"), idb.ap[0:64, 0:64], [U.res(), idb.res()], [P.psr(tb)])
            self.copy("act", UT.ap[:, 8 * q:8 * q + 8, :], pb[:, 0:512].rearrange("p (g t) -> p g t", t=64), [P.psr(tb)],
                      [UT.res(q * 1024, (q + 1) * 1024)])
        for bt in range(4):
            b1 = (2 * bt) % 4
            b2 = (2 * bt + 1) % 4
            for gl in range(8):
                g = 8 * bt + gl
                self.mm(P.ps(b1)[:, gl * 64:(gl + 1) * 64], MT[1].ap[:, g, :], UT.ap[:, g, :], True, True, [MT[1].res(), UT.res()], [P.psr(b1)])
                self.mm(P.ps(b2)[:, gl * 64:(gl + 1) * 64], MT[2].ap[:, g, :], UT.ap[:, g, :], True, True, [MT[2].res(), UT.res()], [P.psr(b2)])
            self.copy("act", ZS1.ap[:, 8 * bt:8 * bt + 8, :], P.ps(b1).rearrange("p (g t) -> p g t", t=64), [P.psr(b1)], [ZS1.res(bt * 2048, (bt + 1) * 2048)])
            self.copy("dve", ZS2.ap[:, 8 * bt:8 * bt + 8, :], P.ps(b2).rearrange("p (g t) -> p g t", t=64), [P.psr(b2)], [ZS2.res(bt * 2048, (bt + 1) * 2048)])
        SE = "pool"
        z1 = ZS1.ap.rearrange("p g (c s) -> p g c s", s=4)
        z2 = ZS2.ap.rearrange("p g (c s) -> p g c s", s=4)
        zz = ZZ.ap.rearrange("p a g (c s) -> p a g c s", s=4)
        zzs = ZZ.ap[:, ::-1].rearrange("p a g (c s) -> p a g c s", s=4)
        cres = [COEF.res()]
        cf = lambda i: COEF.ap[:, l, i, :]
        cfb = lambda i: COEF.ap[:, l, i, :].unsqueeze(2).broadcast_to([128, G, NCH])
        for (ci_, src_i, sgn) in ((0, 0, 1.0), (2, 6, 1.0)):
            self.copy(SE, C2.ap[:, ci_, 0, :], cf(src_i), cres, [C2.res()])
            self.copy(SE, C2.ap[:, ci_, 1, :], cf(src_i), cres, [C2.res()])
            self.copy(SE, C2.ap[:, ci_ + 1, 0, :], cf(src_i + 1), cres, [C2.res()])
            P.op(SE, lambda e, ci_=ci_, src_i=src_i: e.tensor_scalar(out=C2.ap[:, ci_ + 1, 1, :], in0=cf(src_i + 1), scalar1=-1.0, scalar2=None, op0=ALU.mult),
                 cres, [C2.res()])
        c2b = lambda i: C2.ap[:, i].unsqueeze(3).broadcast_to([128, 2, G, NCH])
        for s_ in range(1, 4):
            self.tt(SE, ta2.ap, zz[:, :, :, :, s_ - 1], c2b(0), ALU.mult, [ZZ.res(), C2.res()], [ta2.res()])
            self.tt(SE, tb2.ap, zzs[:, :, :, :, s_ - 1], c2b(1), ALU.mult, [ZZ.res(), C2.res()], [tb2.res()])
            self.tt(SE, ta2.ap, ta2.ap, tb2.ap, ALU.add, [ta2.res(), tb2.res()], [ta2.res()])
            self.tt(SE, zz[:, :, :, :, s_], zz[:, :, :, :, s_], ta2.ap, ALU.add, [ZZ.res(), ta2.res()], [ZZ.res()])
        self.copy(SE, EE.ap[:, :, :, 0], CAR.ap[:, l], [CAR.res()], [EE.res()])
        ees = EE.ap[:, ::-1]
        for c in range(NCH):
            self.tt(SE, t1.ap, EE.ap[:, :, :, c], C2.ap[:, 2], ALU.mult, [EE.res(), C2.res()], [t1.res()])
            self.tt(SE, t2.ap, ees[:, :, :, c], C2.ap[:, 3], ALU.mult, [EE.res(), C2.res()], [t2.res()])
            self.tt(SE, t1.ap, t1.ap, t2.ap, ALU.add, [t1.res(), t2.res()], [t1.res()])
            self.tt(SE, EE.ap[:, :, :, c + 1], t1.ap, zz[:, :, :, c, 3], ALU.add, [t1.res(), ZZ.res()], [EE.res()])
        self.copy(SE, CAR.ap[:, l], EE.ap[:, :, :, NCH], [EE.res()], [CAR.res()])
        items = []
        for b in range(NBLK):
            bA, bB = (0, 1) if b % 2 == 0 else (2, 3)
            for (bank, c0) in ((bA, 512), (bB, 1024)):
                for k in range(8):
                    self.mm(P.ps(bank), HT.ap[:, k, b * 128:(b + 1) * 128], Win.ap[:, k, c0:c0 + 512], k == 0, k == 7,
                            [HT.res(), Win.res()], [P.psr(bank)])
            gv = GV[b]
            P.op("act", lambda e, b=b, bA=bA: e.activation(out=GU.ap[:, b, :], in_=P.ps(bA), func=AF.Gelu_apprx_tanh),
                 [P.psr(bA)], [GU.res(b * 2048, (b + 1) * 2048)])
            P.op("act", lambda e, gv=gv, bB=bB: e.activation(out=gv.ap, in_=P.ps(bB), func=AF.Gelu_apprx_tanh), [P.psr(bB)], [gv.res()])
            items.append((gv.ap, gv.res(), 512, VLN.ap[:, b, :], VLN.res(b * 1024, (b + 1) * 1024), TMP[b]))
        self.layernorm_multi(items[0:2], sg_g, sg_b, add_eng="dve")
        self.layernorm_multi(items[2:4], sg_g, sg_b, add_eng="dve")
        for b in range(NBLK):
            bank = b
            for h in range(8):
                self.mm(P.ps(bank)[:, 64 * h:64 * h + 64], WsT.ap[:, h, :], VLN.ap[:, b, 64 * h:64 * h + 64], True, True,
                        [WsT.res(), VLN.res(b * 1024, (b + 1) * 1024)], [P.psr(bank)])
        for b in range(NBLK):
            bank = b
            y = Y[b % 2]
            yn = YN[b % 2]
            for h in range(8):
                P.op("dve", lambda e, h=h, y=y, bank=bank, b=b: e.scalar_tensor_tensor(
                    out=y.ap[:, 64 * h:64 * h + 64], in0=P.ps(bank)[:, 64 * h:64 * h + 64], scalar=bst.ap[:, h:h + 1],
                    in1=GU.ap[:, b, 64 * h:64 * h + 64], op0=ALU.add, op1=ALU.mult),
                    [P.psr(bank), bst.res(), GU.res(b * 2048, (b + 1) * 2048)], [y.res()])
            st, mv, rstd, nb = self.statset()
            P.op("act", lambda e, y=y, nb=nb: e.activation(out=junk.ap, in_=y.ap, func=AF.Square, accum_out=nb.ap), [y.res()], [junk.res(), nb.res()])
            P.op("act", lambda e, nb=nb, rstd=rstd: e.activation(out=rstd.ap, in_=nb.ap, func=AF.Sqrt, bias=self.epst.ap, scale=1.0 / 512.0),
                 [nb.res(), self.epst.res()], [rstd.res()])
            P.op("dve", lambda e, rstd=rstd: e.reciprocal(out=rstd.ap, in_=rstd.ap), [rstd.res()], [rstd.res()])
            P.op("dve", lambda e, y=y, yn=yn, rstd=rstd: e.scalar_tensor_tensor(out=yn.ap, in0=y.ap, scalar=rstd.ap, in1=gsgu.ap, op0=ALU.mult, op1=ALU.mult),
                 [y.res(), rstd.res(), gsgu.res()], [yn.res()])
            tb = self.tbank()
            pb = P.ps(tb, BF16)
            for q in range(4):
                self.tr(pb[:, q * 128:(q + 1) * 128], yn.ap[:, q * 128:(q + 1) * 128], idb.ap, [yn.res(), idb.res()], [P.psr(tb)])
            self.copy("act", CATT.ap[:, 4:8, b * 128:(b + 1) * 128], pb[:, 0:512].rearrange("p (q t) -> p q t", t=128), [P.psr(tb)],
                      self.htres(CATT, b)[4:8])
        xb = XB.ap.rearrange("p g (c s) -> p g c s", s=4)
        self.copy("dve", xb[:, :, :, 0], ES.ap[:, :, 0:NCH], [ES.res()], [XB.res()])
        for s_ in range(1, 4):
            self.tt("dve", ta.ap, ES.ap[:, :, 0:NCH], cfb(2 * (s_ - 1)), ALU.mult, [ES.res()] + cres, [ta.res()])
            self.tt("dve", tb_.ap, ET.ap[:, :, 0:NCH], cfb(2 * (s_ - 1) + 1), ALU.mult, [ET.res()] + cres, [tb_.res()])
            self.tt("dve", ta.ap, ta.ap, tb_.ap, ALU.add, [ta.res(), tb_.res()], [ta.res()])
            self.tt("dve", xb[:, :, :, s_], ta.ap, z1[:, :, :, s_ - 1], ALU.add, [ta.res(), ZS1.res()], [XB.res()])
        xo[0] = X0
        YG = xalloc([G, TH], BF16)
        YGE = xalloc([8, G, 16], BF16)
        YGT = xalloc([4, UNIT], BF16)
        Wglu = xalloc([4, DSSM], BF16)
        gssm = xalloc([4], F32)
        PT = xalloc([4, UNIT], F32)
        SQ = xalloc([4, UNIT], BF16)
        SG = [xalloc([UNIT], F32) for _ in range(2)]
        RS = xalloc([UNIT], F32)
        for k in (0, 3):
            P.dma("sp", MT[k].ap.rearrange("p g c -> p (g c)"), S["ssm"][l, k], r=[("d:ssm", l * 4 + k, l * 4 + k + 1)], w=[MT[k].res()], key=("mt", k % 2))
        P.dma("sp", Wglu.ap.rearrange("p k n -> p (k n)"), S["wglu"][l], r=[("d:wglu", l, l + 1)], w=[Wglu.res()], key=("wglu",))
        P.dma("sp", gssm.ap, I["out_g_ssm"][l].rearrange("(q p) -> p q", p=128), r=[], w=[gssm.res()], key=("wglu",), allow_slow_non_contiguous=True)
        for bt in range(4):
            bank = bt
            for gl in range(8):
                g = 8 * bt + gl
                o = P.ps(bank)[:, gl * 64:(gl + 1) * 64]
                self.mm(o, MT[0].ap[:, g, :], UT.ap[:, g, :], True, False, [MT[0].res(), UT.res()], [P.psr(bank)])
                self.mm(o, MT[3].ap[:, g, :], XB.ap[:, g, :], False, True, [MT[3].res(), XB.res()], [P.psr(bank)])
            self.copy("act", YG.ap[:, 8 * bt:8 * bt + 8, :], P.ps(bank).rearrange("p (g t) -> p g t", t=64), [P.psr(bank)],
                      [YG.res(bt * 1024, (bt + 1) * 1024)])
        for q in range(4):
            tb = self.tbank()
            pb = P.ps(tb, BF16)
            for gl in range(8):
                g = 8 * q + gl
                self.tr(pb[0:64, gl * 128:(gl + 1) * 128], YG.ap[:, g, :], idb.ap, [YG.res(), idb.res()], [P.psr(tb)])
            P.op("act", lambda e, q=q, pb=pb: e.activation(out=YGE.ap[0:64, :, 8 * q:8 * q + 8, :], in_=pb[0:64, :].rearrange("p (g r h) -> p r g h", r=8, h=16),
                                                            func=AF.Gelu_apprx_tanh), [P.psr(tb)], [YGE.res()])
        for q in range(4):
            tb = self.tbank()
            pb = P.ps(tb, BF16)
            for r in range(8):
                self.tr(pb[:, r * 64:(r + 1) * 64], YGE.ap[0:64, r, 8 * q:8 * q + 8, :].rearrange("p g h -> p (g h)"), idb.ap[0:64, 0:64], [YGE.res(), idb.res()], [P.psr(tb)])
            self.copy("dve", YGT.ap[:, q, :].rearrange("p (t r) -> p r t", r=8), pb[:, 0:512].rearrange("p (r t) -> p r t", t=64), [P.psr(tb)],
                      [YGT.res(q * 1024, (q + 1) * 1024)])
        for co in range(4):
            for k in range(4):
                self.mm(P.ps(co), Wglu.ap[:, k, co * 128:(co + 1) * 128], YGT.ap[:, k, :], k == 0, k == 3, [Wglu.res(), YGT.res()], [P.psr(co)])
            sg = SG[co % 2]
            P.op("act", lambda e, sg=sg, co=co: e.activation(out=sg.ap, in_=P.ps(co), func=AF.Sigmoid), [P.psr(co)], [sg.res()])
            self.tt("dve", PT.ap[:, co, :], sg.ap, YGT.ap[:, co, :], ALU.mult, [sg.res(), YGT.res(co * 1024, (co + 1) * 1024)], [PT.res(co * 2048, (co + 1) * 2048)])
            P.op("act", lambda e, co=co: e.activation(out=SQ.ap[:, co, :], in_=PT.ap[:, co, :], func=AF.Square),
                 [PT.res(co * 2048, (co + 1) * 2048)], [SQ.res(co * 1024, (co + 1) * 1024)])
        for co in range(4):
            self.mm(P.ps(4), self.onesb.ap, SQ.ap[:, co, :], co == 0, co == 3, [self.onesb.res(), SQ.res()], [P.psr(4)])
        P.op("act", lambda e: e.activation(out=RS.ap, in_=P.ps(4), func=AF.Sqrt, bias=self.epst.ap, scale=1.0), [P.psr(4), self.epst.res()], [RS.res()])
        P.op("dve", lambda e: e.reciprocal(out=RS.ap, in_=RS.ap), [RS.res()], [RS.res()])
        for co in range(4):
            P.op("dve", lambda e, co=co: e.scalar_tensor_tensor(out=CATT.ap[:, co, :], in0=PT.ap[:, co, :], scalar=gssm.ap[:, co:co + 1], in1=RS.ap,
                                                                 op0=ALU.mult, op1=ALU.mult),
                 [PT.res(), gssm.res(), RS.res()], [CATT.res(co * 1024, (co + 1) * 1024)])
        P.release(mL)
        mL = P.mark()
        Wout = P.alloc([8, D], BF16)
        P.dma("sp", Wout.ap.rearrange("p k n -> p (k n)"), S["wout"][l], r=[("d:wout", l, l + 1)], w=[Wout.res()], key=("wout",))
        g1 = P.alloc([D], F32)
        b1 = P.alloc([D], F32)
        self.bc_load(g1, I["ln1_g"][l:l + 1, :], D, ("lnp",))
        self.bc_load(b1, I["ln1_b"][l:l + 1, :], D, ("lnp",))
        items = []
        for b in range(NBLK):
            banks = (0, 1) if b % 2 == 0 else (2, 3)
            tmp = TMP[b]
            hres = H.res(b * 4096, (b + 1) * 4096)
            for hf in range(2):
                for k in range(8):
                    self.mm(P.ps(banks[hf]), CATT.ap[:, k, b * 128:(b + 1) * 128], Wout.ap[:, k, hf * 512:(hf + 1) * 512], k == 0, k == 7,
                            [CATT.res(), Wout.res()], [P.psr(banks[hf])])
                P.op("dve", lambda e, b=b, hf=hf, tmp=tmp, bk=banks[hf]: e.scalar_tensor_tensor(
                    out=tmp.ap[:, hf * 512:(hf + 1) * 512], in0=H.ap[:, b, hf * 512:(hf + 1) * 512], scalar=ALPHA, in1=P.ps(bk),
                    op0=ALU.mult, op1=ALU.add), [hres, P.psr(banks[hf])], [tmp.res()])
            items.append((tmp.ap, tmp.res(), D, H.ap[:, b, :], hres, tmp))
        self.layernorm_multi(items[0:2], g1, b1)
        self.to_HT_multi([0, 1])
        self.layernorm_multi(items[2:4], g1, b1)
        self.to_HT_multi([2, 3])
        P.release(mL)
        if self.cfg.get("stop") == "s4":
            self.dump_H(tok0)
            return
        mL = P.mark()
        Wd = P.alloc([NJ, D], BF16)
        ACTT = P.alloc([NJ, UNIT], BF16)
        Wpg = P.alloc([8, D], BF16)
        Wple = P.alloc([2, D], BF16)
        WG = [P.alloc([8, 128], BF16) for _ in range(3)]
        WU = [P.alloc([8, 128], BF16) for _ in range(3)]
        SIL = [P.alloc([UNIT], F32) for _ in range(2)]
        g2 = P.alloc([D], F32)
        b2 = P.alloc([D], F32)
        bpg = P.alloc([D], F32)
        PF = [P.alloc([PLE], F32) for _ in range(4)]
        PB = [P.alloc([PLE], BF16) for _ in range(4)]
        PT2 = [P.alloc([2, 128], BF16) for _ in range(4)]
        for b in range(NBLK):
            pf, pbt = PF[b], PB[b]
            P.dma("sp", pf.ap, I["p"][l, tok0 + b * 128:tok0 + (b + 1) * 128, :], r=[], w=[pf.res()], key=("pf", b % 2))
            self.copy("pool", pbt.ap, pf.ap, [pf.res()], [pbt.res()])
        for b in range(NBLK):
            pbt, pt2 = PB[b], PT2[b]
            tb = self.tbank()
            pb = P.ps(tb, BF16)
            for kk in range(2):
                self.tr(pb[:, kk * 128:(kk + 1) * 128], pbt.ap[:, kk * 128:(kk + 1) * 128], idb.ap, [pbt.res(), idb.res()], [P.psr(tb)])
            self.copy("act", pt2.ap, pb[:, 0:256].rearrange("p (k t) -> p k t", t=128), [P.psr(tb)], [pt2.res()])
        def ring_load(j):
            sl = j % 3
            P.dma("sp", WG[sl].ap.rearrange("p k n -> p (k n)"), S["wg"][l][:, j * 1024:(j + 1) * 1024], r=[("d:wg", l, l + 1)], w=[WG[sl].res()], key=("wg", sl))
            P.dma("sp", WU[sl].ap.rearrange("p k n -> p (k n)"), S["wu"][l][:, j * 1024:(j + 1) * 1024], r=[("d:wu", l, l + 1)], w=[WU[sl].res()], key=("wu", sl))
        for j in range(NJ):
            sl = j % 3
            if j == 0:
                ring_load(0)
                ring_load(1)
                P.dma("sp", Wpg.ap.rearrange("p k n -> p (k n)"), S["wpg"][l], r=[("d:wpg", l, l + 1)], w=[Wpg.res()], key=("wpg",))
                P.dma("sp", Wple.ap.rearrange("p k n -> p (k n)"), S["wple"][l], r=[("d:wple", l, l + 1)], w=[Wple.res()], key=("wpg",))
                P.dma("sp", Wd.ap.rearrange("p j n -> p (j n)"), S["wd"][l], r=[("d:wd", l, l + 1)], w=[Wd.res()], key=("wd",))
                P.dma("sp", g2.ap, I["ln2_g"][l:l + 1, :].broadcast_to([128, D]), r=[], w=[g2.res()], key=("lnp2",))
                P.dma("sp", b2.ap, I["ln2_b"][l:l + 1, :].broadcast_to([128, D]), r=[], w=[b2.res()], key=("lnp2",))
                P.dma("sp", bpg.ap, I["b_ple_gate"][l:l + 1, :].broadcast_to([128, D]), r=[], w=[bpg.res()], key=("lnp2",))
            if j + 2 < NJ:
                ring_load(j + 2)
            bG, bU = (0, 1) if j % 2 == 0 else (2, 3)
            for k in range(8):
                self.mm(P.ps(bG), WG[sl].ap[:, k, :], HT.ap[:, k, :], k == 0, k == 7, [WG[sl].res(), HT.res()], [P.psr(bG)])
            for k in range(8):
                self.mm(P.ps(bU), WU[sl].ap[:, k, :], HT.ap[:, k, :], k == 0, k == 7, [WU[sl].res(), HT.res()], [P.psr(bU)])
            sil = SIL[j % 2]
            P.op("act", lambda e, sil=sil, bG=bG: e.activation(out=sil.ap, in_=P.ps(bG), func=AF.Silu), [P.psr(bG)], [sil.res()])
            self.tt("dve", ACTT.ap[:, j, :], sil.ap, P.ps(bU), ALU.mult, [sil.res(), P.psr(bU)], [ACTT.res(j * 1024, (j + 1) * 1024)])
        pending = None
        for b in range(NBLK):
            hres = H.res(b * 4096, (b + 1) * 4096)
            tmp = TMP[b]
            TQ = TMP[(b + 1) % 4]
            PLEV = TMP[(b + 2) % 4]
            pt2 = PT2[b]
            for hf in range(2):
                for j in range(NJ):
                    self.mm(P.ps(hf), ACTT.ap[:, j, b * 128:(b + 1) * 128], Wd.ap[:, j, hf * 512:(hf + 1) * 512], j == 0, j == NJ - 1,
                            [ACTT.res(), Wd.res()], [P.psr(hf)])
                for kk in range(2):
                    self.mm(P.ps(2 + hf), pt2.ap[:, kk, :], Wple.ap[:, kk, hf * 512:(hf + 1) * 512], kk == 0, kk == 1, [pt2.res(), Wple.res()], [P.psr(2 + hf)])
                for k in range(8):
                    self.mm(P.ps(4 + hf), HT.ap[:, k, b * 128:(b + 1) * 128], Wpg.ap[:, k, hf * 512:(hf + 1) * 512], k == 0, k == 7,
                            self.htres(HT, b) + [Wpg.res()], [P.psr(4 + hf)])
                if hf == 1 and pending is not None:
                    self.to_HT(pending)
                    pending = None
                cs = slice(hf * 512, (hf + 1) * 512)
                self.tt("dve", TQ.ap[:, cs], P.ps(4 + hf), bpg.ap[:, cs], ALU.add, [P.psr(4 + hf), bpg.res()], [TQ.res(hf * 2048, (hf + 1) * 2048)])
                P.op("act", lambda e, cs=cs, TQ=TQ: e.activation(out=TQ.ap[:, cs], in_=TQ.ap[:, cs], func=AF.Sigmoid),
                     [TQ.res(hf * 2048, (hf + 1) * 2048)], [TQ.res(hf * 2048, (hf + 1) * 2048)])
                self.tt("dve", PLEV.ap[:, cs], TQ.ap[:, cs], P.ps(2 + hf), ALU.mult, [TQ.res(hf * 2048, (hf + 1) * 2048), P.psr(2 + hf)],
                        [PLEV.res(hf * 2048, (hf + 1) * 2048)])
                P.op("dve", lambda e, b=b, cs=cs, tmp=tmp, hf=hf: e.scalar_tensor_tensor(
                    out=tmp.ap[:, cs], in0=H.ap[:, b, cs], scalar=ALPHA, in1=P.ps(hf), op0=ALU.mult, op1=ALU.add),
                    [hres, P.psr(hf)], [tmp.res()])
            self.tt("pool", tmp.ap, tmp.ap, PLEV.ap, ALU.add, [tmp.res(), PLEV.res()], [tmp.res()])
            self.layernorm(tmp.ap, tmp.res(), D, g2, b2, H.ap[:, b, :], hres, tmp)
            if last:
                gb = (tok0 // 128) + b
                P.dma("sp", self.y[tok0 + b * 128:tok0 + (b + 1) * 128, :], H.ap[:, b, :], r=[hres], w=[("d:y", gb, gb + 1)], key=("yst", b % 2))
            else:
                pending = b
        if pending is not None:
            self.to_HT(pending)
        P.release(mL)


def build(cfg):
    nc = bass.Bass("TRN2", target_bir_lowering=False)
    B = Builder(nc, cfg)
    with B.es:
        B.declare()
        B.consts()
        layers = cfg.get("layers", list(range(DEPTH)))
        units = cfg.get("units", list(range(cfg.get("ltok", L) // UNIT)))
        if cfg.get("prologue", True):
            if cfg.get("p_ssm", True):
                B.ssm_prologue(layers)
            if cfg.get("p_wst", True):
                B.convert_wst(layers)
            if cfg.get("p_w", True):
                B.convert_weights(layers)
        if cfg.get("main", True):
            B.main(units, layers)
        B.P.emit()
    return nc, B


_CACHE = {}
LTOK = 4096


def kernel(**inputs):
    cfg = {"ltok": LTOK}
    if "nc" not in _CACHE:
        _CACHE["nc"] = build(cfg)[0]
    nc = _CACHE["nc"]
    wnames = [k for k in inputs if k not in ("x", "p")]
    shared = {k: np.ascontiguousarray(inputs[k], dtype=np.float32) for k in wnames}
    car = [np.zeros((128, DEPTH * 2 * G), np.float32) for _ in range(8)]
    outs = [[] for _ in range(8)]
    for k in range(L // LTOK):
        in_maps = []
        for c in range(8):
            m = dict(shared)
            m["x"] = np.ascontiguousarray(inputs["x"][c, k * LTOK:(k + 1) * LTOK], dtype=np.float32)
            m["p"] = np.ascontiguousarray(inputs["p"][:, c, k * LTOK:(k + 1) * LTOK], dtype=np.float32)
            m["car_in"] = car[c]
            in_maps.append(m)
        res = run_bass_kernel_spmd(nc, in_maps, core_ids=list(range(8)))
        for c in range(8):
            outs[c].append(np.asarray(res.results[c]["y"], dtype=np.float32))
            car[c] = np.ascontiguousarray(np.asarray(res.results[c]["car_out"], dtype=np.float32))
    return np.stack([np.concatenate(o, axis=0) for o in outs], axis=0)
```
